# Optimizing a Trainium2 kernel written in Bass

```python
import jax
import jax.numpy as jnp
from jax import lax
import numpy as np

D_MODEL = 2048
BATCH = 32
SEQ = 256
DEPTH = 2
DEC_BATCH = 8
DEC_SEQ = 4096
PAST_LEN = 512

GRID_W = 64
N_BRANCH = 4
BRANCH_W = 512
ATT_HEADS = 8
ATT_KV = 2
HEAD_DIM = 64
ROPE_THETA = 10000.0
Q_BLOCK = 128
SSD_HEADS = 8
SSD_P = 64
SSD_N = 64
SSD_GROUPS = 2
SSD_CONV = 3
SSD_CONV_CH = SSD_HEADS * SSD_P + 2 * SSD_GROUPS * SSD_N
RWKV_HEADS = 8
RWKV_HD = 64
RWKV_W_RANK = 64
RWKV_A_RANK = 64
RWKV_G_RANK = 128
RWKV_DECAY_SCALE = 0.6065306597126334
RWKV_LN_EPS = 64e-5
RWKV_BLOCK = 3 * BRANCH_W + RWKV_W_RANK + RWKV_A_RANK + RWKV_G_RANK
GLA_HEADS = 4
GLA_DK = 64
GLA_DV = 128
GLA_GATE_RANK = 16
GLA_GATE_NORM = 16.0
CHUNK = 64
D_FF = ((8 * D_MODEL // 3 + 255) // 256) * 256
IN_SPLITS = (ATT_HEADS * HEAD_DIM, ATT_KV * HEAD_DIM, ATT_KV * HEAD_DIM,
             SSD_HEADS * SSD_P, SSD_HEADS * SSD_P, SSD_GROUPS * SSD_N, SSD_GROUPS * SSD_N, 2 * SSD_HEADS,
             RWKV_BLOCK,
             GLA_HEADS * GLA_DK, GLA_HEADS * GLA_DK, GLA_HEADS * GLA_DV, GLA_GATE_RANK, GLA_HEADS * GLA_DV)
D_IN = sum(IN_SPLITS)
F32 = jnp.float32

kernel_name = 'hybrid_diffusion_prefix_step'


def rms_norm(x, g, eps=1e-6):
    xf = x.astype(F32)
    y = xf * lax.rsqrt(jnp.mean(xf * xf, axis=-1, keepdims=True) + eps)
    return (y * g.astype(F32)).astype(x.dtype)


def split_cols(a, sizes):
    return jnp.split(a, np.cumsum(sizes)[:-1].tolist(), axis=-1)


def flip(a):
    return jnp.flip(a, axis=1)


def rope_2d(x, rows, cols):
    half = HEAD_DIM // 2
    nf = half // 2
    freqs = ROPE_THETA ** (-jnp.arange(nf, dtype=F32) / nf)

    def rot(xh, pos):
        ang = pos.astype(F32)[:, None] * freqs[None, :]
        cos = jnp.cos(ang)[None, :, None, :]
        sin = jnp.sin(ang)[None, :, None, :]
        x1 = xh[..., :nf].astype(F32)
        x2 = xh[..., nf:].astype(F32)
        return jnp.concatenate([x1 * cos - x2 * sin, x2 * cos + x1 * sin], axis=-1)

    return jnp.concatenate([rot(x[..., :half], rows), rot(x[..., half:], cols)], axis=-1).astype(x.dtype)


def blocked_attention(q, k, v):
    b, t = q.shape[:2]
    grp = ATT_HEADS // ATT_KV
    qb = q.astype(F32).reshape(b, t // Q_BLOCK, Q_BLOCK, ATT_KV, grp, HEAD_DIM).swapaxes(0, 1)
    kf = k.astype(F32)
    vf = v.astype(F32)
    scale = HEAD_DIM ** -0.5

    def one_block(qblk):
        s = jnp.einsum('bqkgd,bskd->bkgqs', qblk, kf) * scale
        w = jax.nn.softmax(s, axis=-1)
        return jnp.einsum('bkgqs,bskd->bqkgd', w, vf)

    o = lax.map(one_block, qb)
    return o.swapaxes(0, 1).reshape(b, t, ATT_HEADS * HEAD_DIM).astype(q.dtype)


def chunk_scan(q, k, v, logdec, s0):
    b, t, h, _ = q.shape
    nc = t // CHUNK
    scalar = logdec.shape[-1] == 1
    mask = jnp.tril(jnp.ones((CHUNK, CHUNK), dtype=bool))[None, :, :, None, None]

    def to_chunks(a):
        return a.astype(F32).reshape(b, nc, CHUNK, *a.shape[2:]).swapaxes(0, 1)

    def step(S, inp):
        qc, kc, vc, gc = inp
        G = jnp.cumsum(gc, axis=1)
        o = jnp.einsum('bchk,bhkv->bchv', qc * jnp.exp(G), S)
        diff = G[:, :, None] - G[:, None, :]
        dec = jnp.exp(jnp.where(mask, diff, -jnp.inf))
        if scalar:
            att = jnp.einsum('bihk,bjhk->bhij', qc, kc) * dec[..., 0].transpose(0, 3, 1, 2)
        else:
            att = jnp.einsum('bihk,bjhk,bijhk->bhij', qc, kc, dec)
        o = o + jnp.einsum('bhij,bjhv->bihv', att, vc)
        g_last = G[:, -1]
        kd = kc * jnp.exp(g_last[:, None] - G)
        S = jnp.exp(g_last)[..., None] * S + jnp.einsum('bchk,bchv->bhkv', kd, vc)
        return S, o

    S, o = lax.scan(step, s0.astype(F32), (to_chunks(q), to_chunks(k), to_chunks(v), to_chunks(logdec)))
    o = o.swapaxes(0, 1).reshape(b, t, h, v.shape[-1])
    return o.astype(v.dtype), S


def rwkv_scan(r, w, k, v, kk, a, s0):
    def step(S, inp):
        rt, wt, kt, vt, kkt, at = inp
        sa = jnp.einsum('bhvk,bhk->bhv', S, -kkt)
        S = S * wt[:, :, None, :] + sa[..., None] * (kkt * at)[:, :, None, :] + vt[..., None] * kt[:, :, None, :]
        return S, jnp.einsum('bhvk,bhk->bhv', S, rt)

    xs = tuple(z.astype(F32).swapaxes(0, 1) for z in (r, w, k, v, kk, a))
    S, o = lax.scan(step, s0.astype(F32), xs)
    return o.swapaxes(0, 1), S


def centred_dwconv(a, w, bias):
    pad = w.shape[-1] // 2
    t = a.shape[1]
    ap = jnp.pad(a, ((0, 0), (pad, pad), (0, 0)))
    out = bias + ap[:, 0:t] * w[:, 0]
    for j in range(1, w.shape[-1]):
        out = out + ap[:, j:j + t] * w[:, j]
    return out


def token_shift(a):
    ap = jnp.pad(a, ((0, 0), (1, 1), (0, 0)))
    return 0.5 * (ap[:, :-2] + ap[:, 2:])


def attention_branch(aq, ak, av, q_norm, k_norm, pos, ctx_k, ctx_v):
    b, t = aq.shape[:2]
    q = rms_norm(aq.reshape(b, t, ATT_HEADS, HEAD_DIM), q_norm)
    k = rms_norm(ak.reshape(b, t, ATT_KV, HEAD_DIM), k_norm)
    v = av.reshape(b, t, ATT_KV, HEAD_DIM)
    if pos is None:
        return blocked_attention(q, k, v), k, v
    q = rope_2d(q, *pos)
    k = rope_2d(k, *pos)
    k_all = jnp.concatenate([k, ctx_k.astype(k.dtype)], axis=1)
    v_all = jnp.concatenate([v, ctx_v.astype(v.dtype)], axis=1)
    return blocked_attention(q, k_all, v_all), k, v


def ssd_branch(sz, sx, sb, sc, sdt, p, s0):
    b, t = sx.shape[:2]
    xbc = jax.nn.silu(centred_dwconv(jnp.concatenate([sx, sb, sc], axis=-1), p['ssd_conv_w'], p['ssd_conv_b']))
    xs, bm, cm = split_cols(xbc, (SSD_HEADS * SSD_P, SSD_GROUPS * SSD_N, SSD_GROUPS * SSD_N))
    rep = SSD_HEADS // SSD_GROUPS
    xs = xs.reshape(b, t, SSD_HEADS, SSD_P)
    bm = jnp.repeat(bm.reshape(b, t, SSD_GROUPS, SSD_N), rep, axis=2)
    cm = jnp.repeat(cm.reshape(b, t, SSD_GROUPS, SSD_N), rep, axis=2)
    dt = jax.nn.softplus(sdt.reshape(b, t, 2, SSD_HEADS).astype(F32) + p['ssd_dt_bias'].astype(F32))
    logdec = dt * -jnp.exp(p['ssd_a_log'].astype(F32))
    kf = bm[:, :, None].astype(F32) * dt[..., None]
    y_f, s_f = chunk_scan(cm, kf[:, :, 0], xs, logdec[:, :, 0, :, None], s0[:, 0])
    y_b, s_b = chunk_scan(flip(cm), flip(kf[:, :, 1]), flip(xs), flip(logdec[:, :, 1, :, None]), s0[:, 1])
    y = y_f + flip(y_b) + p['ssd_d'][:, None] * xs
    y = rms_norm(y.reshape(b, t, SSD_HEADS * SSD_P) * jax.nn.silu(sz), p['ssd_norm'])
    return y, jnp.stack([s_f, s_b], axis=1)


def rwkv_branch(blk, p, s0):
    b, t = blk.shape[:2]
    blk = blk + (token_shift(blk) - blk) * p['rwkv_mu']
    r, k, v, wl, al, gl = split_cols(blk, (BRANCH_W, BRANCH_W, BRANCH_W, RWKV_W_RANK, RWKV_A_RANK, RWKV_G_RANK))

    def heads(z):
        return z.astype(F32).reshape(b, t, RWKV_HEADS, RWKV_HD)

    w_logit = p['rwkv_w0'] + jnp.einsum('btr,zrc->btzc', jnp.tanh(wl), p['rwkv_w2'])
    decay = jnp.exp(-RWKV_DECAY_SCALE * jax.nn.sigmoid(w_logit.astype(F32))).reshape(b, t, 2, RWKV_HEADS, RWKV_HD)
    a = heads(jax.nn.sigmoid(p['rwkv_a0'] + al @ p['rwkv_a2']))
    g = jax.nn.sigmoid(gl) @ p['rwkv_g2']
    r_h, k_h, v_h = heads(r), heads(k), heads(v)
    kk = k_h * p['rwkv_kk'].astype(F32).reshape(RWKV_HEADS, RWKV_HD)
    kk = kk * lax.rsqrt(jnp.sum(kk * kk, axis=-1, keepdims=True) + 1e-12)
    k_h = k_h * (1.0 + (a - 1.0) * p['rwkv_ka'].astype(F32).reshape(RWKV_HEADS, RWKV_HD))
    o_f, s_f = rwkv_scan(r_h, decay[:, :, 0], k_h, v_h, kk, a, s0[:, 0])
    o_b, s_b = rwkv_scan(flip(r_h), flip(decay[:, :, 1]), flip(k_h), flip(v_h), flip(kk), flip(a), s0[:, 1])
    o = o_f + flip(o_b)
    mu = jnp.mean(o, axis=-1, keepdims=True)
    var = jnp.mean(jnp.square(o - mu), axis=-1, keepdims=True)
    o = ((o - mu) * lax.rsqrt(var + RWKV_LN_EPS)).reshape(b, t, BRANCH_W) * p['rwkv_ln_g'] + p['rwkv_ln_b']
    bonus = jnp.sum(r_h * k_h * p['rwkv_rk'].astype(F32).reshape(RWKV_HEADS, RWKV_HD), axis=-1, keepdims=True) * v_h
    o = (o + bonus.reshape(b, t, BRANCH_W)) * g
    return o.astype(blk.dtype), jnp.stack([s_f, s_b], axis=1)


def gla_branch(gq, gk, gv, ggl, gog, p, s0):
    b, t = gq.shape[:2]
    q = gq.reshape(b, t, GLA_HEADS, GLA_DK) * (GLA_DK ** -0.5)
    k = gk.reshape(b, t, GLA_HEADS, GLA_DK)
    v = gv.reshape(b, t, GLA_HEADS, GLA_DV)
    logit = jnp.einsum('btr,zrk->btzk', ggl, p['gla_g2']) + p['gla_gb']
    log_a = (jax.nn.log_sigmoid(logit.astype(F32)) / GLA_GATE_NORM).reshape(b, t, 2, GLA_HEADS, GLA_DK)
    o_f, s_f = chunk_scan(q, k, v, log_a[:, :, 0], s0[:, 0])
    o_b, s_b = chunk_scan(flip(q), flip(k), flip(v), flip(log_a[:, :, 1]), s0[:, 1])
    o = rms_norm(o_f + flip(o_b), p['gla_norm']).reshape(b, t, GLA_HEADS * GLA_DV) * jax.nn.silu(gog)
    return o, jnp.stack([s_f, s_b], axis=1)


def token_mix(h, p, pos, ctx):
    b = h.shape[0]
    (aq, ak, av, sz, sx, sb, sc, sdt, rblk, gq, gk, gv, ggl, gog) = split_cols(h @ p['w_in'], IN_SPLITS)
    if ctx is None:
        ctx_k = ctx_v = None
        s_ssd = jnp.zeros((b, 2, SSD_HEADS, SSD_N, SSD_P), F32)
        s_rwkv = jnp.zeros((b, 2, RWKV_HEADS, RWKV_HD, RWKV_HD), F32)
        s_gla = jnp.zeros((b, 2, GLA_HEADS, GLA_DK, GLA_DV), F32)
    else:
        ctx_k, ctx_v, s_ssd, s_rwkv, s_gla = ctx
    o_att, k_own, v_own = attention_branch(aq, ak, av, p['q_norm'], p['k_norm'], pos, ctx_k, ctx_v)
    o_ssd, s_ssd = ssd_branch(sz, sx, sb, sc, sdt, p, s_ssd)
    o_rwkv, s_rwkv = rwkv_branch(rblk, p, s_rwkv)
    o_gla, s_gla = gla_branch(gq, gk, gv, ggl, gog, p, s_gla)
    merged = None
    for i, o in enumerate((o_att, o_ssd, o_rwkv, o_gla)):
        term = jax.nn.sigmoid(h @ p['w_gate'][i]) * (o @ p['w_branch'][i])
        merged = term if merged is None else merged + term
    return merged @ p['w_o'], (k_own, v_own, s_ssd, s_rwkv, s_gla)


def block(x, cond, p, pos, ctx):
    mod = (jax.nn.silu(cond) @ p['w_mod'] + p['b_mod']).reshape(-1, 1, 6 * D_MODEL)
    sh1, sc1, g1, sh2, sc2, g2 = jnp.split(mod, 6, axis=-1)
    h = rms_norm(x, p['norm1']) * (1 + sc1) + sh1
    mix, ctx_out = token_mix(h, p, pos, ctx)
    x = x + g1 * mix
    h = rms_norm(x, p['norm2']) * (1 + sc2) + sh2
    ff = (jax.nn.silu(h @ p['ffn_w1']) * (h @ p['ffn_w3'])) @ p['ffn_w2']
    return x + g2 * ff, ctx_out


def setup_inputs(seed: int = 0) -> dict:
    key = jax.random.key(seed)
    keys = iter(jax.random.split(key, 64))

    def nrm(shape, scale):
        return jax.random.normal(next(keys), shape, F32) * scale

    def gain(shape):
        return 1.0 + nrm(shape, 0.02)

    L, D = DEPTH, D_MODEL
    x_prompt = nrm((BATCH, SEQ, D), 1.0)
    x_sample = nrm((DEC_BATCH, DEC_SEQ, D), 1.0)
    cache_attn_k = nrm((DEC_BATCH, L, PAST_LEN, ATT_KV, HEAD_DIM), 1.0)
    cache_attn_v = nrm((DEC_BATCH, L, PAST_LEN, ATT_KV, HEAD_DIM), 1.0)
    state_ssd = nrm((DEC_BATCH, L, 2, SSD_HEADS, SSD_N, SSD_P), 0.3)
    state_rwkv = nrm((DEC_BATCH, L, 2, RWKV_HEADS, RWKV_HD, RWKV_HD), 0.3)
    state_gla = nrm((DEC_BATCH, L, 2, GLA_HEADS, GLA_DK, GLA_DV), 0.3)
    c = nrm((DEC_BATCH, D), 1.0)
    c_ctx = nrm((D,), 1.0)
    w_mod = nrm((L, D, 6 * D), 0.5 * D ** -0.5)
    b_mod = nrm((L, 6 * D), 0.02)
    norm1 = gain((L, D))
    norm2 = gain((L, D))
    w_in = nrm((L, D, D_IN), D ** -0.5)
    q_norm = gain((L, HEAD_DIM))
    k_norm = gain((L, HEAD_DIM))
    ssd_conv_w = nrm((L, SSD_CONV_CH, SSD_CONV), 0.5)
    ssd_conv_b = nrm((L, SSD_CONV_CH), 0.02)
    dt0 = jnp.exp(jax.random.uniform(next(keys), (L, 2, SSD_HEADS), F32, -6.9, -2.3))
    ssd_dt_bias = dt0 + jnp.log(-jnp.expm1(-dt0))
    ssd_a_log = jnp.log(jax.random.uniform(next(keys), (L, 2, SSD_HEADS), F32, 1.0, 16.0))
    ssd_d = gain((L, SSD_HEADS))
    ssd_norm = gain((L, SSD_HEADS * SSD_P))
    rwkv_mu = jax.random.uniform(next(keys), (L, RWKV_BLOCK), F32)
    rwkv_w0 = jax.random.uniform(next(keys), (L, 2, BRANCH_W), F32, -3.0, 1.0)
    rwkv_w2 = nrm((L, 2, RWKV_W_RANK, BRANCH_W), 0.1)
    rwkv_a0 = nrm((L, BRANCH_W), 0.5)
    rwkv_a2 = nrm((L, RWKV_A_RANK, BRANCH_W), 0.5 * RWKV_A_RANK ** -0.5)
    rwkv_g2 = nrm((L, RWKV_G_RANK, BRANCH_W), RWKV_G_RANK ** -0.5)
    rwkv_kk = 0.85 + nrm((L, BRANCH_W), 0.02)
    rwkv_ka = gain((L, BRANCH_W))
    rwkv_rk = nrm((L, BRANCH_W), 0.1)
    rwkv_ln_g = gain((L, BRANCH_W))
    rwkv_ln_b = nrm((L, BRANCH_W), 0.02)
    gla_g2 = nrm((L, 2, GLA_GATE_RANK, GLA_HEADS * GLA_DK), 0.5 * GLA_GATE_RANK ** -0.5)
    gla_gb = 1.0 + nrm((L, 2, GLA_HEADS * GLA_DK), 0.5)
    gla_norm = gain((L, GLA_DV))
    w_gate = nrm((L, N_BRANCH, D, D), D ** -0.5)
    w_branch = nrm((L, N_BRANCH, BRANCH_W, D), BRANCH_W ** -0.5)
    w_o = nrm((L, D, D), D ** -0.5)
    ffn_w1 = nrm((L, D, D_FF), D ** -0.5)
    ffn_w3 = nrm((L, D, D_FF), D ** -0.5)
    ffn_w2 = nrm((L, D_FF, D), D_FF ** -0.5)
    final_norm = gain((D,))
    return {'x_prompt': x_prompt, 'x_sample': x_sample,
            'cache_attn_k': cache_attn_k, 'cache_attn_v': cache_attn_v,
            'state_ssd': state_ssd, 'state_rwkv': state_rwkv, 'state_gla': state_gla,
            'c': c, 'c_ctx': c_ctx,
            'w_mod': w_mod, 'b_mod': b_mod, 'norm1': norm1, 'norm2': norm2, 'w_in': w_in,
            'q_norm': q_norm, 'k_norm': k_norm,
            'ssd_conv_w': ssd_conv_w, 'ssd_conv_b': ssd_conv_b, 'ssd_dt_bias': ssd_dt_bias,
            'ssd_a_log': ssd_a_log, 'ssd_d': ssd_d, 'ssd_norm': ssd_norm,
            'rwkv_mu': rwkv_mu, 'rwkv_w0': rwkv_w0, 'rwkv_w2': rwkv_w2, 'rwkv_a0': rwkv_a0,
            'rwkv_a2': rwkv_a2, 'rwkv_g2': rwkv_g2, 'rwkv_kk': rwkv_kk, 'rwkv_ka': rwkv_ka,
            'rwkv_rk': rwkv_rk, 'rwkv_ln_g': rwkv_ln_g, 'rwkv_ln_b': rwkv_ln_b,
            'gla_g2': gla_g2, 'gla_gb': gla_gb, 'gla_norm': gla_norm,
            'w_gate': w_gate, 'w_branch': w_branch, 'w_o': w_o,
            'ffn_w1': ffn_w1, 'ffn_w3': ffn_w3, 'ffn_w2': ffn_w2, 'final_norm': final_norm}


def reference(x_prompt, x_sample, cache_attn_k, cache_attn_v, state_ssd, state_rwkv, state_gla, c, c_ctx,
              w_mod, b_mod, norm1, norm2, w_in, q_norm, k_norm,
              ssd_conv_w, ssd_conv_b, ssd_dt_bias, ssd_a_log, ssd_d, ssd_norm,
              rwkv_mu, rwkv_w0, rwkv_w2, rwkv_a0, rwkv_a2, rwkv_g2, rwkv_kk, rwkv_ka, rwkv_rk,
              rwkv_ln_g, rwkv_ln_b, gla_g2, gla_gb, gla_norm,
              w_gate, w_branch, w_o, ffn_w1, ffn_w3, ffn_w2, final_norm):
    def params_at(l):
        return dict(w_mod=w_mod[l], b_mod=b_mod[l], norm1=norm1[l], norm2=norm2[l], w_in=w_in[l],
                    q_norm=q_norm[l], k_norm=k_norm[l],
                    ssd_conv_w=ssd_conv_w[l], ssd_conv_b=ssd_conv_b[l], ssd_dt_bias=ssd_dt_bias[l],
                    ssd_a_log=ssd_a_log[l], ssd_d=ssd_d[l], ssd_norm=ssd_norm[l],
                    rwkv_mu=rwkv_mu[l], rwkv_w0=rwkv_w0[l], rwkv_w2=rwkv_w2[l], rwkv_a0=rwkv_a0[l],
                    rwkv_a2=rwkv_a2[l], rwkv_g2=rwkv_g2[l], rwkv_kk=rwkv_kk[l], rwkv_ka=rwkv_ka[l],
                    rwkv_rk=rwkv_rk[l], rwkv_ln_g=rwkv_ln_g[l], rwkv_ln_b=rwkv_ln_b[l],
                    gla_g2=gla_g2[l], gla_gb=gla_gb[l], gla_norm=gla_norm[l],
                    w_gate=w_gate[l], w_branch=w_branch[l], w_o=w_o[l],
                    ffn_w1=ffn_w1[l], ffn_w3=ffn_w3[l], ffn_w2=ffn_w2[l])

    xp = x_prompt
    new_k, new_v, new_ssd, new_rwkv, new_gla = [], [], [], [], []
    for l in range(DEPTH):
        xp, (k_l, v_l, ssd_l, rwkv_l, gla_l) = block(xp, c_ctx, params_at(l), None, None)
        new_k.append(k_l)
        new_v.append(v_l)
        new_ssd.append(ssd_l)
        new_rwkv.append(rwkv_l)
        new_gla.append(gla_l)

    rows = x_sample.shape[1] // GRID_W
    t_idx = jnp.arange(rows * GRID_W)
    pos = (t_idx // GRID_W, t_idx % GRID_W)
    xs = x_sample
    for l in range(DEPTH):
        ctx = (cache_attn_k[:, l], cache_attn_v[:, l], state_ssd[:, l], state_rwkv[:, l], state_gla[:, l])
        xs, _ = block(xs, c, params_at(l), pos, ctx)

    y_prompt = rms_norm(xp, final_norm)
    y_sample = rms_norm(xs, final_norm)
    return (y_prompt, y_sample, jnp.stack(new_k, axis=1), jnp.stack(new_v, axis=1),
            jnp.stack(new_ssd, axis=1), jnp.stack(new_rwkv, axis=1), jnp.stack(new_gla, axis=1))
```

```python
import numpy as np
import concourse.bass as bass
import concourse.mybir as mybir
from concourse.bass_utils import run_bass_kernel_spmd
from contextlib import ExitStack

F32 = mybir.dt.float32
BF16 = mybir.dt.bfloat16
AF = mybir.ActivationFunctionType
ALU = mybir.AluOpType
AX = mybir.AxisListType

D = 2048
L = 2
KC = D // 128
DFF = 5632
FC = DFF // 128
D_IN = 5408
N_CORES = 8


DEBUG_SITES = True


def _site():
    if not DEBUG_SITES:
        return None
    import sys
    f = sys._getframe(2)
    out = []
    while f is not None and len(out) < 4:
        if f.f_code.co_name not in ("X", "ACT", "TT_", "TS_", "STT", "COPY", "LOAD", "STORE", "mm_group", "transposes"):
            out.append(f"{f.f_code.co_name}:{f.f_lineno}")
        f = f.f_back
    return out


class Buf:
    __slots__ = ("w", "r")

    def __init__(self):
        self.w = None
        self.r = {}


class Tile:
    __slots__ = ("t", "b")

    def __init__(self, t, b=None):
        self.t = t
        self.b = b if b is not None else Buf()

    def __getitem__(self, k):
        return self.t[k]


def _bufs(xs):
    return [x.b if isinstance(x, Tile) else x for x in xs]


class EngState:
    def __init__(self, fw, name):
        self.fw = fw
        self.name = name
        self.prog = []
        self.known = {}
        self.sem = None
        self.cnt = 0
        self.dma_sems = []
        self.dma_cnt = []
        self.dma_rr = 0

    def new_sem(self):
        self.sem = self.fw.nc.alloc_semaphore(f"s_{self.name}_{self.fw.nsem}")
        self.fw.nsem += 1
        self.cnt = 0


class FW:
    SEM_MAX = 8000
    DMA_SEM_MAX = 500

    def __init__(self, nc, n_dma_sems=12):
        self.nc = nc
        self.nsem = 0
        self.eng = {}
        self.old_sems = []
        for n in ("pe", "act", "dve", "pool", "sp"):
            st = EngState(self, n)
            self.eng[n] = st
            if n != "sp":
                st.new_sem()
        for n in ("sp", "pool", "act"):
            st = self.eng[n]
            k = n_dma_sems if n == "sp" else 8
            for i in range(k):
                st.dma_sems.append(nc.alloc_semaphore(f"d_{n}_{i}"))
                st.dma_cnt.append(0)
                self.nsem += 1
        self.n_ops = 0

    def _wait(self, st, tok):
        sem, val = tok
        if st.known.get(sem.num, 0) < val:
            st.prog.append(("wait", sem, val))
            st.known[sem.num] = val

    def _deps(self, eng, reads, writes):
        st = self.eng[eng]
        deps = {}

        def add(tok):
            if tok is None:
                return
            s, v = tok
            if deps.get(s.num, (None, 0))[1] < v:
                deps[s.num] = (s, v)

        for b in reads:
            add(b.w)
        for b in writes:
            add(b.w)
            for t in b.r.values():
                add(t)
        for s, v in deps.values():
            if eng == "pe" and st.sem is not None and s.num == st.sem.num:
                continue
            self._wait(st, (s, v))

    def _mark(self, tok, reads, writes):
        s, v = tok
        for b in reads:
            b.r[s.num] = tok
        for b in writes:
            b.w = tok
            b.r = {}

    def op(self, eng, fn, reads=(), writes=()):
        reads = _bufs(reads)
        writes = _bufs(writes)
        st = self.eng[eng]
        self._deps(eng, reads, writes)
        if st.cnt >= self.SEM_MAX:
            self.old_sems.append((st.sem, st.cnt))
            st.new_sem()
        st.cnt += 1
        tok = (st.sem, st.cnt)
        st.prog.append(("op", fn, st.sem, 1, _site()))
        self._mark(tok, reads, writes)
        self.n_ops += 1
        return tok

    def dma(self, q, out, in_, reads=(), writes=(), **kw):
        reads = _bufs(reads)
        writes = _bufs(writes)
        st = self.eng[q]
        self._deps(q, reads, writes)
        i = st.dma_rr
        st.dma_rr = (i + 1) % len(st.dma_sems)
        sem = st.dma_sems[i]
        c = st.dma_cnt[i]
        if c > 0:
            self._wait(st, (sem, 16 * c))
        if c >= self.DMA_SEM_MAX:
            self.old_sems.append((sem, 16 * c))
            sem = self.nc.alloc_semaphore(f"d_{q}_{self.nsem}")
            self.nsem += 1
            st.dma_sems[i] = sem
            c = 0
        st.dma_cnt[i] = c + 1
        tok = (sem, 16 * (c + 1))
        st.prog.append(("op", lambda e: e.dma_start(out=out, in_=in_, **kw), sem, 16, _site()))
        self._mark(tok, reads, writes)
        self.n_ops += 1
        return tok

    def barrier(self, engines=("pe", "act", "dve", "pool", "sp")):
        toks = list(self.old_sems)
        for n, st in self.eng.items():
            if st.sem is not None and st.cnt > 0:
                toks.append((st.sem, st.cnt))
            for sem, c in zip(st.dma_sems, st.dma_cnt):
                if c > 0:
                    toks.append((sem, 16 * c))
        for n in engines:
            st = self.eng[n]
            for t in toks:
                if st.sem is not None and t[0].num == st.sem.num:
                    continue
                self._wait(st, t)

    def finish(self):
        self.barrier(engines=("sp",))

    def emit(self):
        nc = self.nc
        engs = {"pe": "tensor", "act": "scalar", "dve": "vector", "pool": "gpsimd", "sp": "sync"}
        with nc.Block() as block:
            for n, attr in engs.items():
                st = self.eng[n]

                def body(e, st=st):
                    for it in st.prog:
                        if it[0] == "wait":
                            e.wait_ge(it[1], it[2])
                        else:
                            try:
                                ins = it[1](e)
                            except Exception:
                                print("FAILED OP SITE:", it[4])
                                raise
                            ins.then_inc(it[2], it[3])

                getattr(block, attr)(body)


class Ctx:
    def __init__(self, nc, cfg, dbg=()):
        self.nc = nc
        self.fw = FW(nc)
        self.cfg = cfg
        self.dbg = set(dbg)
        self.dr = {}
        self.uid = 0
        self.es = None
        self.ps = [Tile(nc.alloc_psum_tensor(f"psb{i}", [128, 512], F32)) for i in range(8)]
        self.ps_i = 0
        self.ps_n = 8
        self.ev_i = 0
        self.TS = cfg["TS"]
        self.NP = cfg["NP"]
        self.TP = cfg["TP"]
        self.PAST = cfg["PAST"]
        self.TT = self.TS + self.NP * self.TP
        assert self.TS % 512 == 0 and (self.NP * self.TP) % 512 == 0
        self.NTILE = self.TT // 512
        self.seqs = [(0, self.TS, True, -1)] + [(self.TS + p * self.TP, self.TP, False, p) for p in range(self.NP)]

    def psum(self):
        p = self.ps[self.ps_i]
        self.ps_i = (self.ps_i + 1) % self.ps_n
        return p

    def dram(self, name, shape, dtype=F32, kind=None):
        if kind is None:
            kind = "ExternalOutput" if name in self.dbg else "Internal"
        t = self.nc.dram_tensor(name, list(shape), dtype, kind=kind).ap()
        self.dr[name] = t
        return t

    def phase_begin(self):
        assert self.es is None
        self.es = ExitStack()

    def phase_end(self):
        self.fw.barrier()
        self.es.close()
        self.es = None

    def sb(self, shape, dtype=F32, name="t"):
        self.uid += 1
        t = self.es.enter_context(self.nc.sbuf_tensor(f"{name}_{self.uid}", list(shape), dtype))
        return Tile(t)

    def ring(self, n, shape, dtype=F32, name="r"):
        return Ring([self.sb(shape, dtype, name) for _ in range(n)])

    def evac_eng(self):
        self.ev_i += 1
        return "act" if self.ev_i % 2 == 0 else "dve"

    def copy(self, eng, out_ap, in_ap, reads, writes):
        if eng == "act":
            return self.fw.op("act", lambda e: e.activation(out=out_ap, in_=in_ap, func=AF.Copy), reads, writes)
        return self.fw.op(eng, lambda e: e.tensor_copy(out_ap, in_ap), reads, writes)


class Ring:
    def __init__(self, tiles):
        self.tiles = tiles
        self.i = 0

    def next(self):
        t = self.tiles[self.i]
        self.i = (self.i + 1) % len(self.tiles)
        return t


FM_JOBS = []
for c in range(5):
    FM_JOBS.append((c * 128, 128, "qkT", c * 128))
for c in range(6):
    FM_JOBS.append((1280 + c * 128, 128, "xbcT", c * 128))
for c in range(14):
    FM_JOBS.append((2064 + c * 128, 128, "rblkT", c * 128))
for c in range(4):
    FM_JOBS.append((3856 + c * 128, 128, "gqkT", c * 128))
FM_JOBS.append((4880, 16, "gglT", 0))
TM_GROUPS = [
    [(640, 128, "av_tm", 0), (2048, 16, "sdt_tm", 0), (4112, 256, "gk_tm", 0)],
    [(768, 512, "sz_tm", 0)],
    [(4368, 512, "gv_tm", 0)],
    [(4896, 512, "gog_tm", 0)],
]


def _pack_cols(W, c0, n):
    K = W.shape[0]
    blk = W[:, c0:c0 + n].reshape(K // 128, 128, n).transpose(1, 0, 2).reshape(128, (K // 128) * n)
    return blk


def pack_w_in(w_in_l):
    parts = []
    offs_fm = []
    off = 0
    for (c0, n, _, _) in FM_JOBS:
        parts.append(_pack_cols(w_in_l, c0, n))
        offs_fm.append(off)
        off += KC * n
    offs_tm = []
    for grp in TM_GROUPS:
        cols = np.concatenate([np.arange(c0, c0 + n) for (c0, n, _, _) in grp])
        Wg = w_in_l[:, cols]
        parts.append(_pack_cols(Wg, 0, Wg.shape[1]))
        offs_tm.append(off)
        off += KC * Wg.shape[1]
    return np.ascontiguousarray(np.concatenate(parts, axis=1)), offs_fm, offs_tm


def w_in_offsets():
    offs_fm = []
    off = 0
    for (c0, n, _, _) in FM_JOBS:
        offs_fm.append(off)
        off += KC * n
    offs_tm = []
    for grp in TM_GROUPS:
        n = sum(g[1] for g in grp)
        offs_tm.append(off)
        off += KC * n
    return offs_fm, offs_tm, off


def pack_chunks(W):
    C = W.shape[1]
    return np.ascontiguousarray(np.concatenate([_pack_cols(W, c * 128, 128) for c in range(C // 128)], axis=1))


def vec_pp(v):
    return np.ascontiguousarray(v.reshape(-1, 128).T)


def _bind(method, args, kw):
    return lambda e: getattr(e, method)(*args, **kw)


def X(cx, eng, method, *args, reads=(), writes=(), **kw):
    return cx.fw.op(eng, _bind(method, args, kw), reads, writes)


def ACT(cx, out, in_, func, reads, writes, **kw):
    return X(cx, "act", "activation", out=out, in_=in_, func=func, reads=reads, writes=writes, **kw)


def TT_(cx, eng, out, in0, in1, op, reads, writes):
    return X(cx, eng, "tensor_tensor", out, in0, in1, op, reads=reads, writes=writes)


def TS_(cx, eng, out, in0, s1, s2, op0, op1, reads, writes):
    if s2 is None:
        return X(cx, eng, "tensor_scalar", out, in0, s1, None, op0, reads=reads, writes=writes)
    return X(cx, eng, "tensor_scalar", out, in0, s1, s2, op0, op1, reads=reads, writes=writes)


def STT(cx, eng, out, in0, scalar, in1, op0, op1, reads, writes):
    return X(cx, eng, "scalar_tensor_tensor", out, in0, scalar, in1, op0, op1, reads=reads, writes=writes)


def COPY(cx, eng, out, in_, reads, writes):
    if eng == "act":
        return ACT(cx, out, in_, AF.Copy, reads, writes)
    return X(cx, eng, "tensor_copy", out, in_, reads=reads, writes=writes)


def _mm_fn(ps_ap, pairs):
    n = len(pairs)

    def fn(e):
        ins = None
        for i, (l, r) in enumerate(pairs):
            ins = e.matmul(ps_ap, lhsT=l, rhs=r, start=(i == 0), stop=(i == n - 1))
        return ins
    return fn


def mm_group(cx, ps_ap, pairs, reads, ps_tile):
    return cx.fw.op("pe", _mm_fn(ps_ap, list(pairs)), reads=reads, writes=[ps_tile])


def _tr_fn(items, ident_ap):
    def fn(e):
        ins = None
        for (o, i) in items:
            ins = e.transpose(o, i, ident_ap)
        return ins
    return fn


def transposes(cx, items, reads, ps_tile, np_=128):
    ident = cx.C["ident"]
    return cx.fw.op("pe", _tr_fn(list(items), ident[0:np_, 0:np_]), reads=list(reads) + [ident], writes=[ps_tile])


def LOAD(cx, out_ap, in_ap, tile_):
    return cx.fw.dma("pool", out_ap, in_ap, writes=[tile_])


def STORE(cx, out_ap, in_ap, tile_):
    return cx.fw.dma("sp", out_ap, in_ap, reads=[tile_])


def load_consts(cx):
    nc = cx.nc
    C = {}
    for name in cx.const_names:
        shp = list(cx.inp[name].shape)
        t = Tile(nc.alloc_sbuf_tensor("c_" + name, shp, F32))
        LOAD(cx, t[:], cx.inp[name], t)
        C[name] = t
    cx.C = C


def phase_mod(cx):
    nc = cx.nc
    cx.MOD = []
    pers = lambda name, shape: Tile(nc.alloc_sbuf_tensor("m_" + name, list(shape), F32))
    Ms = [{k: pers(f"{k}_{l}", [128, KC, 2]) for k in ("A1", "B1", "G1", "A2", "B2", "G2")} for l in range(L)]
    fn = pers("fnorm", [128, KC])
    cx.phase_begin()
    cond = cx.sb([128, KC, 2])
    LOAD(cx, cond[:], cx.inp["condT"], cond)
    scond = cx.sb([128, KC, 2])
    ACT(cx, scond[:], cond[:], AF.Silu, [cond], [scond])
    wring = cx.ring(3, [128, KC * 128], F32, "wmod")
    for l in range(L):
        modT = cx.sb([128, 96, 2])
        bm = cx.sb([128, 96])
        LOAD(cx, bm[:], cx.inp["b_modT"][l], bm)
        for c in range(96):
            wt = wring.next()
            LOAD(cx, wt[:], cx.inp["w_mod_p"][l, :, c * KC * 128:(c + 1) * KC * 128], wt)
            ps = cx.psum()
            mm_group(cx, ps[:, 0:2], [(wt[:, k * 128:(k + 1) * 128], scond[:, k, :]) for k in range(KC)], [wt, scond], ps)
            TS_(cx, "dve", modT[:, c, :], ps[:, 0:2], bm[:, c:c + 1], None, ALU.add, None, [ps, bm], [modT])
        n1 = cx.sb([128, KC])
        n2 = cx.sb([128, KC])
        LOAD(cx, n1[:], cx.inp["norm1T"][l], n1)
        LOAD(cx, n2[:], cx.inp["norm2T"][l], n2)
        M = Ms[l]

        def mk(dst, ch, nrm=None):
            src = modT[:, ch * 16:(ch + 1) * 16, :]
            if nrm is None:
                COPY(cx, "dve", dst[:], src, [modT], [dst])
            else:
                STT(cx, "dve", dst[:], src, 1.0, nrm[:].unsqueeze(2).broadcast_to([128, KC, 2]), ALU.add, ALU.mult, [modT, nrm], [dst])
        mk(M["B1"], 0)
        mk(M["A1"], 1, n1)
        mk(M["G1"], 2)
        mk(M["B2"], 3)
        mk(M["A2"], 4, n2)
        mk(M["G2"], 5)
        cx.MOD.append(M)
    LOAD(cx, fn[:], cx.inp["fnormT"], fn)
    cx.FNORM = fn
    cx.phase_end()


def rms_fm(cx, xT, scal, bias, outs, sqring, tmpring, small):
    ps = cx.psum()
    ones = cx.C["ones"]
    for k in range(KC):
        sq = sqring.next()
        ACT(cx, sq[:], xT[:, k, :], AF.Square, [xT], [sq])
        X(cx, "pe", "matmul", ps[:], lhsT=ones[:], rhs=sq[:], start=(k == 0), stop=(k == KC - 1), reads=[sq, ones], writes=[ps])
    sd, rstd = small
    ACT(cx, sd[:], ps[:], AF.Sqrt, [ps], [sd], bias=cx.C["eps6"][:, 0:1], scale=1.0 / D)
    X(cx, "dve", "reciprocal", rstd[:], sd[:], reads=[sd], writes=[rstd])
    for k in range(KC):
        oap, ot = outs[k]
        if bias is None:
            STT(cx, "dve", oap, xT[:, k, :], scal[k], rstd[:], ALU.mult, ALU.mult, [xT, rstd], [ot])
        else:
            tmp = tmpring.next()
            STT(cx, "dve", tmp[:], xT[:, k, :], scal[k], rstd[:], ALU.mult, ALU.mult, [xT, rstd], [tmp])
            ACT(cx, oap, tmp[:], AF.Identity, [tmp], [ot], bias=bias[k], scale=1.0)


def phase_A(cx, l):
    offs_fm, offs_tm, _ = w_in_offsets()
    wp = cx.inp["w_in_p"]
    M = cx.MOD[l]
    cx.phase_begin()
    xring = cx.ring(2, [128, KC, 512], F32, "xT")
    hring = cx.ring(2, [128, KC, 512], BF16, "hT")
    sqring = cx.ring(2, [128, 512], F32, "sq")
    tmpring = cx.ring(2, [128, 512], F32, "tmp")
    small = (cx.sb([128, 512]), cx.sb([128, 512]))
    wring = cx.ring(3, [128, KC * 512], BF16, "w")
    oring = cx.ring(4, [128, 512], F32, "o")
    src = cx.dr["xres"] if l > 0 else cx.inp["xT_in"]
    for ti in range(cx.NTILE):
        t0 = ti * 512
        ci = 1 if t0 < cx.TS else 0
        xT = xring.next()
        LOAD(cx, xT[:], src[:, t0:t0 + 512].rearrange("(k p) n -> p k n", p=128), xT)
        hT = hring.next()
        rms_fm(cx, xT, [M["A1"][:, k, ci:ci + 1] for k in range(KC)], [M["B1"][:, k, ci:ci + 1] for k in range(KC)],
               [(hT[:, k, :], hT) for k in range(KC)], sqring, tmpring, small)
        STORE(cx, cx.dr["hT"][:, t0:t0 + 512].rearrange("(k p) n -> p k n", p=128), hT[:], hT)
        for j, (c0, n, dst, doff) in enumerate(FM_JOBS):
            wt = wring.next()
            LOAD(cx, wt[:, 0:KC * n], wp[l, :, offs_fm[j]:offs_fm[j] + KC * n], wt)
            ps = cx.psum()
            mm_group(cx, ps[0:n, :], [(wt[:, k * n:(k + 1) * n], hT[:, k, :]) for k in range(KC)], [wt, hT], ps)
            o = oring.next()
            COPY(cx, cx.evac_eng(), o[0:n, :], ps[0:n, :], [ps], [o])
            STORE(cx, cx.dr[dst][doff:doff + n, t0:t0 + 512], o[0:n, :], o)
        for g, grp in enumerate(TM_GROUPS):
            n = sum(x[1] for x in grp)
            wt = wring.next()
            LOAD(cx, wt[:, 0:KC * n], wp[l, :, offs_tm[g]:offs_tm[g] + KC * n], wt)
            for s in range(4):
                ps = cx.psum()
                mm_group(cx, ps[:, 0:n], [(hT[:, k, s * 128:(s + 1) * 128], wt[:, k * n:(k + 1) * n]) for k in range(KC)], [wt, hT], ps)
                o = oring.next()
                COPY(cx, cx.evac_eng(), o[:, 0:n], ps[:, 0:n], [ps], [o])
                co = 0
                for (c0, nn, dst, dcol) in grp:
                    STORE(cx, cx.dr[dst][t0 + s * 128:t0 + (s + 1) * 128, dcol:dcol + nn], o[:, co:co + nn], o)
                    co += nn
    cx.phase_end()


def phase_C(cx, l, last):
    M = cx.MOD[l]
    cx.phase_begin()
    xT = cx.sb([128, KC, 512], F32, "xT")
    hT = cx.sb([128, KC, 512], BF16, "hT")
    U = cx.sb([128, FC, 512], BF16, "U")
    oin = cx.ring(2, [128, 4, 512], F32, "oin")
    sqring = cx.ring(2, [128, 512], F32, "sq")
    tmpring = cx.ring(4, [128, 512], F32, "tmp")
    small = (cx.sb([128, 512]), cx.sb([128, 512]))
    wring = cx.ring(3, [128, FC * 128], BF16, "w")
    src = cx.dr["xres"] if l > 0 else cx.inp["xT_in"]
    br_names = ["o_att", "o_ssd", "o_rwkv", "o_gla"]
    GSZ = KC * 128
    BSZ = 4 * 128
    W2SZ = FC * 128
    for ti in range(cx.NTILE):
        t0 = ti * 512
        ci = 1 if t0 < cx.TS else 0
        LOAD(cx, xT[:], src[:, t0:t0 + 512].rearrange("(k p) n -> p k n", p=128), xT)
        LOAD(cx, hT[:], cx.dr["hT"][:, t0:t0 + 512].rearrange("(k p) n -> p k n", p=128), hT)
        for i in range(4):
            ot = oin.next()
            LOAD(cx, ot[:], cx.dr[br_names[i]][t0:t0 + 512, :].rearrange("(s p) c -> p s c", p=128), ot)
            for c in range(4):
                ps = cx.psum()
                transposes(cx, [(ps[:, s * 128:(s + 1) * 128], ot[:, s, c * 128:(c + 1) * 128]) for s in range(4)], [ot], ps)
                COPY(cx, cx.evac_eng(), U[:, i * 4 + c, :], ps[:], [ps], [U])
        for f in range(KC):
            acc = None
            for i in range(4):
                wg = wring.next()
                LOAD(cx, wg[:, 0:GSZ], cx.inp["w_gate_p"][l, :, (f * 4 + i) * GSZ:(f * 4 + i + 1) * GSZ], wg)
                wb = wring.next()
                LOAD(cx, wb[:, 0:BSZ], cx.inp["w_br_p"][l, :, (f * 4 + i) * BSZ:(f * 4 + i + 1) * BSZ], wb)
                psg = cx.psum()
                mm_group(cx, psg[:], [(wg[:, k * 128:(k + 1) * 128], hT[:, k, :]) for k in range(KC)], [wg, hT], psg)
                psb = cx.psum()
                mm_group(cx, psb[:], [(wb[:, k * 128:(k + 1) * 128], U[:, i * 4 + k, :]) for k in range(4)], [wb, U], psb)
                sig = tmpring.next()
                ACT(cx, sig[:], psg[:], AF.Sigmoid, [psg], [sig])
                if i == 0:
                    acc = tmpring.next()
                    TT_(cx, "dve", acc[:], sig[:], psb[:], ALU.mult, [sig, psb], [acc])
                else:
                    TT_(cx, "dve", sig[:], sig[:], psb[:], ALU.mult, [sig, psb], [sig])
                    if i < 3:
                        TT_(cx, "dve", acc[:], acc[:], sig[:], ALU.add, [sig, acc], [acc])
                    else:
                        TT_(cx, "dve", U[:, 16 + f, :], acc[:], sig[:], ALU.add, [sig, acc], [U])
        for f in range(KC):
            wo = wring.next()
            LOAD(cx, wo[:, 0:GSZ], cx.inp["w_o_p"][l, :, f * GSZ:(f + 1) * GSZ], wo)
            ps = cx.psum()
            mm_group(cx, ps[:], [(wo[:, k * 128:(k + 1) * 128], U[:, 16 + k, :]) for k in range(KC)], [wo, U], ps)
            STT(cx, "dve", xT[:, f, :], ps[:], M["G1"][:, f, ci:ci + 1], xT[:, f, :], ALU.mult, ALU.add, [ps, xT], [xT])
        rms_fm(cx, xT, [M["A2"][:, k, ci:ci + 1] for k in range(KC)], [M["B2"][:, k, ci:ci + 1] for k in range(KC)],
               [(hT[:, k, :], hT) for k in range(KC)], sqring, tmpring, small)
        for j in range(FC):
            w1 = wring.next()
            LOAD(cx, w1[:, 0:GSZ], cx.inp["w1_p"][l, :, j * GSZ:(j + 1) * GSZ], w1)
            w3 = wring.next()
            LOAD(cx, w3[:, 0:GSZ], cx.inp["w3_p"][l, :, j * GSZ:(j + 1) * GSZ], w3)
            p1 = cx.psum()
            mm_group(cx, p1[:], [(w1[:, k * 128:(k + 1) * 128], hT[:, k, :]) for k in range(KC)], [w1, hT], p1)
            p3 = cx.psum()
            mm_group(cx, p3[:], [(w3[:, k * 128:(k + 1) * 128], hT[:, k, :]) for k in range(KC)], [w3, hT], p3)
            s1 = tmpring.next()
            ACT(cx, s1[:], p1[:], AF.Silu, [p1], [s1])
            TT_(cx, "dve", U[:, j, :], s1[:], p3[:], ALU.mult, [s1, p3], [U])
        for f in range(KC):
            w2 = wring.next()
            LOAD(cx, w2[:, 0:W2SZ], cx.inp["w2_p"][l, :, f * W2SZ:(f + 1) * W2SZ], w2)
            ps = cx.psum()
            mm_group(cx, ps[:], [(w2[:, k * 128:(k + 1) * 128], U[:, k, :]) for k in range(FC)], [w2, U], ps)
            STT(cx, "dve", xT[:, f, :], ps[:], M["G2"][:, f, ci:ci + 1], xT[:, f, :], ALU.mult, ALU.add, [ps, xT], [xT])
        if not last:
            STORE(cx, cx.dr["xres"][:, t0:t0 + 512].rearrange("(k p) n -> p k n", p=128), xT[:], xT)
        else:
            outs = []
            stg = [tmpring.next() for _ in range(4)]
            ps = cx.psum()
            ones = cx.C["ones"]
            for k in range(KC):
                sq = sqring.next()
                ACT(cx, sq[:], xT[:, k, :], AF.Square, [xT], [sq])
                X(cx, "pe", "matmul", ps[:], lhsT=ones[:], rhs=sq[:], start=(k == 0), stop=(k == KC - 1), reads=[sq, ones], writes=[ps])
            sd, rstd = small
            ACT(cx, sd[:], ps[:], AF.Sqrt, [ps], [sd], bias=cx.C["eps6"][:, 0:1], scale=1.0 / D)
            X(cx, "dve", "reciprocal", rstd[:], sd[:], reads=[sd], writes=[rstd])
            for k in range(KC):
                t = stg[k % 4]
                STT(cx, "dve", t[:], xT[:, k, :], cx.FNORM[:, k:k + 1], rstd[:], ALU.mult, ALU.mult, [xT, rstd], [t])
                STORE(cx, cx.out["yT"][k * 128:(k + 1) * 128, t0:t0 + 512], t[:], t)
    cx.phase_end()


def phase_attn(cx, l):
    TS, TT = cx.TS, cx.TT
    qn_d = cx.dr["qn_d"]
    cx.phase_begin()
    gq = cx.sb([128, 1]); gk = cx.sb([128, 1])
    LOAD(cx, gq[:], cx.inp["qgT"][l], gq)
    LOAD(cx, gk[:], cx.inp["kgT"][l], gk)
    cos = cx.sb([128, TS]); sin = cx.sb([128, TS]); rotP = cx.sb([128, 128])
    LOAD(cx, cos[:], cx.inp["ropecos"], cos)
    LOAD(cx, sin[:], cx.inp["ropesin"], sin)
    LOAD(cx, rotP[:], cx.inp["rotP"], rotP)
    rawr = cx.ring(2, [128, 5, 512], F32, "raw")
    sqr = cx.ring(2, [128, 512], F32, "sq")
    sdr = cx.ring(2, [128, 512], F32, "sd")
    qnr = cx.ring(2, [128, 512], F32, "qn")
    t1r = cx.ring(2, [128, 512], F32, "t1")
    obr = cx.ring(3, [128, 512], BF16, "ob")
    ktr = cx.ring(2, [128, 4, 128], F32, "kt")
    bones = cx.C["bones"]
    for ti in range(cx.NTILE):
        t0 = ti * 512
        samp = t0 < TS
        raw = rawr.next()
        LOAD(cx, raw[:], cx.dr["qkT"][:, t0:t0 + 512].rearrange("(c p) n -> p c n", p=128), raw)
        for c in range(5):
            g = gq if c < 4 else gk
            sq = sqr.next()
            ACT(cx, sq[:], raw[:, c, :], AF.Square, [raw], [sq])
            ps = cx.psum()
            X(cx, "pe", "matmul", ps[:], lhsT=bones[:], rhs=sq[:], start=True, stop=True, reads=[sq, bones], writes=[ps])
            sd = sdr.next()
            ACT(cx, sd[:], ps[:], AF.Sqrt, [ps], [sd], bias=cx.C["eps6"][:, 0:1], scale=1.0 / 64)
            X(cx, "dve", "reciprocal", sd[:], sd[:], reads=[sd], writes=[sd])
            qn = qnr.next()
            STT(cx, "dve", qn[:], raw[:, c, :], g[:, 0:1], sd[:], ALU.mult, ALU.mult, [raw, g, sd], [qn])
            ob = obr.next()
            if samp:
                ps2 = cx.psum()
                X(cx, "pe", "matmul", ps2[:], lhsT=rotP[:], rhs=qn[:], start=True, stop=True, reads=[qn, rotP], writes=[ps2])
                t1 = t1r.next()
                TT_(cx, "dve", t1[:], qn[:], cos[:, t0:t0 + 512], ALU.mult, [qn, cos], [t1])
                t2 = t1r.next()
                TT_(cx, "dve", t2[:], ps2[:], sin[:, t0:t0 + 512], ALU.mult, [ps2, sin], [t2])
                TT_(cx, "dve", ob[:], t1[:], t2[:], ALU.add, [t1, t2], [ob])
            else:
                ACT(cx, ob[:], qn[:], AF.Copy, [qn], [ob])
                if c == 4:
                    ps3 = cx.psum()
                    transposes(cx, [(ps3[:, s * 128:(s + 1) * 128], qn[:, s * 128:(s + 1) * 128]) for s in range(4)], [qn], ps3)
                    kt = ktr.next()
                    COPY(cx, cx.evac_eng(), kt[:].rearrange("p s c -> p (s c)"), ps3[:], [ps3], [kt])
                    for s in range(4):
                        tok = t0 + s * 128 - TS
                        p, tin = tok // cx.TP, tok % cx.TP
                        STORE(cx, cx.out["new_k"][p, l, tin:tin + 128, :], kt[:, s, :], kt)
            STORE(cx, qn_d[c * 128:(c + 1) * 128, t0:t0 + 512], ob[:], ob)
    for (st, T, samp, p) in cx.seqs:
        if not samp:
            cx.fw.dma("sp", cx.out["new_v"][p, l], cx.dr["av_tm"][st:st + T, :])
    cx.phase_end()
    cx.phase_begin()
    save_ps = cx.ps_n
    cx.ps_n = 6
    cx.ps_i = 0
    po_banks = [cx.ps[6], cx.ps[7]]
    po_i = 0
    Smax = TS + cx.PAST
    KTa = cx.sb([128, Smax], BF16, "KTa")
    KTb = cx.sb([128, Smax], BF16, "KTb")
    NCHmax = Smax // 128
    V1 = cx.sb([128, NCHmax, 2, 128], BF16, "V1")
    ckr = cx.ring(2, [128, 128], F32, "ck")
    qTr = cx.ring(2, [128, 4, 512], BF16, "qT")
    pTr = cx.ring(3, [128, 512], BF16, "pT")
    osr = cx.ring(2, [128, 4, 512], F32, "os")
    recr = cx.ring(2, [128, 4], F32, "rec")
    for (st, T, samp, p) in cx.seqs:
        S = T + (cx.PAST if samp else 0)
        NCH = S // 128
        NO = T // 128
        LOAD(cx, KTa[:, 0:T], qn_d[512:640, st:st + T], KTa)
        LOAD(cx, KTb[0:64, 0:T], qn_d[576:640, st:st + T], KTb)
        LOAD(cx, KTb[64:128, 0:T], qn_d[512:576, st:st + T], KTb)
        X(cx, "dve", "memset", V1[:, :, :, 64:66], 1.0, reads=[], writes=[V1])
        for kv_ in range(2):
            for c0_ in range(0, NO, 8):
                c1_ = min(NO, c0_ + 8)
                LOAD(cx, V1[:, c0_:c1_, kv_, 0:64],
                     cx.dr["av_tm"][st + c0_ * 128:st + c1_ * 128, kv_ * 64:(kv_ + 1) * 64].rearrange("(c p) d -> p c d", p=128), V1)
        if samp:
            NPC = cx.PAST // 128
            for kv_ in range(2):
                LOAD(cx, V1[:, NO:NO + NPC, kv_, 0:64], cx.inp["cache_v"][l][:, kv_ * 64:(kv_ + 1) * 64].rearrange("(c p) d -> p c d", p=128), V1)
            for j in range(NPC):
                for (KTx, swap) in ((KTa, False), (KTb, True)):
                    ck = ckr.next()
                    src = cx.inp["cache_k"][l, j * 128:(j + 1) * 128, :]
                    if not swap:
                        LOAD(cx, ck[:], src, ck)
                    else:
                        LOAD(cx, ck[:, 0:64], src[:, 64:128], ck)
                        LOAD(cx, ck[:, 64:128], src[:, 0:64], ck)
                    ps = cx.psum()
                    transposes(cx, [(ps[:, 0:128], ck[:])], [ck], ps)
                    COPY(cx, cx.evac_eng(), KTx[:, T + j * 128:T + (j + 1) * 128], ps[:, 0:128], [ps], [KTx])
        QN = min(512, T)
        ns = QN // 128
        for q0 in range(0, T, QN):
            qT = qTr.next()
            LOAD(cx, qT[:, :, 0:QN], qn_d[0:512, st + q0:st + q0 + QN].rearrange("(c p) n -> p c n", p=128), qT)
            osb = osr.next()
            for h in range(8):
                kv = h // 4
                base = (h % 2) * 64
                c = h // 2
                KT = KTa if (h % 2) == kv else KTb
                po = po_banks[po_i]
                po_i = 1 - po_i
                for ch in range(NCH):
                    ps = cx.psum()
                    X(cx, "pe", "matmul", ps[:, 0:QN], lhsT=KT[base:base + 64, ch * 128:(ch + 1) * 128], rhs=qT[base:base + 64, c, 0:QN],
                      start=True, stop=True, reads=[KT, qT], writes=[ps])
                    pT = pTr.next()
                    ACT(cx, pT[:, 0:QN], ps[:, 0:QN], AF.Exp, [ps], [pT], scale=0.125)
                    cx.fw.op("pe", _pv_fn(po, pT, V1, ch, kv, ns, ch == 0, ch == NCH - 1), reads=[pT, V1], writes=[po])
                pov = po[:, 0:ns * 128].rearrange("p (s c) -> p s c", c=128)
                rec = recr.next()
                X(cx, "dve", "reciprocal", rec[:, 0:ns], pov[:, :, 64], reads=[po], writes=[rec])
                TT_(cx, "dve", osb[:, 0:ns, h * 64:(h + 1) * 64], pov[:, :, 0:64], rec[:, 0:ns].unsqueeze(2).broadcast_to([128, ns, 64]),
                    ALU.mult, [po, rec], [osb])
            STORE(cx, cx.dr["o_att"][st + q0:st + q0 + QN, :].rearrange("(s p) c -> p s c", p=128), osb[:, 0:ns, :], osb)
    cx.ps_n = save_ps
    cx.ps_i = 0
    cx.phase_end()


def _pv_fn(po, pT, V1, ch, kv, ns, first, last):
    def fn(e):
        ins = None
        for s in range(ns):
            ins = e.matmul(po[:, s * 128:s * 128 + 65], lhsT=pT[:, s * 128:(s + 1) * 128], rhs=V1[:, ch, kv, 0:65], start=(first and s == 0), stop=last)
        return ins
    return fn


def _segments(cx, n=512):
    segs = []
    for (st, T, samp, p) in cx.seqs:
        for a in range(0, T, n):
            segs.append((st + a, min(n, T - a), st, st + T))
    return segs


def _load_halo(cx, tile_, nch, src, t0, n, s0, s1):
    lo = t0 - 1 if t0 > s0 else t0
    hi = t0 + n + 1 if t0 + n < s1 else t0 + n
    if lo == t0:
        X(cx, "dve", "memset", tile_[:, :, 0:1], 0.0, reads=[], writes=[tile_])
    if hi == t0 + n:
        X(cx, "dve", "memset", tile_[:, :, n + 1:n + 2], 0.0, reads=[], writes=[tile_])
    LOAD(cx, tile_[:, :, lo - (t0 - 1):hi - (t0 - 1)], src[:, lo:hi].rearrange("(c p) n -> p c n", p=128), tile_)


def phase_ssd(cx, l):
    TS, TT = cx.TS, cx.TT
    C = cx.C
    cx.phase_begin()
    cw = cx.sb([128, 6, 3]); cb = cx.sb([128, 6])
    LOAD(cx, cw[:], cx.inp["ssd_cwT"][l], cw)
    LOAD(cx, cb[:], cx.inp["ssd_cbT"][l], cb)
    dtb = cx.sb([128, 16]); nA = cx.sb([128, 16])
    LOAD(cx, dtb[:], cx.inp["ssd_dt_bias"][l:l + 1, :].partition_broadcast(128), dtb)
    LOAD(cx, nA[:], cx.inp["ssd_a_log"][l:l + 1, :].partition_broadcast(128), nA)
    ACT(cx, nA[:], nA[:], AF.Exp, [nA], [nA])
    TS_(cx, "dve", nA[:], nA[:], -1.0, None, ALU.mult, None, [nA], [nA])
    xbr = cx.ring(2, [128, 6, 514], F32, "xb")
    xcr = cx.ring(2, [128, 6, 512], F32, "xc")
    tr_ = cx.ring(2, [128, 512], F32, "t")
    xtr = cx.ring(2, [128, 640], F32, "xt")
    smr = cx.ring(6, [128, 16], F32, "sm")
    for (t0, n, s0, s1) in _segments(cx):
        xb = xbr.next()
        _load_halo(cx, xb, 6, cx.dr["xbcT"], t0, n, s0, s1)
        xc = xcr.next()
        for c in range(6):
            t = tr_.next()
            TS_(cx, "dve", t[:, 0:n], xb[:, c, 0:n], cw[:, c, 0:1], cb[:, c:c + 1], ALU.mult, ALU.add, [xb, cw, cb], [t])
            STT(cx, "dve", t[:, 0:n], xb[:, c, 1:n + 1], cw[:, c, 1:2], t[:, 0:n], ALU.mult, ALU.add, [xb, cw, t], [t])
            STT(cx, "dve", t[:, 0:n], xb[:, c, 2:n + 2], cw[:, c, 2:3], t[:, 0:n], ALU.mult, ALU.add, [xb, cw, t], [t])
            ACT(cx, xc[:, c, 0:n], t[:, 0:n], AF.Silu, [t], [xc])
        STORE(cx, cx.dr["bcT"][:, t0:t0 + n].rearrange("(c p) n -> p c n", p=128), xc[:, 4:6, 0:n], xc)
        for s in range(n // 128):
            psa = cx.psum()
            transposes(cx, [(psa[:, c * 128:(c + 1) * 128], xc[:, c, s * 128:(s + 1) * 128]) for c in range(4)], [xc], psa)
            psb = cx.psum()
            transposes(cx, [(psb[:, 0:128], xc[:, 4, s * 128:(s + 1) * 128])], [xc], psb)
            xt = xtr.next()
            COPY(cx, "act", xt[:, 0:512], psa[:], [psa], [xt])
            COPY(cx, "dve", xt[:, 512:640], psb[:, 0:128], [psb], [xt])
            STORE(cx, cx.dr["xs_tm"][t0 + s * 128:t0 + (s + 1) * 128, :], xt[:, 0:512], xt)
            STORE(cx, cx.dr["b_tm"][t0 + s * 128:t0 + (s + 1) * 128, :], xt[:, 512:640], xt)
            x0 = smr.next(); a0 = smr.next(); dt = smr.next()
            LOAD(cx, x0[:], cx.dr["sdt_tm"][t0 + s * 128:t0 + (s + 1) * 128, :], x0)
            TT_(cx, "dve", x0[:], x0[:], dtb[:], ALU.add, [x0, dtb], [x0])
            ACT(cx, a0[:], x0[:], AF.Abs, [x0], [a0])
            ACT(cx, a0[:], a0[:], AF.Exp, [a0], [a0], scale=-1.0)
            ACT(cx, a0[:], a0[:], AF.Ln, [a0], [a0], bias=C["one"][:, 0:1], scale=1.0)
            TS_(cx, "dve", x0[:], x0[:], 0.0, None, ALU.max, None, [x0], [x0])
            TT_(cx, "dve", dt[:], x0[:], a0[:], ALU.add, [x0, a0], [dt])
            TT_(cx, "dve", a0[:], dt[:], nA[:], ALU.mult, [dt, nA], [a0])
            STORE(cx, cx.dr["dt_tm"][t0 + s * 128:t0 + (s + 1) * 128, :], dt[:], dt)
            STORE(cx, cx.dr["ld_tm"][t0 + s * 128:t0 + (s + 1) * 128, :], a0[:], a0)
    cx.phase_end()
    import os
    if os.environ.get("SSD_STOP") == "1":
        return
    cx.phase_begin()
    S = [cx.sb([128, 256], F32, f"S{d}") for d in range(2)]
    bcr = cx.ring(3, [128, 2, 128], F32, "bc")
    btr = cx.ring(3, [128, 128], F32, "bt")
    xkr = cx.ring(3, [128, 512], F32, "xk")
    dlr = cx.ring(3, [128, 32], F32, "dl")
    rbr = cx.ring(2, [128, 8, 128], F32, "rb")
    gcr = cx.ring(2, [128, 48], F32, "gc")
    bsr = cx.ring(2, [128, 256], F32, "bs")
    der = cx.ring(3, [128, 128], F32, "de")
    atr = cx.ring(2, [128, 8, 128], F32, "at")
    tmr = cx.ring(4, [128, 512], F32, "tm")
    ones, ident = C["ones"], C["ident"]
    for (st, T, samp, p) in cx.seqs:
        NC = T // 128
        for d in range(2):
            if samp and not os.environ.get("SSD_NOSTATE"):
                for g in range(2):
                    LOAD(cx, S[d][g * 64:(g + 1) * 64, :].rearrange("n (h p) -> n h p", p=64),
                         cx.inp["state_ssd"][l, d, g * 4:(g + 1) * 4].rearrange("h n p -> n h p"), S[d])
            else:
                X(cx, "dve", "memset", S[d][:], 0.0, reads=[], writes=[S[d]])
        for step in range(NC):
            for d in range(2):
                ck = step if d == 0 else NC - 1 - step
                k0 = st + ck * 128
                CUM = C["cum128f"] if d == 0 else C["cum128b"]
                NEGM = C["negmf"] if d == 0 else C["negmb"]
                end = 127 if d == 0 else 0
                Sd = S[d]
                bc = bcr.next(); bt = btr.next(); xk = xkr.next(); dl = dlr.next()
                LOAD(cx, bc[:], cx.dr["bcT"][:, k0:k0 + 128].rearrange("(c p) n -> p c n", p=128), bc)
                LOAD(cx, bt[:], cx.dr["b_tm"][k0:k0 + 128, :], bt)
                LOAD(cx, xk[:], cx.dr["xs_tm"][k0:k0 + 128, :], xk)
                LOAD(cx, dl[:, 0:16], cx.dr["dt_tm"][k0:k0 + 128, :], dl)
                LOAD(cx, dl[:, 16:32], cx.dr["ld_tm"][k0:k0 + 128, :], dl)
                dtc = dl[:, d * 8:(d + 1) * 8]
                ldc = dl[:, 16 + d * 8:16 + (d + 1) * 8]
                rb = rbr.next()
                TT_(cx, "dve", rb[:], CUM[:].unsqueeze(1).broadcast_to([128, 8, 128]), ldc.unsqueeze(2).broadcast_to([128, 8, 128]),
                    ALU.mult, [CUM, dl], [rb])
                if int(os.environ.get("SSD_STEPS", 99)) <= 1:
                    continue
                pg = [cx.psum(), cx.psum()]
                for half in range(2):
                    mm_group(cx, pg[half][:], [(ones[:], rb[:, half * 4:(half + 1) * 4, :].rearrange("p h i -> p (h i)")), (ident[:], NEGM[:])],
                             [rb, ones, ident, NEGM], pg[half])
                if int(os.environ.get("SSD_STEPS", 99)) <= 2:
                    continue
                pc = cx.psum()
                X(cx, "pe", "matmul", pc[:, 0:8], lhsT=CUM[:], rhs=ldc, start=True, stop=True, reads=[CUM, dl], writes=[pc])
                gc = gcr.next()
                COPY(cx, "dve", gc[:, 0:8], pc[:, 0:8], [pc], [gc])
                TS_(cx, "dve", gc[:, 8:16], gc[:, 0:8], -1.0, None, ALU.mult, None, [gc], [gc])
                ACT(cx, gc[:, 16:24], gc[:, 0:8], AF.Exp, [gc], [gc])
                for half in range(2):
                    gend = pg[half][:, :].rearrange("p (h i) -> p h i", i=128)[:, :, end]
                    COPY(cx, "dve", gc[:, 24 + half * 4:28 + half * 4], gend, [pg[half]], [gc])
                ACT(cx, gc[:, 40:48], gc[:, 24:32], AF.Exp, [gc], [gc])
                TT_(cx, "dve", gc[:, 32:40], gc[:, 24:32], gc[:, 0:8], ALU.subtract, [gc], [gc])
                ACT(cx, gc[:, 32:40], gc[:, 32:40], AF.Exp, [gc], [gc])
                TT_(cx, "dve", gc[:, 32:40], gc[:, 32:40], dtc, ALU.mult, [gc, dl], [gc])
                if int(os.environ.get("SSD_STEPS", 99)) <= 3:
                    continue
                bs = bsr.next()
                for g in range(2):
                    pbc = cx.psum()
                    X(cx, "pe", "matmul", pbc[:, 0:128], lhsT=bc[g * 64:(g + 1) * 64, 0, :], rhs=bc[g * 64:(g + 1) * 64, 1, :],
                      start=True, stop=True, reads=[bc], writes=[pbc])
                    COPY(cx, "act", bs[:, g * 128:(g + 1) * 128], pbc[:, 0:128], [pbc], [bs])
                if int(os.environ.get("SSD_STEPS", 99)) <= 4:
                    continue
                at = atr.next()
                for h in range(8):
                    de = der.next()
                    ACT(cx, de[:], pg[h // 4][:, (h % 4) * 128:(h % 4 + 1) * 128], AF.Exp, [pg[h // 4], gc], [de], bias=gc[:, 8 + h:9 + h], scale=1.0)
                    STT(cx, "dve", at[:, h, :], de[:], dtc[:, h:h + 1], bs[:, (h // 4) * 128:(h // 4 + 1) * 128], ALU.mult, ALU.mult, [de, dl, bs], [at])
                if int(os.environ.get("SSD_STEPS", 99)) <= 5:
                    continue
                po = cx.psum()
                cx.fw.op("pe", _ssd_intra_fn(po, at, xk), reads=[at, xk], writes=[po])
                pis = []
                for g in range(2):
                    pi = cx.psum()
                    X(cx, "pe", "matmul", pi[:, 0:256], lhsT=bc[g * 64:(g + 1) * 64, 1, :], rhs=Sd[g * 64:(g + 1) * 64, :],
                      start=True, stop=True, reads=[bc, Sd], writes=[pi])
                    pis.append(pi)
                if int(os.environ.get("SSD_STEPS", 99)) <= 7:
                    continue
                t1 = tmr.next()
                for g in range(2):
                    TT_(cx, "dve", t1[:, g * 256:(g + 1) * 256].rearrange("p (h q) -> p h q", q=64), pis[g][:, 0:256].rearrange("p (h q) -> p h q", q=64),
                        gc[:, 16 + g * 4:20 + g * 4].unsqueeze(2).broadcast_to([128, 4, 64]), ALU.mult, [pis[g], gc], [t1])
                TT_(cx, "dve", t1[:], t1[:], po[:], ALU.add, [t1, po], [t1])
                STORE(cx, cx.dr["o_ssd_d"][d, k0:k0 + 128, :], t1[:], t1)
                if int(os.environ.get("SSD_STEPS", 99)) <= 8:
                    continue
                xs2 = tmr.next()
                TT_(cx, "dve", xs2[:].rearrange("p (h q) -> p h q", q=64), xk[:].rearrange("p (h q) -> p h q", q=64),
                    gc[:, 32:40].unsqueeze(2).broadcast_to([128, 8, 64]), ALU.mult, [xk, gc], [xs2])
                pst = cx.psum()
                X(cx, "pe", "matmul", pst[:], lhsT=bt[:], rhs=xs2[:], start=True, stop=True, reads=[bt, xs2], writes=[pst])
                for g in range(2):
                    Sg = Sd[g * 64:(g + 1) * 64, :]
                    TT_(cx, "dve", Sg.rearrange("n (h q) -> n h q", q=64), Sg.rearrange("n (h q) -> n h q", q=64),
                        gc[g * 64:(g + 1) * 64, 40 + g * 4:44 + g * 4].unsqueeze(2).broadcast_to([64, 4, 64]), ALU.mult, [Sd, gc], [Sd])
                    TT_(cx, "dve", Sg, Sg, pst[g * 64:(g + 1) * 64, g * 256:(g + 1) * 256], ALU.add, [Sd, pst], [Sd])
        if not samp and not os.environ.get("SSD_NOSTATE"):
            for d in range(2):
                for g in range(2):
                    STORE(cx, cx.out["new_ssd"][p, l, d, g * 4:(g + 1) * 4].rearrange("h n p -> n h p"),
                          S[d][g * 64:(g + 1) * 64, :].rearrange("n (h p) -> n h p", p=64), S[d])
    cx.phase_end()
    if os.environ.get("SSD_STOP") == "2":
        return
    cx.phase_begin()
    Db = cx.sb([128, 8]); nb = cx.sb([128, 512])
    LOAD(cx, Db[:], cx.inp["ssd_d"][l:l + 1, :].partition_broadcast(128), Db)
    LOAD(cx, nb[:], cx.inp["ssd_norm"][l:l + 1, :].partition_broadcast(128), nb)
    inr = cx.ring(8, [128, 512], F32, "in")
    ssr = cx.ring(4, [128, 1], F32, "ss")
    for b0 in range(0, TT, 128):
        of = inr.next(); ob = inr.next(); xs = inr.next(); sz = inr.next()
        LOAD(cx, of[:], cx.dr["o_ssd_d"][0, b0:b0 + 128, :], of)
        LOAD(cx, ob[:], cx.dr["o_ssd_d"][1, b0:b0 + 128, :], ob)
        LOAD(cx, xs[:], cx.dr["xs_tm"][b0:b0 + 128, :], xs)
        LOAD(cx, sz[:], cx.dr["sz_tm"][b0:b0 + 128, :], sz)
        TT_(cx, "dve", of[:], of[:], ob[:], ALU.add, [of, ob], [of])
        TT_(cx, "dve", xs[:].rearrange("p (h q) -> p h q", q=64), xs[:].rearrange("p (h q) -> p h q", q=64),
            Db[:].unsqueeze(2).broadcast_to([128, 8, 64]), ALU.mult, [xs, Db], [xs])
        TT_(cx, "dve", of[:], of[:], xs[:], ALU.add, [of, xs], [of])
        ACT(cx, sz[:], sz[:], AF.Silu, [sz], [sz])
        TT_(cx, "dve", of[:], of[:], sz[:], ALU.mult, [of, sz], [of])
        ss = ssr.next()
        ACT(cx, ob[:], of[:], AF.Square, [of], [ob, ss], accum_out=ss[:, 0:1])
        ACT(cx, ss[:], ss[:], AF.Sqrt, [ss], [ss], bias=C["eps6"][:, 0:1], scale=1.0 / 512)
        X(cx, "dve", "reciprocal", ss[:], ss[:], reads=[ss], writes=[ss])
        STT(cx, "dve", of[:], of[:], ss[:, 0:1], nb[:], ALU.mult, ALU.mult, [of, ss, nb], [of])
        STORE(cx, cx.dr["o_ssd"][b0:b0 + 128, :], of[:], of)
    cx.phase_end()


def _ssd_intra_fn(po, at, xk):
    def fn(e):
        ins = None
        for h in range(8):
            ins = e.matmul(po[:, h * 64:(h + 1) * 64], lhsT=at[:, h, :], rhs=xk[:, h * 64:(h + 1) * 64], start=True, stop=True)
        return ins
    return fn


def phase_gla(cx, l):
    TS, TT = cx.TS, cx.TT
    C = cx.C
    cx.phase_begin()
    g2 = cx.sb([16, 2, 256]); gb = cx.sb([128, 2, 256])
    LOAD(cx, g2[:], cx.inp["gla_g2"][l].rearrange("z r k -> r z k"), g2)
    for d in range(2):
        LOAD(cx, gb[:, d, :], cx.inp["gla_gb"][l, d:d + 1, :].partition_broadcast(128), gb)
    S = [[cx.sb([128, 128], F32, f"S{d}{pr}") for pr in range(2)] for d in range(2)]
    qkr = cx.ring(3, [128, 4, 128], F32, "qk")
    gkr = cx.ring(3, [128, 256], F32, "gk")
    gvr = cx.ring(3, [128, 512], F32, "gv")
    ggr = cx.ring(3, [16, 128], F32, "gg")
    t2r = cx.ring(8, [128, 256], F32, "t2")
    atr = cx.ring(2, [128, 4, 128], F32, "at")
    osr = cx.ring(2, [128, 512], F32, "os")
    for (st, T, samp, p) in cx.seqs:
        NC = T // 128
        for d in range(2):
            for pr in range(2):
                if samp:
                    LOAD(cx, S[d][pr][:], cx.inp["state_gla"][l, d, 2 * pr:2 * pr + 2].rearrange("h k v -> (h k) v"), S[d][pr])
                else:
                    X(cx, "dve", "memset", S[d][pr][:], 0.0, reads=[], writes=[S[d][pr]])
        for step in range(NC):
            for d in range(2):
                ck = step if d == 0 else NC - 1 - step
                k0 = st + ck * 128
                CUM = C["cum128f"] if d == 0 else C["cum128b"]
                SFX = C["sfx128f"] if d == 0 else C["sfx128b"]
                end = 127 if d == 0 else 0
                qk = qkr.next(); gk = gkr.next(); gv = gvr.next(); gg = ggr.next()
                LOAD(cx, qk[:], cx.dr["gqkT"][:, k0:k0 + 128].rearrange("(c p) n -> p c n", p=128), qk)
                LOAD(cx, gk[:], cx.dr["gk_tm"][k0:k0 + 128, :], gk)
                LOAD(cx, gv[:], cx.dr["gv_tm"][k0:k0 + 128, :], gv)
                LOAD(cx, gg[:], cx.dr["gglT"][:, k0:k0 + 128], gg)
                pl = cx.psum()
                X(cx, "pe", "matmul", pl[:, 0:256], lhsT=gg[:], rhs=g2[:, d, :], start=True, stop=True, reads=[gg, g2], writes=[pl])
                lg = t2r.next(); ab = t2r.next()
                TT_(cx, "dve", lg[:], pl[:, 0:256], gb[:, d, :], ALU.add, [pl, gb], [lg])
                ACT(cx, ab[:], lg[:], AF.Abs, [lg], [ab])
                ACT(cx, ab[:], ab[:], AF.Exp, [ab], [ab], scale=-1.0)
                ACT(cx, ab[:], ab[:], AF.Ln, [ab], [ab], bias=C["one"][:, 0:1], scale=1.0)
                TS_(cx, "dve", lg[:], lg[:], 0.0, None, ALU.min, None, [lg], [lg])
                TT_(cx, "dve", lg[:], lg[:], ab[:], ALU.subtract, [lg, ab], [lg])
                TS_(cx, "dve", lg[:], lg[:], 1.0 / 16.0, None, ALU.mult, None, [lg], [lg])
                pg = cx.psum()
                for pr in range(2):
                    X(cx, "pe", "matmul", pg[:, pr * 128:(pr + 1) * 128], lhsT=lg[:, pr * 128:(pr + 1) * 128], rhs=CUM[:], start=True, stop=True,
                      reads=[lg, CUM], writes=[pg])
                E = t2r.next(); Ei = t2r.next()
                ACT(cx, E[:], pg[:, 0:256], AF.Exp, [pg], [E])
                ACT(cx, Ei[:], pg[:, 0:256], AF.Exp, [pg], [Ei], scale=-1.0)
                qt = t2r.next(); kt = t2r.next()
                STT(cx, "dve", qt[:], qk[:, 0:2, :].rearrange("p c n -> p (c n)"), 0.125, E[:], ALU.mult, ALU.mult, [qk, E], [qt])
                TT_(cx, "dve", kt[:], qk[:, 2:4, :].rearrange("p c n -> p (c n)"), Ei[:], ALU.mult, [qk, Ei], [kt])
                pd = cx.psum()
                X(cx, "pe", "matmul", pd[:, 0:256], lhsT=SFX[:], rhs=lg[:], start=True, stop=True, reads=[SFX, lg], writes=[pd])
                kd = t2r.next()
                ACT(cx, kd[:], pd[:, 0:256], AF.Exp, [pd], [kd])
                TT_(cx, "dve", kd[:], kd[:], gk[:], ALU.mult, [kd, gk], [kd])
                at = atr.next()
                po = cx.psum()
                for h in range(4):
                    pr, base = h // 2, (h % 2) * 64
                    pa = cx.psum()
                    X(cx, "pe", "matmul", pa[:, 0:128], lhsT=kt[base:base + 64, pr * 128:(pr + 1) * 128], rhs=qt[base:base + 64, pr * 128:(pr + 1) * 128],
                      start=True, stop=True, reads=[kt, qt], writes=[pa])
                    TT_(cx, "dve", at[:, h, :], pa[:, 0:128], CUM[:], ALU.mult, [pa, CUM], [at])
                    mm_group(cx, po[:, h * 128:(h + 1) * 128],
                             [(at[:, h, :], gv[:, h * 128:(h + 1) * 128]),
                              (qt[base:base + 64, pr * 128:(pr + 1) * 128], S[d][pr][base:base + 64, :])], [at, gv, qt, S[d][pr]], po)
                osb = osr.next()
                COPY(cx, "act", osb[:], po[:], [po], [osb])
                STORE(cx, cx.dr["o_gla_d"][d, k0:k0 + 128, :], osb[:], osb)
                pst = cx.psum()
                for pr in range(2):
                    for hl in range(2):
                        h = 2 * pr + hl
                        X(cx, "pe", "matmul", pst[:, h * 128:(h + 1) * 128], lhsT=kd[:, pr * 128:(pr + 1) * 128], rhs=gv[:, h * 128:(h + 1) * 128],
                          start=True, stop=True, reads=[kd, gv], writes=[pst])
                for pr in range(2):
                    for hl in range(2):
                        h = 2 * pr + hl
                        rows = slice(hl * 64, (hl + 1) * 64)
                        STT(cx, "dve", S[d][pr][rows, :], S[d][pr][rows, :], E[rows, pr * 128 + end:pr * 128 + end + 1], pst[rows, h * 128:(h + 1) * 128],
                            ALU.mult, ALU.add, [S[d][pr], E, pst], [S[d][pr]])
        if not samp:
            for d in range(2):
                for pr in range(2):
                    STORE(cx, cx.out["new_gla"][p, l, d, 2 * pr:2 * pr + 2].rearrange("h k v -> (h k) v"), S[d][pr][:], S[d][pr])
    cx.phase_end()
    cx.phase_begin()
    nb = cx.sb([128, 128])
    LOAD(cx, nb[:], cx.inp["gla_norm"][l:l + 1, :].partition_broadcast(128), nb)
    inr = cx.ring(8, [128, 512], F32, "in")
    ssr = cx.ring(4, [128, 4], F32, "ss")
    for b0 in range(0, TT, 128):
        of = inr.next(); ob = inr.next(); go = inr.next()
        LOAD(cx, of[:], cx.dr["o_gla_d"][0, b0:b0 + 128, :], of)
        LOAD(cx, ob[:], cx.dr["o_gla_d"][1, b0:b0 + 128, :], ob)
        LOAD(cx, go[:], cx.dr["gog_tm"][b0:b0 + 128, :], go)
        TT_(cx, "dve", of[:], of[:], ob[:], ALU.add, [of, ob], [of])
        ACT(cx, ob[:], of[:], AF.Square, [of], [ob])
        ss = ssr.next()
        X(cx, "dve", "tensor_reduce", ss[:], ob[:].rearrange("p (h q) -> p h q", q=128), AX.X, ALU.add, reads=[ob], writes=[ss])
        ACT(cx, ss[:], ss[:], AF.Sqrt, [ss], [ss], bias=C["eps6"][:, 0:1], scale=1.0 / 128)
        X(cx, "dve", "reciprocal", ss[:], ss[:], reads=[ss], writes=[ss])
        o3 = of[:].rearrange("p (h q) -> p h q", q=128)
        TT_(cx, "dve", o3, o3, ss[:].unsqueeze(2).broadcast_to([128, 4, 128]), ALU.mult, [of, ss], [of])
        TT_(cx, "dve", o3, o3, nb[:].unsqueeze(1).broadcast_to([128, 4, 128]), ALU.mult, [of, nb], [of])
        ACT(cx, go[:], go[:], AF.Silu, [go], [go])
        TT_(cx, "dve", of[:], of[:], go[:], ALU.mult, [of, go], [of])
        STORE(cx, cx.dr["o_gla"][b0:b0 + 128, :], of[:], of)
    cx.phase_end()


def phase_rwkv(cx, l):
    TS, TT = cx.TS, cx.TT
    C = cx.C
    fw = cx.fw
    cx.phase_begin()
    mu = cx.sb([128, 14]); om = cx.sb([128, 14]); hm = cx.sb([128, 14])
    LOAD(cx, mu[:], cx.inp["rw_muT"][l], mu)
    TS_(cx, "dve", om[:], mu[:], -1.0, 1.0, ALU.mult, ALU.add, [mu], [om])
    TS_(cx, "dve", hm[:], mu[:], 0.5, None, ALU.mult, None, [mu], [hm])
    pp = cx.sb([128, 5, 4])
    for i, nm in enumerate(("rw_a0T", "rw_kkT", "rw_kaT", "rw_rkT")):
        LOAD(cx, pp[:, i, :], cx.inp[nm][l], pp)
    TS_(cx, "dve", pp[:, 4, :], pp[:, 2, :], -1.0, 1.0, ALU.mult, ALU.add, [pp], [pp])
    w0b = cx.sb([128, 2, 512])
    for d in range(2):
        LOAD(cx, w0b[:, d, :], cx.inp["rw_w0"][l, d:d + 1, :].partition_broadcast(128), w0b)
    w2s = cx.sb([64, 2, 512])
    LOAD(cx, w2s[:], cx.inp["rw_w2"][l].rearrange("z r c -> r z c"), w2s)
    a2s = cx.sb([128, 512])
    LOAD(cx, a2s[64:128, :], cx.inp["rw_a2"][l], a2s)
    g2s = cx.sb([128, 512])
    LOAD(cx, g2s[:], cx.inp["rw_g2"][l], g2s)
    rbr = cx.ring(1, [128, 14, 514], F32, "rb")
    rs = cx.sb([128, 14, 512], F32, "rs")
    t5r = cx.ring(3, [128, 512], F32, "t5")
    aT = cx.sb([128, 4, 512], F32, "aT")
    kkn = cx.sb([128, 4, 512], F32, "kkn")
    kp = cx.sb([128, 4, 512], F32, "kp")
    bet = cx.sb([128, 4, 512], F32, "bet")
    big = cx.ring(2, [128, 4, 512], F32, "big")
    twl = cx.sb([64, 512], F32, "twl")
    sgl = cx.sb([128, 512], F32, "sgl")
    o5r = cx.ring(4, [128, 512], F32, "o5")
    s8r = cx.ring(2, [128, 8], F32, "s8")
    bones = C["bones"]
    for (t0, n, s0, s1) in _segments(cx):
        rb = rbr.next()
        _load_halo(cx, rb, 14, cx.dr["rblkT"], t0, n, s0, s1)
        for c in range(14):
            t = t5r.next()
            TT_(cx, "dve", t[:, 0:n], rb[:, c, 0:n], rb[:, c, 2:n + 2], ALU.add, [rb], [t])
            TS_(cx, "dve", rs[:, c, 0:n], rb[:, c, 1:n + 1], om[:, c:c + 1], None, ALU.mult, None, [rb, om], [rs])
            STT(cx, "dve", rs[:, c, 0:n], t[:, 0:n], hm[:, c:c + 1], rs[:, c, 0:n], ALU.mult, ALU.add, [t, hm, rs], [rs])
        for c4 in range(4):
            ps = cx.psum()
            X(cx, "pe", "matmul", ps[:, 0:n], lhsT=a2s[64:128, c4 * 128:(c4 + 1) * 128], rhs=rs[64:128, 12, 0:n], start=True, stop=True,
              reads=[a2s, rs], writes=[ps])
            ACT(cx, aT[:, c4, 0:n], ps[:, 0:n], AF.Sigmoid, [ps, pp], [aT], bias=pp[:, 0, c4:c4 + 1], scale=1.0)
        kr = big.next()
        TT_(cx, "dve", kr[:, :, 0:n], rs[:, 4:8, 0:n], pp[:, 1, :].unsqueeze(2).broadcast_to([128, 4, n]), ALU.mult, [rs, pp], [kr])
        for c4 in range(4):
            sq = t5r.next()
            ACT(cx, sq[:, 0:n], kr[:, c4, 0:n], AF.Square, [kr], [sq])
            ps = cx.psum()
            X(cx, "pe", "matmul", ps[:, 0:n], lhsT=bones[:], rhs=sq[:, 0:n], start=True, stop=True, reads=[sq, bones], writes=[ps])
            sd = t5r.next()
            ACT(cx, sd[:, 0:n], ps[:, 0:n], AF.Sqrt, [ps], [sd], bias=C["eps12"][:, 0:1], scale=1.0)
            X(cx, "dve", "reciprocal", sd[:, 0:n], sd[:, 0:n], reads=[sd], writes=[sd])
            TT_(cx, "dve", kkn[:, c4, 0:n], kr[:, c4, 0:n], sd[:, 0:n], ALU.mult, [kr, sd], [kkn])
        tb = big.next()
        TT_(cx, "dve", tb[:, :, 0:n], aT[:, :, 0:n], pp[:, 2, :].unsqueeze(2).broadcast_to([128, 4, n]), ALU.mult, [aT, pp], [tb])
        TT_(cx, "dve", tb[:, :, 0:n], tb[:, :, 0:n], pp[:, 4, :].unsqueeze(2).broadcast_to([128, 4, n]), ALU.add, [tb, pp], [tb])
        TT_(cx, "dve", kp[:, :, 0:n], rs[:, 4:8, 0:n], tb[:, :, 0:n], ALU.mult, [rs, tb], [kp])
        TT_(cx, "dve", bet[:, :, 0:n], kkn[:, :, 0:n], aT[:, :, 0:n], ALU.mult, [kkn, aT], [bet])
        for q, src in enumerate((None, kp, kkn, bet)):
            sap = rs[:, 0:4, 0:n] if src is None else src[:, :, 0:n]
            STORE(cx, cx.dr["rw_fm"][q, :, t0:t0 + n].rearrange("(c p) n -> p c n", p=128), sap, rs if src is None else src)
        pr_ = big.next()
        TT_(cx, "dve", pr_[:, :, 0:n], rs[:, 0:4, 0:n], kp[:, :, 0:n], ALU.mult, [rs, kp], [pr_])
        TT_(cx, "dve", pr_[:, :, 0:n], pr_[:, :, 0:n], pp[:, 3, :].unsqueeze(2).broadcast_to([128, 4, n]), ALU.mult, [pr_, pp], [pr_])
        ACT(cx, twl[:, 0:n], rs[0:64, 12, 0:n], AF.Tanh, [rs], [twl])
        ACT(cx, sgl[:, 0:n], rs[:, 13, 0:n], AF.Sigmoid, [rs], [sgl])
        for s in range(n // 128):
            b0 = t0 + s * 128
            sl = slice(s * 128, (s + 1) * 128)
            ps = cx.psum()
            for c4 in range(4):
                X(cx, "pe", "matmul", ps[:, c4 * 2:(c4 + 1) * 2], lhsT=pr_[:, c4, sl], rhs=C["ind2"][:], start=True, stop=True,
                  reads=[pr_, C["ind2"]], writes=[ps])
            s8 = s8r.next()
            COPY(cx, "dve", s8[:], ps[:, 0:8], [ps], [s8])
            STORE(cx, cx.dr["rw_s_tm"][b0:b0 + 128, :], s8[:], s8)
            pv = cx.psum()
            transposes(cx, [(pv[:, c4 * 128:(c4 + 1) * 128], rs[:, 8 + c4, sl]) for c4 in range(4)], [rs], pv)
            o5 = o5r.next()
            COPY(cx, "act", o5[:], pv[:], [pv], [o5])
            STORE(cx, cx.dr["rw_v_tm"][b0:b0 + 128, :], o5[:], o5)
            pgm = cx.psum()
            X(cx, "pe", "matmul", pgm[:], lhsT=sgl[:, sl], rhs=g2s[:], start=True, stop=True, reads=[sgl, g2s], writes=[pgm])
            o5 = o5r.next()
            COPY(cx, "act", o5[:], pgm[:], [pgm], [o5])
            STORE(cx, cx.dr["rw_g_tm"][b0:b0 + 128, :], o5[:], o5)
            for d in range(2):
                pw = cx.psum()
                X(cx, "pe", "matmul", pw[:], lhsT=twl[:, sl], rhs=w2s[:, d, :], start=True, stop=True, reads=[twl, w2s], writes=[pw])
                o5 = o5r.next()
                TT_(cx, "dve", o5[:], pw[:], w0b[:, d, :], ALU.add, [pw, w0b], [o5])
                ACT(cx, o5[:], o5[:], AF.Sigmoid, [o5], [o5])
                TS_(cx, "dve", o5[:], o5[:], -0.6065306597126334, None, ALU.mult, None, [o5], [o5])
                STORE(cx, cx.dr["rw_lw_tm"][d, b0:b0 + 128, :], o5[:], o5)
    cx.phase_end()
    cx.phase_begin()
    NU = 8
    S0 = [[cx.sb([128, 64], F32, f"S{d}{pr}") for pr in range(4)] for d in range(2)]
    U = []
    for u in range(NU):
        t = {}
        for nm, shp in (("AR", [128, 256]), ("Bb", [128, 128]), ("Kb", [128, 128]), ("A1", [128, 256]), ("A2", [128, 256]), ("Mq", [128, 128]),
                        ("PQ0", [128, 256]), ("PQ1", [128, 256]), ("X0", [128, 64]), ("X1", [128, 64]), ("EE", [128, 128]), ("E2", [128, 64]),
                        ("KT", [128, 128]), ("BT", [128, 128]), ("tmp", [128, 64])):
            t[nm] = cx.sb(shp, F32, nm)
        for nm in ("AR", "Bb", "Kb"):
            X(cx, "dve", "memset", t[nm][:], 0.0, reads=[], writes=[t[nm]])
        U.append(t)
    x4r = [cx.ring(2, [128, 4, 4, 64], F32, "x4") for d in range(2)]
    lwr = [cx.ring(2, [64, 512], F32, "lw") for d in range(2)]
    vsr = [cx.ring(2, [128, 4, 64], F32, "vs") for d in range(2)]
    osr = [cx.ring(2, [128, 4, 64], F32, "os") for d in range(2)]
    ident = C["ident"]

    def unit(d, pr, T_, x4, lw, vs, osb, end):
        CUMc = C["cum64f"] if d == 0 else C["cum64b"]
        MSK = C["msk64f"] if d == 0 else C["msk64b"]
        MST = C["mst64f"] if d == 0 else C["mst64b"]
        S = S0[d][pr]
        AR, Bb, Kb = T_["AR"], T_["Bb"], T_["Kb"]
        EE, E2 = T_["EE"], T_["E2"]
        pG = cx.psum()
        X(cx, "pe", "matmul", pG[:, 0:128], lhsT=lw[:, pr * 128:(pr + 1) * 128], rhs=CUMc[:], start=True, stop=True, reads=[lw, CUMc], writes=[pG])
        ACT(cx, EE[:], pG[:, 0:128], AF.Exp, [pG], [EE])
        ACT(cx, E2[:], pG[:, 0:64], AF.Exp, [pG], [E2], scale=-1.0)
        yield
        for hl in range(2):
            r_ = slice(hl * 64, (hl + 1) * 64)
            c_ = slice(hl * 64, (hl + 1) * 64)
            c2 = slice(128 + hl * 64, 128 + (hl + 1) * 64)
            STT(cx, "dve", AR[r_, c_], x4[r_, 2, pr, :], -1.0, EE[r_, 64:128], ALU.mult, ALU.mult, [x4, EE], [AR])
            TT_(cx, "dve", AR[r_, c2], x4[r_, 0, pr, :], EE[r_, 0:64], ALU.mult, [x4, EE], [AR])
            TT_(cx, "dve", Bb[r_, c_], x4[r_, 3, pr, :], E2[r_, :], ALU.mult, [x4, E2], [Bb])
            TT_(cx, "dve", Kb[r_, c_], x4[r_, 1, pr, :], E2[r_, :], ALU.mult, [x4, E2], [Kb])
        yield
        p1 = cx.psum()
        X(cx, "pe", "matmul", p1[:, 0:256], lhsT=Bb[:], rhs=AR[:], start=True, stop=True, reads=[Bb, AR], writes=[p1])
        p2 = cx.psum()
        X(cx, "pe", "matmul", p2[:, 0:256], lhsT=Kb[:], rhs=AR[:], start=True, stop=True, reads=[Kb, AR], writes=[p2])
        p3 = cx.psum()
        X(cx, "pe", "matmul", p3[:, 0:128], lhsT=AR[:, 0:128], rhs=Bb[:], start=True, stop=True, reads=[Bb, AR], writes=[p3])
        A1, A2, Mq = T_["A1"], T_["A2"], T_["Mq"]
        TT_(cx, "dve", A1[:], p1[:, 0:256], MSK[:], ALU.mult, [p1, MSK], [A1])
        TT_(cx, "dve", A2[:], p2[:, 0:256], MSK[:], ALU.mult, [p2, MSK], [A2])
        TT_(cx, "dve", Mq[:], p3[:, 0:128], MST[:], ALU.mult, [p3, MST], [Mq])
        yield
        pr_ = cx.psum()
        mm_group(cx, pr_[:, 0:64], [(AR[:, 0:128], S[:]), (A2[:, 0:128], vs[:, pr, :])], [AR, S, A2, vs], pr_)
        Xc = T_["X0"]
        COPY(cx, "act", Xc[:], pr_[:, 0:64], [pr_], [Xc])
        yield
        P_ap, Q_ap = A1[:, 0:128], Mq[:]
        P_t, Q_t = A1, Mq
        PQ = [T_["PQ0"], T_["PQ1"]]
        Xn = T_["X1"]
        for lev in range(6):
            px = cx.psum()
            mm_group(cx, px[:, 0:64], [(ident[:], Xc[:]), (P_ap, Xc[:])], [ident, Xc, P_t], px)
            COPY(cx, "act", Xn[:], px[:, 0:64], [px], [Xn])
            Xc, Xn = Xn, Xc
            if lev < 5:
                pp_ = cx.psum()
                X(cx, "pe", "matmul", pp_[:, 0:128], lhsT=Q_ap, rhs=P_ap, start=True, stop=True, reads=[P_t, Q_t], writes=[pp_])
                X(cx, "pe", "matmul", pp_[:, 128:256], lhsT=P_ap, rhs=Q_ap, start=True, stop=True, reads=[P_t, Q_t], writes=[pp_])
                nt = PQ[lev % 2]
                COPY(cx, "dve", nt[:], pp_[:, 0:256], [pp_], [nt])
                P_ap, Q_ap = nt[:, 0:128], nt[:, 128:256]
                P_t = Q_t = nt
            yield
        Uc = Xc
        po = cx.psum()
        mm_group(cx, po[:, 0:64], [(AR[:, 128:256], S[:]), (A2[:, 128:256], vs[:, pr, :]), (A1[:, 128:256], Uc[:])], [AR, S, A2, vs, A1, Uc], po)
        COPY(cx, "act", osb[:, pr, :], po[:, 0:64], [po], [osb])
        yield
        KT, BT = T_["KT"], T_["BT"]
        pk = cx.psum()
        transposes(cx, [(pk[:, 0:128], Kb[:]), (pk[:, 128:256], Bb[:])], [Kb, Bb], pk)
        COPY(cx, "act", KT[:], pk[:, 0:128], [pk], [KT])
        COPY(cx, "act", BT[:], pk[:, 128:256], [pk], [BT])
        pst = cx.psum()
        mm_group(cx, pst[:, 0:64], [(KT[:], vs[:, pr, :]), (BT[:], Uc[:])], [KT, BT, vs, Uc], pst)
        tmp = T_["tmp"]
        TT_(cx, "dve", tmp[:], pst[:, 0:64], S[:], ALU.add, [pst, S], [tmp])
        TS_(cx, "dve", S[:], tmp[:], EE[:, end:end + 1], None, ALU.mult, None, [tmp, EE], [S])
        yield

    for (st, T, samp, p) in cx.seqs:
        NC = T // 64
        for d in range(2):
            for pr in range(4):
                if samp:
                    LOAD(cx, S0[d][pr][:], cx.inp["state_rwkvT"][l, d, 2 * pr:2 * pr + 2].rearrange("h k v -> (h k) v"), S0[d][pr])
                else:
                    X(cx, "dve", "memset", S0[d][pr][:], 0.0, reads=[], writes=[S0[d][pr]])
        for step in range(NC):
            gens = []
            stores = []
            for d in range(2):
                ck = step if d == 0 else NC - 1 - step
                k0 = st + ck * 64
                end = 63 if d == 0 else 0
                x4 = x4r[d].next(); lw = lwr[d].next(); vs = vsr[d].next(); osb = osr[d].next()
                for q_ in range(4):
                    LOAD(cx, x4[:, q_, :, :], cx.dr["rw_fm"][q_, :, k0:k0 + 64].rearrange("(c p) n -> p c n", p=128), x4)
                LOAD(cx, lw[:], cx.dr["rw_lw_tm"][d, k0:k0 + 64, :], lw)
                for hl in range(2):
                    LOAD(cx, vs[hl * 64:(hl + 1) * 64, :, :],
                         cx.dr["rw_v_tm"][k0:k0 + 64, :].rearrange("t (pr hl v) -> t hl pr v", hl=2, v=64)[:, hl], vs)
                for pr in range(4):
                    gens.append(unit(d, pr, U[d * 4 + pr], x4, lw, vs, osb, end))
                stores.append((d, k0, osb))
            while gens:
                for g in list(gens):
                    try:
                        next(g)
                    except StopIteration:
                        gens.remove(g)
            for (d, k0, osb) in stores:
                for hl in range(2):
                    STORE(cx, cx.dr["o_rw_d"][d, k0:k0 + 64, :].rearrange("t (pr hl v) -> t hl pr v", hl=2, v=64)[:, hl],
                          osb[hl * 64:(hl + 1) * 64, :, :], osb)
        if not samp:
            for d in range(2):
                for pr in range(4):
                    STORE(cx, cx.out["new_rwkvT"][p, l, d, 2 * pr:2 * pr + 2].rearrange("h k v -> (h k) v"), S0[d][pr][:], S0[d][pr])
    cx.phase_end()
    cx.phase_begin()
    lg = cx.sb([128, 512]); lb = cx.sb([128, 512])
    LOAD(cx, lg[:], cx.inp["rw_ln_g"][l:l + 1, :].partition_broadcast(128), lg)
    LOAD(cx, lb[:], cx.inp["rw_ln_b"][l:l + 1, :].partition_broadcast(128), lb)
    inr = cx.ring(10, [128, 512], F32, "in")
    ssr = cx.ring(6, [128, 8], F32, "ss")
    for b0 in range(0, TT, 128):
        of = inr.next(); ob = inr.next(); vt = inr.next(); gt = inr.next(); sq = inr.next()
        s8 = ssr.next(); mean = ssr.next(); var = ssr.next()
        LOAD(cx, of[:], cx.dr["o_rw_d"][0, b0:b0 + 128, :], of)
        LOAD(cx, ob[:], cx.dr["o_rw_d"][1, b0:b0 + 128, :], ob)
        LOAD(cx, vt[:], cx.dr["rw_v_tm"][b0:b0 + 128, :], vt)
        LOAD(cx, gt[:], cx.dr["rw_g_tm"][b0:b0 + 128, :], gt)
        LOAD(cx, s8[:], cx.dr["rw_s_tm"][b0:b0 + 128, :], s8)
        TT_(cx, "dve", of[:], of[:], ob[:], ALU.add, [of, ob], [of])
        o3 = of[:].rearrange("p (h q) -> p h q", q=64)
        X(cx, "dve", "tensor_reduce", mean[:], o3, AX.X, ALU.add, reads=[of], writes=[mean])
        TS_(cx, "dve", mean[:], mean[:], 1.0 / 64, None, ALU.mult, None, [mean], [mean])
        TT_(cx, "dve", o3, o3, mean[:].unsqueeze(2).broadcast_to([128, 8, 64]), ALU.subtract, [of, mean], [of])
        ACT(cx, sq[:], of[:], AF.Square, [of], [sq])
        X(cx, "dve", "tensor_reduce", var[:], sq[:].rearrange("p (h q) -> p h q", q=64), AX.X, ALU.add, reads=[sq], writes=[var])
        ACT(cx, var[:], var[:], AF.Sqrt, [var], [var], bias=C["epsln"][:, 0:1], scale=1.0 / 64)
        X(cx, "dve", "reciprocal", var[:], var[:], reads=[var], writes=[var])
        TT_(cx, "dve", o3, o3, var[:].unsqueeze(2).broadcast_to([128, 8, 64]), ALU.mult, [of, var], [of])
        TT_(cx, "dve", of[:], of[:], lg[:], ALU.mult, [of, lg], [of])
        TT_(cx, "dve", of[:], of[:], lb[:], ALU.add, [of, lb], [of])
        v3 = vt[:].rearrange("p (h q) -> p h q", q=64)
        TT_(cx, "dve", v3, v3, s8[:].unsqueeze(2).broadcast_to([128, 8, 64]), ALU.mult, [vt, s8], [vt])
        TT_(cx, "dve", of[:], of[:], vt[:], ALU.add, [of, vt], [of])
        TT_(cx, "dve", of[:], of[:], gt[:], ALU.mult, [of, gt], [of])
        STORE(cx, cx.dr["o_rwkv"][b0:b0 + 128, :], of[:], of)
    cx.phase_end()


def run_mixers(cx, l):
    phase_attn(cx, l)
    phase_ssd(cx, l)
    phase_gla(cx, l)
    phase_rwkv(cx, l)


CONST_SHAPES = {"ident": [128, 128], "ones": [128, 128], "bones": [128, 128], "eps6": [128, 1], "one": [128, 1], "eps12": [128, 1],
                "epsln": [128, 1], "ind2": [128, 2], "cum128f": [128, 128], "cum128b": [128, 128], "sfx128f": [128, 128], "sfx128b": [128, 128],
                "negmf": [128, 512], "negmb": [128, 512], "cum64f": [64, 128], "cum64b": [64, 128], "msk64f": [128, 256], "msk64b": [128, 256],
                "mst64f": [128, 128], "mst64b": [128, 128]}


def host_consts():
    f32 = np.float32
    d = {}
    d["ident"] = np.eye(128, dtype=f32)
    d["ones"] = np.ones((128, 128), f32)
    bo = np.zeros((128, 128), f32)
    bo[:64, :64] = 1
    bo[64:, 64:] = 1
    d["bones"] = bo
    d["eps6"] = np.full((128, 1), 1e-6, f32)
    d["one"] = np.full((128, 1), 1.0, f32)
    d["eps12"] = np.full((128, 1), 1e-12, f32)
    d["epsln"] = np.full((128, 1), 64e-5, f32)
    ind = np.zeros((128, 2), f32)
    ind[:64, 0] = 1
    ind[64:, 1] = 1
    d["ind2"] = ind
    a = np.arange(128)
    le = (a[:, None] <= a[None, :]).astype(f32)
    ge = (a[:, None] >= a[None, :]).astype(f32)
    gt = (a[:, None] > a[None, :]).astype(f32)
    lt = (a[:, None] < a[None, :]).astype(f32)
    d["cum128f"], d["cum128b"] = le, ge
    d["sfx128f"], d["sfx128b"] = gt, lt
    d["negmf"] = np.tile((le - 1.0) * 1e30, (1, 4)).astype(f32)
    d["negmb"] = np.tile((ge - 1.0) * 1e30, (1, 4)).astype(f32)
    b = np.arange(64)
    le6 = (b[:, None] <= b[None, :]).astype(f32)
    ge6 = (b[:, None] >= b[None, :]).astype(f32)
    gt6 = (b[:, None] > b[None, :]).astype(f32)
    lt6 = (b[:, None] < b[None, :]).astype(f32)
    d["cum64f"] = np.concatenate([le6, lt6], axis=1)
    d["cum64b"] = np.concatenate([ge6, gt6], axis=1)
    t22 = lambda m: np.tile(m, (2, 2))
    d["msk64f"] = np.concatenate([t22(lt6), t22(le6)], axis=1)
    d["msk64b"] = np.concatenate([t22(gt6), t22(ge6)], axis=1)
    d["mst64f"] = t22(gt6)
    d["mst64b"] = t22(lt6)
    return {k: np.ascontiguousarray(v, dtype=f32) for k, v in d.items()}


def rope_tables(TS):
    f32 = np.float32
    freqs = (10000.0 ** (-np.arange(16, dtype=f32) / f32(16))).astype(f32)
    t = np.arange(TS)
    rows = (t // 64).astype(f32)
    cols = (t % 64).astype(f32)
    cos = np.zeros((128, TS), f32)
    sin = np.zeros((128, TS), f32)
    rot = np.zeros((128, 128), f32)
    for p in range(128):
        dd = p % 64
        pos = rows if dd < 32 else cols
        ang = (pos * freqs[dd % 16]).astype(f32)
        cos[p] = np.cos(ang)
        first = (dd % 32) < 16
        sin[p] = (-np.sin(ang) if first else np.sin(ang))
        partner = p + 16 if first else p - 16
        rot[partner, p] = 1.0
    return cos, sin, rot


MIX_IN = ("qkT", "xbcT", "rblkT", "gqkT", "gglT", "av_tm", "sdt_tm", "gk_tm", "sz_tm", "gv_tm", "gog_tm")
BR = ("o_att", "o_ssd", "o_rwkv", "o_gla")


def declare_io(cx, mode="full"):
    nc = cx.nc
    TT, TS, NP, TP, PAST = cx.TT, cx.TS, cx.NP, cx.TP, cx.PAST
    cx.inp = {}
    cx.out = {}

    def I(name, shape, dtype=F32):
        cx.inp[name] = nc.dram_tensor(name, list(shape), dtype, kind="ExternalInput").ap()

    def O(name, shape, dtype=F32):
        cx.out[name] = nc.dram_tensor(name, list(shape), dtype, kind="ExternalOutput").ap()

    def Sx(name, shape, dtype=F32):
        kind = None
        if mode == "mix" and name in MIX_IN:
            kind = "ExternalInput"
        if mode == "mix" and name in BR:
            kind = "ExternalOutput"
        if mode == "dense" and name in BR:
            kind = "ExternalInput"
        cx.dram(name, shape, dtype, kind=kind)

    cx.const_names = []
    for n, shp in CONST_SHAPES.items():
        I(n, shp)
        cx.const_names.append(n)
    if mode in ("full", "dense"):
        _, _, win_sz = w_in_offsets()
        I("xT_in", [D, TT])
        I("condT", [128, KC, 2])
        I("b_modT", [L, 128, 96])
        I("norm1T", [L, 128, KC])
        I("norm2T", [L, 128, KC])
        I("fnormT", [128, KC])
        I("w_mod_p", [L, 128, 96 * KC * 128])
        I("w_in_p", [L, 128, win_sz])
        I("w_gate_p", [L, 128, 64 * KC * 128])
        I("w_br_p", [L, 128, 64 * 4 * 128])
        I("w_o_p", [L, 128, KC * KC * 128])
        I("w1_p", [L, 128, FC * KC * 128])
        I("w3_p", [L, 128, FC * KC * 128])
        I("w2_p", [L, 128, KC * FC * 128])
        O("yT", [D, TT])
        Sx("xres", [D, TT])
        Sx("hT", [D, TT], BF16)
    if mode in ("full", "mix"):
        I("qgT", [L, 128, 1]); I("kgT", [L, 128, 1])
        I("ropecos", [128, TS]); I("ropesin", [128, TS]); I("rotP", [128, 128])
        I("cache_k", [L, PAST, 128]); I("cache_v", [L, PAST, 128])
        I("ssd_cwT", [L, 128, 6, 3]); I("ssd_cbT", [L, 128, 6])
        I("ssd_dt_bias", [L, 16]); I("ssd_a_log", [L, 16]); I("ssd_d", [L, 8]); I("ssd_norm", [L, 512])
        I("state_ssd", [L, 2, 8, 64, 64])
        I("gla_g2", [L, 2, 16, 256]); I("gla_gb", [L, 2, 256]); I("gla_norm", [L, 128])
        I("state_gla", [L, 2, 4, 64, 128])
        I("rw_muT", [L, 128, 14])
        for n in ("rw_a0T", "rw_kkT", "rw_kaT", "rw_rkT"):
            I(n, [L, 128, 4])
        I("rw_w0", [L, 2, 512]); I("rw_w2", [L, 2, 64, 512]); I("rw_a2", [L, 64, 512]); I("rw_g2", [L, 128, 512])
        I("rw_ln_g", [L, 512]); I("rw_ln_b", [L, 512])
        I("state_rwkvT", [L, 2, 8, 64, 64])
        O("new_k", [NP, L, TP, 128]); O("new_v", [NP, L, TP, 128])
        O("new_ssd", [NP, L, 2, 8, 64, 64]); O("new_rwkvT", [NP, L, 2, 8, 64, 64]); O("new_gla", [NP, L, 2, 4, 64, 128])
        Sx("qn_d", [640, TT], BF16)
        Sx("bcT", [256, TT]); Sx("xs_tm", [TT, 512]); Sx("b_tm", [TT, 128]); Sx("dt_tm", [TT, 16]); Sx("ld_tm", [TT, 16])
        Sx("o_ssd_d", [2, TT, 512]); Sx("o_gla_d", [2, TT, 512])
        Sx("rw_fm", [4, 512, TT]); Sx("rw_s_tm", [TT, 8]); Sx("rw_v_tm", [TT, 512]); Sx("rw_g_tm", [TT, 512])
        Sx("rw_lw_tm", [2, TT, 512]); Sx("o_rw_d", [2, TT, 512])
    Sx("qkT", [640, TT]); Sx("xbcT", [768, TT]); Sx("rblkT", [1792, TT]); Sx("gqkT", [512, TT]); Sx("gglT", [16, TT])
    Sx("av_tm", [TT, 128]); Sx("sdt_tm", [TT, 16]); Sx("gk_tm", [TT, 256]); Sx("sz_tm", [TT, 512]); Sx("gv_tm", [TT, 512]); Sx("gog_tm", [TT, 512])
    for n in BR:
        Sx(n, [TT, 512])


def build_program(cfg, dbg=(), mode="full", layers=None, mixers=("attn", "ssd", "gla", "rwkv")):
    nc = bass.Bass("TRN2", target_bir_lowering=False)
    cx = Ctx(nc, cfg, dbg)
    declare_io(cx, mode)
    load_consts(cx)
    fns = {"attn": phase_attn, "ssd": phase_ssd, "gla": phase_gla, "rwkv": phase_rwkv}
    if mode == "mix":
        for l in (layers if layers is not None else range(L)):
            for m in mixers:
                fns[m](cx, l)
    else:
        phase_mod(cx)
        for l in range(L):
            phase_A(cx, l)
            if mode == "full":
                for m in mixers:
                    fns[m](cx, l)
            phase_C(cx, l, last=(l == L - 1))
    cx.fw.finish()
    cx.fw.emit()
    return nc, cx


def host_common_inputs(inp, cfg, mode="full"):
    f32 = np.float32
    A = lambda k: np.asarray(inp[k], f32)
    d = {}
    d.update(host_consts())
    if mode in ("full", "dense"):
        d["b_modT"] = np.stack([vec_pp(A("b_mod")[l]) for l in range(L)])
        d["norm1T"] = np.stack([vec_pp(A("norm1")[l]) for l in range(L)])
        d["norm2T"] = np.stack([vec_pp(A("norm2")[l]) for l in range(L)])
        d["fnormT"] = vec_pp(A("final_norm"))
        d["w_mod_p"] = np.stack([pack_chunks(A("w_mod")[l]) for l in range(L)])
        d["w_in_p"] = np.stack([pack_w_in(A("w_in")[l])[0] for l in range(L)])
        wg, wb = [], []
        for l in range(L):
            g = A("w_gate")[l]
            b = A("w_branch")[l]
            wg.append(np.concatenate([_pack_cols(g[i], f * 128, 128) for f in range(KC) for i in range(4)], axis=1))
            wb.append(np.concatenate([_pack_cols(b[i], f * 128, 128) for f in range(KC) for i in range(4)], axis=1))
        d["w_gate_p"] = np.stack(wg)
        d["w_br_p"] = np.stack(wb)
        d["w_o_p"] = np.stack([pack_chunks(A("w_o")[l]) for l in range(L)])
        d["w1_p"] = np.stack([pack_chunks(A("ffn_w1")[l]) for l in range(L)])
        d["w3_p"] = np.stack([pack_chunks(A("ffn_w3")[l]) for l in range(L)])
        d["w2_p"] = np.stack([pack_chunks(A("ffn_w2")[l]) for l in range(L)])
    if mode in ("full", "mix"):
        d["qgT"] = np.stack([np.tile(A("q_norm")[l], 2)[:, None] for l in range(L)])
        d["kgT"] = np.stack([np.tile(A("k_norm")[l], 2)[:, None] for l in range(L)])
        cos, sin, rot = rope_tables(cfg["TS"])
        d["ropecos"], d["ropesin"], d["rotP"] = cos, sin, rot
        cw = A("ssd_conv_w")
        d["ssd_cwT"] = np.ascontiguousarray(cw.reshape(L, 6, 128, 3).transpose(0, 2, 1, 3))
        d["ssd_cbT"] = np.stack([vec_pp(A("ssd_conv_b")[l]) for l in range(L)])
        d["ssd_dt_bias"] = A("ssd_dt_bias").reshape(L, 16)
        d["ssd_a_log"] = A("ssd_a_log").reshape(L, 16)
        d["ssd_d"] = A("ssd_d")
        d["ssd_norm"] = A("ssd_norm")
        d["gla_g2"] = A("gla_g2"); d["gla_gb"] = A("gla_gb"); d["gla_norm"] = A("gla_norm")
        d["rw_muT"] = np.stack([vec_pp(A("rwkv_mu")[l]) for l in range(L)])
        for n, k in (("rw_a0T", "rwkv_a0"), ("rw_kkT", "rwkv_kk"), ("rw_kaT", "rwkv_ka"), ("rw_rkT", "rwkv_rk")):
            d[n] = np.stack([vec_pp(A(k)[l]) for l in range(L)])
        d["rw_w0"] = A("rwkv_w0"); d["rw_w2"] = A("rwkv_w2"); d["rw_a2"] = A("rwkv_a2"); d["rw_g2"] = A("rwkv_g2")
        d["rw_ln_g"] = A("rwkv_ln_g"); d["rw_ln_b"] = A("rwkv_ln_b")
    return {k: np.ascontiguousarray(v, dtype=f32) for k, v in d.items()}


def host_core_inputs(inp, cfg, b_idx, p_idx, mode="full"):
    f32 = np.float32
    A = lambda k: np.asarray(inp[k], f32)
    d = {}
    if mode in ("full", "dense"):
        xs = A("x_sample")[b_idx]
        xp = np.concatenate([A("x_prompt")[p] for p in p_idx], axis=0)
        d["xT_in"] = np.concatenate([xs, xp], axis=0).T
        cond = np.stack([A("c_ctx"), A("c")[b_idx]], axis=-1)
        d["condT"] = cond.reshape(KC, 128, 2).transpose(1, 0, 2)
    if mode in ("full", "mix"):
        d["cache_k"] = A("cache_attn_k")[b_idx].reshape(L, cfg["PAST"], 128)
        d["cache_v"] = A("cache_attn_v")[b_idx].reshape(L, cfg["PAST"], 128)
        d["state_ssd"] = A("state_ssd")[b_idx]
        d["state_gla"] = A("state_gla")[b_idx]
        d["state_rwkvT"] = A("state_rwkv")[b_idx].transpose(0, 1, 2, 4, 3)
    return {k: np.ascontiguousarray(v, dtype=f32) for k, v in d.items()}


_PROG_CACHE = {}


def kernel(**inp):
    import sys
    global DEBUG_SITES
    DEBUG_SITES = False
    xs = np.asarray(inp["x_sample"])
    xp = np.asarray(inp["x_prompt"])
    n_cores = xs.shape[0]
    TS = xs.shape[1]
    NP = xp.shape[0] // n_cores
    TP = xp.shape[1]
    PAST = np.asarray(inp["cache_attn_k"]).shape[2]
    cfg = dict(TS=TS, NP=NP, TP=TP, PAST=PAST)
    key = (TS, NP, TP, PAST)
    if key not in _PROG_CACHE:
        _PROG_CACHE[key] = build_program(cfg, mode="full")
    nc, cx = _PROG_CACHE[key]
    common = host_common_inputs(inp, cfg, mode="full")
    in_maps = []
    for i in range(n_cores):
        d = dict(common)
        d.update(host_core_inputs(inp, cfg, i, list(range(i * NP, (i + 1) * NP)), mode="full"))
        in_maps.append(d)
    res = run_bass_kernel_spmd(nc, in_maps, core_ids=list(range(n_cores)))
    f32 = np.float32
    B = n_cores
    y_prompt = np.zeros((B * NP, TP, D), f32)
    y_sample = np.zeros((B, TS, D), f32)
    new_k = np.zeros((B * NP, L, TP, 2, 64), f32)
    new_v = np.zeros((B * NP, L, TP, 2, 64), f32)
    new_ssd = np.zeros((B * NP, L, 2, 8, 64, 64), f32)
    new_rwkv = np.zeros((B * NP, L, 2, 8, 64, 64), f32)
    new_gla = np.zeros((B * NP, L, 2, 4, 64, 128), f32)
    for i in range(n_cores):
        r = res.results[i]
        yT = np.asarray(r["yT"])
        y_sample[i] = yT[:, :TS].T
        for p in range(NP):
            y_prompt[i * NP + p] = yT[:, TS + p * TP:TS + (p + 1) * TP].T
        sl = slice(i * NP, (i + 1) * NP)
        new_k[sl] = np.asarray(r["new_k"]).reshape(NP, L, TP, 2, 64)
        new_v[sl] = np.asarray(r["new_v"]).reshape(NP, L, TP, 2, 64)
        new_ssd[sl] = np.asarray(r["new_ssd"])
        new_rwkv[sl] = np.asarray(r["new_rwkvT"]).transpose(0, 1, 2, 3, 5, 4)
        new_gla[sl] = np.asarray(r["new_gla"])
    return (y_prompt, y_sample, new_k, new_v, new_ssd, new_rwkv, new_gla)
```

```python
import numpy as np
import concourse.bass as bass
import concourse.mybir as mybir
from concourse.bass_utils import run_bass_kernel_spmd
from contextlib import ExitStack

F32 = mybir.dt.float32
BF16 = mybir.dt.bfloat16
AF = mybir.ActivationFunctionType
ALU = mybir.AluOpType
AX = mybir.AxisListType

D = 2048
L = 2
KC = D // 128
DFF = 5632
FC = DFF // 128
D_IN = 5408
N_CORES = 8


DEBUG_SITES = True


def _site():
    if not DEBUG_SITES:
        return None
    import sys
    f = sys._getframe(2)
    out = []
    while f is not None and len(out) < 4:
        if f.f_code.co_name not in ("X", "ACT", "TT_", "TS_", "STT", "COPY", "LOAD", "STORE", "mm_group", "transposes"):
            out.append(f"{f.f_code.co_name}:{f.f_lineno}")
        f = f.f_back
    return out


class Buf:
    __slots__ = ("w", "r")

    def __init__(self):
        self.w = None
        self.r = {}


class Tile:
    __slots__ = ("t", "b")

    def __init__(self, t, b=None):
        self.t = t
        self.b = b if b is not None else Buf()

    def __getitem__(self, k):
        return self.t[k]


def _bufs(xs):
    return [x.b if isinstance(x, Tile) else x for x in xs]


class EngState:
    def __init__(self, fw, name):
        self.fw = fw
        self.name = name
        self.prog = []
        self.known = {}
        self.sem = None
        self.cnt = 0
        self.dma_sems = []
        self.dma_cnt = []
        self.dma_rr = 0

    def new_sem(self):
        self.sem = self.fw.nc.alloc_semaphore(f"s_{self.name}_{self.fw.nsem}")
        self.fw.nsem += 1
        self.cnt = 0


class FW:
    SEM_MAX = 8000
    DMA_SEM_MAX = 500

    def __init__(self, nc, n_dma_sems=12):
        self.nc = nc
        self.nsem = 0
        self.eng = {}
        self.old_sems = []
        for n in ("pe", "act", "dve", "pool", "sp"):
            st = EngState(self, n)
            self.eng[n] = st
            if n != "sp":
                st.new_sem()
        for n in ("sp", "pool", "act"):
            st = self.eng[n]
            k = n_dma_sems if n == "sp" else 8
            for i in range(k):
                st.dma_sems.append(nc.alloc_semaphore(f"d_{n}_{i}"))
                st.dma_cnt.append(0)
                self.nsem += 1
        self.n_ops = 0

    def _wait(self, st, tok):
        sem, val = tok
        if st.known.get(sem.num, 0) < val:
            st.prog.append(("wait", sem, val))
            st.known[sem.num] = val

    def _deps(self, eng, reads, writes):
        st = self.eng[eng]
        deps = {}

        def add(tok):
            if tok is None:
                return
            s, v = tok
            if deps.get(s.num, (None, 0))[1] < v:
                deps[s.num] = (s, v)

        for b in reads:
            add(b.w)
        for b in writes:
            add(b.w)
            for t in b.r.values():
                add(t)
        for s, v in deps.values():
            if eng == "pe" and st.sem is not None and s.num == st.sem.num:
                continue
            self._wait(st, (s, v))

    def _mark(self, tok, reads, writes):
        s, v = tok
        for b in reads:
            b.r[s.num] = tok
        for b in writes:
            b.w = tok
            b.r = {}

    def op(self, eng, fn, reads=(), writes=()):
        reads = _bufs(reads)
        writes = _bufs(writes)
        st = self.eng[eng]
        self._deps(eng, reads, writes)
        if st.cnt >= self.SEM_MAX:
            self.old_sems.append((st.sem, st.cnt))
            st.new_sem()
        st.cnt += 1
        tok = (st.sem, st.cnt)
        st.prog.append(("op", fn, st.sem, 1, _site()))
        self._mark(tok, reads, writes)
        self.n_ops += 1
        return tok

    def dma(self, q, out, in_, reads=(), writes=(), **kw):
        reads = _bufs(reads)
        writes = _bufs(writes)
        st = self.eng[q]
        self._deps(q, reads, writes)
        i = st.dma_rr
        st.dma_rr = (i + 1) % len(st.dma_sems)
        sem = st.dma_sems[i]
        c = st.dma_cnt[i]
        if c > 0:
            self._wait(st, (sem, 16 * c))
        if c >= self.DMA_SEM_MAX:
            self.old_sems.append((sem, 16 * c))
            sem = self.nc.alloc_semaphore(f"d_{q}_{self.nsem}")
            self.nsem += 1
            st.dma_sems[i] = sem
            c = 0
        st.dma_cnt[i] = c + 1
        tok = (sem, 16 * (c + 1))
        st.prog.append(("op", lambda e: e.dma_start(out=out, in_=in_, **kw), sem, 16, _site()))
        self._mark(tok, reads, writes)
        self.n_ops += 1
        return tok

    def barrier(self, engines=("pe", "act", "dve", "pool", "sp")):
        toks = list(self.old_sems)
        for n, st in self.eng.items():
            if st.sem is not None and st.cnt > 0:
                toks.append((st.sem, st.cnt))
            for sem, c in zip(st.dma_sems, st.dma_cnt):
                if c > 0:
                    toks.append((sem, 16 * c))
        for n in engines:
            st = self.eng[n]
            for t in toks:
                if st.sem is not None and t[0].num == st.sem.num:
                    continue
                self._wait(st, t)

    def finish(self):
        self.barrier(engines=("sp",))

    def emit(self):
        nc = self.nc
        engs = {"pe": "tensor", "act": "scalar", "dve": "vector", "pool": "gpsimd", "sp": "sync"}
        with nc.Block() as block:
            for n, attr in engs.items():
                st = self.eng[n]

                def body(e, st=st):
                    for it in st.prog:
                        if it[0] == "wait":
                            e.wait_ge(it[1], it[2])
                        else:
                            try:
                                ins = it[1](e)
                            except Exception:
                                print("FAILED OP SITE:", it[4])
                                raise
                            ins.then_inc(it[2], it[3])

                getattr(block, attr)(body)


class Ctx:
    def __init__(self, nc, cfg, dbg=()):
        self.nc = nc
        self.fw = FW(nc)
        self.cfg = cfg
        self.dbg = set(dbg)
        self.dr = {}
        self.uid = 0
        self.es = None
        self.ps = [Tile(nc.alloc_psum_tensor(f"psb{i}", [128, 512], F32)) for i in range(8)]
        self.ps_i = 0
        self.ps_n = 8
        self.ev_i = 0
        self.TS = cfg["TS"]
        self.NP = cfg["NP"]
        self.TP = cfg["TP"]
        self.PAST = cfg["PAST"]
        self.TT = self.TS + self.NP * self.TP
        assert self.TS % 512 == 0 and (self.NP * self.TP) % 512 == 0
        self.NTILE = self.TT // 512
        self.seqs = [(0, self.TS, True, -1)] + [(self.TS + p * self.TP, self.TP, False, p) for p in range(self.NP)]

    def psum(self):
        p = self.ps[self.ps_i]
        self.ps_i = (self.ps_i + 1) % self.ps_n
        return p

    def dram(self, name, shape, dtype=F32, kind=None):
        if kind is None:
            kind = "ExternalOutput" if name in self.dbg else "Internal"
        t = self.nc.dram_tensor(name, list(shape), dtype, kind=kind).ap()
        self.dr[name] = t
        return t

    def phase_begin(self):
        assert self.es is None
        self.es = ExitStack()

    def phase_end(self):
        self.fw.barrier()
        self.es.close()
        self.es = None

    def sb(self, shape, dtype=F32, name="t"):
        self.uid += 1
        t = self.es.enter_context(self.nc.sbuf_tensor(f"{name}_{self.uid}", list(shape), dtype))
        return Tile(t)

    def ring(self, n, shape, dtype=F32, name="r"):
        return Ring([self.sb(shape, dtype, name) for _ in range(n)])

    def evac_eng(self):
        self.ev_i += 1
        return "act" if self.ev_i % 2 == 0 else "dve"

    def copy(self, eng, out_ap, in_ap, reads, writes):
        if eng == "act":
            return self.fw.op("act", lambda e: e.activation(out=out_ap, in_=in_ap, func=AF.Copy), reads, writes)
        return self.fw.op(eng, lambda e: e.tensor_copy(out_ap, in_ap), reads, writes)


class Ring:
    def __init__(self, tiles):
        self.tiles = tiles
        self.i = 0

    def next(self):
        t = self.tiles[self.i]
        self.i = (self.i + 1) % len(self.tiles)
        return t


FM_JOBS = []
for c in range(5):
    FM_JOBS.append((c * 128, 128, "qkT", c * 128))
for c in range(6):
    FM_JOBS.append((1280 + c * 128, 128, "xbcT", c * 128))
for c in range(14):
    FM_JOBS.append((2064 + c * 128, 128, "rblkT", c * 128))
for c in range(4):
    FM_JOBS.append((3856 + c * 128, 128, "gqkT", c * 128))
FM_JOBS.append((4880, 16, "gglT", 0))
TM_GROUPS = [
    [(640, 128, "av_tm", 0), (2048, 16, "sdt_tm", 0), (4112, 256, "gk_tm", 0)],
    [(768, 512, "sz_tm", 0)],
    [(4368, 512, "gv_tm", 0)],
    [(4896, 512, "gog_tm", 0)],
]


def _pack_cols(W, c0, n):
    K = W.shape[0]
    blk = W[:, c0:c0 + n].reshape(K // 128, 128, n).transpose(1, 0, 2).reshape(128, (K // 128) * n)
    return blk


def pack_w_in(w_in_l):
    parts = []
    offs_fm = []
    off = 0
    for (c0, n, _, _) in FM_JOBS:
        parts.append(_pack_cols(w_in_l, c0, n))
        offs_fm.append(off)
        off += KC * n
    offs_tm = []
    for grp in TM_GROUPS:
        cols = np.concatenate([np.arange(c0, c0 + n) for (c0, n, _, _) in grp])
        Wg = w_in_l[:, cols]
        parts.append(_pack_cols(Wg, 0, Wg.shape[1]))
        offs_tm.append(off)
        off += KC * Wg.shape[1]
    return np.ascontiguousarray(np.concatenate(parts, axis=1)), offs_fm, offs_tm


def w_in_offsets():
    offs_fm = []
    off = 0
    for (c0, n, _, _) in FM_JOBS:
        offs_fm.append(off)
        off += KC * n
    offs_tm = []
    for grp in TM_GROUPS:
        n = sum(g[1] for g in grp)
        offs_tm.append(off)
        off += KC * n
    return offs_fm, offs_tm, off


def pack_chunks(W):
    C = W.shape[1]
    return np.ascontiguousarray(np.concatenate([_pack_cols(W, c * 128, 128) for c in range(C // 128)], axis=1))


def vec_pp(v):
    return np.ascontiguousarray(v.reshape(-1, 128).T)


def _bind(method, args, kw):
    return lambda e: getattr(e, method)(*args, **kw)


def X(cx, eng, method, *args, reads=(), writes=(), **kw):
    return cx.fw.op(eng, _bind(method, args, kw), reads, writes)


def ACT(cx, out, in_, func, reads, writes, **kw):
    return X(cx, "act", "activation", out=out, in_=in_, func=func, reads=reads, writes=writes, **kw)


def TT_(cx, eng, out, in0, in1, op, reads, writes):
    return X(cx, eng, "tensor_tensor", out, in0, in1, op, reads=reads, writes=writes)


def TS_(cx, eng, out, in0, s1, s2, op0, op1, reads, writes):
    if s2 is None:
        return X(cx, eng, "tensor_scalar", out, in0, s1, None, op0, reads=reads, writes=writes)
    return X(cx, eng, "tensor_scalar", out, in0, s1, s2, op0, op1, reads=reads, writes=writes)


def STT(cx, eng, out, in0, scalar, in1, op0, op1, reads, writes):
    return X(cx, eng, "scalar_tensor_tensor", out, in0, scalar, in1, op0, op1, reads=reads, writes=writes)


def COPY(cx, eng, out, in_, reads, writes):
    if eng == "act":
        return ACT(cx, out, in_, AF.Copy, reads, writes)
    return X(cx, eng, "tensor_copy", out, in_, reads=reads, writes=writes)


def _mm_fn(ps_ap, pairs):
    n = len(pairs)

    def fn(e):
        ins = None
        for i, (l, r) in enumerate(pairs):
            ins = e.matmul(ps_ap, lhsT=l, rhs=r, start=(i == 0), stop=(i == n - 1))
        return ins
    return fn


def mm_group(cx, ps_ap, pairs, reads, ps_tile):
    return cx.fw.op("pe", _mm_fn(ps_ap, list(pairs)), reads=reads, writes=[ps_tile])


def _tr_fn(items, ident_ap):
    def fn(e):
        ins = None
        for (o, i) in items:
            ins = e.transpose(o, i, ident_ap)
        return ins
    return fn


def transposes(cx, items, reads, ps_tile, np_=128):
    ident = cx.C["ident"]
    return cx.fw.op("pe", _tr_fn(list(items), ident[0:np_, 0:np_]), reads=list(reads) + [ident], writes=[ps_tile])


def LOAD(cx, out_ap, in_ap, tile_, reads=()):
    return cx.fw.dma("pool", out_ap, in_ap, reads=list(reads), writes=[tile_])


WNAMES = ("w_in_p", "w_gate_p", "w_br_p", "w_o_p", "w1_p", "w3_p", "w2_p")


def phase_convert(cx, l):
    CH = 16384
    for n in WNAMES:
        src = cx.inp[n]
        dst = cx.dr[n + "_b"]
        X_ = src.shape[2]
        for a in range(0, X_, CH):
            b = min(X_, a + CH)
            cx.fw.dma("pool", dst[l, :, a:b], src[l, :, a:b])
    st = cx.fw.eng["pool"]
    for sem, c in zip(st.dma_sems, st.dma_cnt):
        if c > 0:
            cx.fw._wait(st, (sem, 16 * c))


def STORE(cx, out_ap, in_ap, tile_):
    return cx.fw.dma("sp", out_ap, in_ap, reads=[tile_])


def load_consts(cx):
    nc = cx.nc
    C = {}
    for name in cx.const_names:
        shp = list(cx.inp[name].shape)
        t = Tile(nc.alloc_sbuf_tensor("c_" + name, shp, F32))
        LOAD(cx, t[:], cx.inp[name], t)
        C[name] = t
    cx.C = C


def phase_mod(cx):
    nc = cx.nc
    cx.MOD = []
    pers = lambda name, shape: Tile(nc.alloc_sbuf_tensor("m_" + name, list(shape), F32))
    Ms = [{k: pers(f"{k}_{l}", [128, KC, 2]) for k in ("A1", "B1", "G1", "A2", "B2", "G2")} for l in range(L)]
    fn = pers("fnorm", [128, KC])
    cx.phase_begin()
    cond = cx.sb([128, KC, 2])
    LOAD(cx, cond[:], cx.inp["condT"], cond)
    scond = cx.sb([128, KC, 2])
    ACT(cx, scond[:], cond[:], AF.Silu, [cond], [scond])
    wring = cx.ring(3, [128, KC * 128], F32, "wmod")
    for l in range(L):
        modT = cx.sb([128, 96, 2])
        bm = cx.sb([128, 96])
        LOAD(cx, bm[:], cx.inp["b_modT"][l], bm)
        for c in range(96):
            wt = wring.next()
            LOAD(cx, wt[:], cx.inp["w_mod_p"][l, :, c * KC * 128:(c + 1) * KC * 128], wt)
            ps = cx.psum()
            mm_group(cx, ps[:, 0:2], [(wt[:, k * 128:(k + 1) * 128], scond[:, k, :]) for k in range(KC)], [wt, scond], ps)
            TS_(cx, "dve", modT[:, c, :], ps[:, 0:2], bm[:, c:c + 1], None, ALU.add, None, [ps, bm], [modT])
        n1 = cx.sb([128, KC])
        n2 = cx.sb([128, KC])
        LOAD(cx, n1[:], cx.inp["norm1T"][l], n1)
        LOAD(cx, n2[:], cx.inp["norm2T"][l], n2)
        M = Ms[l]

        def mk(dst, ch, nrm=None):
            src = modT[:, ch * 16:(ch + 1) * 16, :]
            if nrm is None:
                COPY(cx, "dve", dst[:], src, [modT], [dst])
            else:
                STT(cx, "dve", dst[:], src, 1.0, nrm[:].unsqueeze(2).broadcast_to([128, KC, 2]), ALU.add, ALU.mult, [modT, nrm], [dst])
        mk(M["B1"], 0)
        mk(M["A1"], 1, n1)
        mk(M["G1"], 2)
        mk(M["B2"], 3)
        mk(M["A2"], 4, n2)
        mk(M["G2"], 5)
        cx.MOD.append(M)
    LOAD(cx, fn[:], cx.inp["fnormT"], fn)
    cx.FNORM = fn
    cx.phase_end()


def rms_fm(cx, xT, scal, bias, outs, sqring, tmpring, small):
    ps = cx.psum()
    ones = cx.C["ones"]
    for k in range(KC):
        sq = sqring.next()
        ACT(cx, sq[:], xT[:, k, :], AF.Square, [xT], [sq])
        X(cx, "pe", "matmul", ps[:], lhsT=ones[:], rhs=sq[:], start=(k == 0), stop=(k == KC - 1), reads=[sq, ones], writes=[ps])
    sd, rstd = small
    ACT(cx, sd[:], ps[:], AF.Sqrt, [ps], [sd], bias=cx.C["eps6"][:, 0:1], scale=1.0 / D)
    X(cx, "dve", "reciprocal", rstd[:], sd[:], reads=[sd], writes=[rstd])
    for k in range(KC):
        oap, ot = outs[k]
        if bias is None:
            STT(cx, "dve", oap, xT[:, k, :], scal[k], rstd[:], ALU.mult, ALU.mult, [xT, rstd], [ot])
        else:
            tmp = tmpring.next()
            STT(cx, "dve", tmp[:], xT[:, k, :], scal[k], rstd[:], ALU.mult, ALU.mult, [xT, rstd], [tmp])
            ACT(cx, oap, tmp[:], AF.Identity, [tmp], [ot], bias=bias[k], scale=1.0)


def phase_A(cx, l):
    offs_fm, offs_tm, _ = w_in_offsets()
    wp = cx.dr["w_in_p_b"]
    wrd = [cx.wbuf[l]]
    M = cx.MOD[l]
    cx.phase_begin()
    xring = cx.ring(2, [128, KC, 512], F32, "xT")
    hring = cx.ring(2, [128, KC, 512], BF16, "hT")
    sqring = cx.ring(2, [128, 512], F32, "sq")
    tmpring = cx.ring(2, [128, 512], F32, "tmp")
    small = (cx.sb([128, 512]), cx.sb([128, 512]))
    wring = cx.ring(3, [128, KC * 512], BF16, "w")
    oring = cx.ring(4, [128, 512], F32, "o")
    src = cx.dr["xres"] if l > 0 else cx.inp["xT_in"]
    for ti in range(cx.NTILE):
        t0 = ti * 512
        ci = 1 if t0 < cx.TS else 0
        xT = xring.next()
        LOAD(cx, xT[:], src[:, t0:t0 + 512].rearrange("(k p) n -> p k n", p=128), xT)
        hT = hring.next()
        rms_fm(cx, xT, [M["A1"][:, k, ci:ci + 1] for k in range(KC)], [M["B1"][:, k, ci:ci + 1] for k in range(KC)],
               [(hT[:, k, :], hT) for k in range(KC)], sqring, tmpring, small)
        STORE(cx, cx.dr["hT"][:, t0:t0 + 512].rearrange("(k p) n -> p k n", p=128), hT[:], hT)
        for j, (c0, n, dst, doff) in enumerate(FM_JOBS):
            wt = wring.next()
            LOAD(cx, wt[:, 0:KC * n], wp[l, :, offs_fm[j]:offs_fm[j] + KC * n], wt, wrd)
            ps = cx.psum()
            mm_group(cx, ps[0:n, :], [(wt[:, k * n:(k + 1) * n], hT[:, k, :]) for k in range(KC)], [wt, hT], ps)
            o = oring.next()
            COPY(cx, cx.evac_eng(), o[0:n, :], ps[0:n, :], [ps], [o])
            STORE(cx, cx.dr[dst][doff:doff + n, t0:t0 + 512], o[0:n, :], o)
        for g, grp in enumerate(TM_GROUPS):
            n = sum(x[1] for x in grp)
            wt = wring.next()
            LOAD(cx, wt[:, 0:KC * n], wp[l, :, offs_tm[g]:offs_tm[g] + KC * n], wt, wrd)
            for s in range(4):
                ps = cx.psum()
                mm_group(cx, ps[:, 0:n], [(hT[:, k, s * 128:(s + 1) * 128], wt[:, k * n:(k + 1) * n]) for k in range(KC)], [wt, hT], ps)
                o = oring.next()
                COPY(cx, cx.evac_eng(), o[:, 0:n], ps[:, 0:n], [ps], [o])
                co = 0
                for (c0, nn, dst, dcol) in grp:
                    STORE(cx, cx.dr[dst][t0 + s * 128:t0 + (s + 1) * 128, dcol:dcol + nn], o[:, co:co + nn], o)
                    co += nn
    cx.phase_end()


def phase_C(cx, l, last):
    M = cx.MOD[l]
    wrd = [cx.wbuf[l]]
    cx.phase_begin()
    xT = cx.sb([128, KC, 512], F32, "xT")
    hT = cx.sb([128, KC, 512], BF16, "hT")
    U = cx.sb([128, FC, 512], BF16, "U")
    oin = cx.ring(2, [128, 4, 512], F32, "oin")
    sqring = cx.ring(2, [128, 512], F32, "sq")
    tmpring = cx.ring(4, [128, 512], F32, "tmp")
    small = (cx.sb([128, 512]), cx.sb([128, 512]))
    wring = cx.ring(3, [128, FC * 128], BF16, "w")
    src = cx.dr["xres"] if l > 0 else cx.inp["xT_in"]
    br_names = ["o_att", "o_ssd", "o_rwkv", "o_gla"]
    GSZ = KC * 128
    BSZ = 4 * 128
    W2SZ = FC * 128
    for ti in range(cx.NTILE):
        t0 = ti * 512
        ci = 1 if t0 < cx.TS else 0
        LOAD(cx, xT[:], src[:, t0:t0 + 512].rearrange("(k p) n -> p k n", p=128), xT)
        LOAD(cx, hT[:], cx.dr["hT"][:, t0:t0 + 512].rearrange("(k p) n -> p k n", p=128), hT)
        for i in range(4):
            ot = oin.next()
            LOAD(cx, ot[:], cx.dr[br_names[i]][t0:t0 + 512, :].rearrange("(s p) c -> p s c", p=128), ot)
            for c in range(4):
                ps = cx.psum()
                transposes(cx, [(ps[:, s * 128:(s + 1) * 128], ot[:, s, c * 128:(c + 1) * 128]) for s in range(4)], [ot], ps)
                COPY(cx, cx.evac_eng(), U[:, i * 4 + c, :], ps[:], [ps], [U])
        for f in range(KC):
            acc = None
            for i in range(4):
                wg = wring.next()
                LOAD(cx, wg[:, 0:GSZ], cx.dr["w_gate_p_b"][l, :, (f * 4 + i) * GSZ:(f * 4 + i + 1) * GSZ], wg, wrd)
                wb = wring.next()
                LOAD(cx, wb[:, 0:BSZ], cx.dr["w_br_p_b"][l, :, (f * 4 + i) * BSZ:(f * 4 + i + 1) * BSZ], wb, wrd)
                psg = cx.psum()
                mm_group(cx, psg[:], [(wg[:, k * 128:(k + 1) * 128], hT[:, k, :]) for k in range(KC)], [wg, hT], psg)
                psb = cx.psum()
                mm_group(cx, psb[:], [(wb[:, k * 128:(k + 1) * 128], U[:, i * 4 + k, :]) for k in range(4)], [wb, U], psb)
                sig = tmpring.next()
                ACT(cx, sig[:], psg[:], AF.Sigmoid, [psg], [sig])
                if i == 0:
                    acc = tmpring.next()
                    TT_(cx, "dve", acc[:], sig[:], psb[:], ALU.mult, [sig, psb], [acc])
                else:
                    TT_(cx, "dve", sig[:], sig[:], psb[:], ALU.mult, [sig, psb], [sig])
                    if i < 3:
                        TT_(cx, "dve", acc[:], acc[:], sig[:], ALU.add, [sig, acc], [acc])
                    else:
                        TT_(cx, "dve", U[:, 16 + f, :], acc[:], sig[:], ALU.add, [sig, acc], [U])
        for f in range(KC):
            wo = wring.next()
            LOAD(cx, wo[:, 0:GSZ], cx.dr["w_o_p_b"][l, :, f * GSZ:(f + 1) * GSZ], wo, wrd)
            ps = cx.psum()
            mm_group(cx, ps[:], [(wo[:, k * 128:(k + 1) * 128], U[:, 16 + k, :]) for k in range(KC)], [wo, U], ps)
            STT(cx, "dve", xT[:, f, :], ps[:], M["G1"][:, f, ci:ci + 1], xT[:, f, :], ALU.mult, ALU.add, [ps, xT], [xT])
        rms_fm(cx, xT, [M["A2"][:, k, ci:ci + 1] for k in range(KC)], [M["B2"][:, k, ci:ci + 1] for k in range(KC)],
               [(hT[:, k, :], hT) for k in range(KC)], sqring, tmpring, small)
        for j in range(FC):
            w1 = wring.next()
            LOAD(cx, w1[:, 0:GSZ], cx.dr["w1_p_b"][l, :, j * GSZ:(j + 1) * GSZ], w1, wrd)
            w3 = wring.next()
            LOAD(cx, w3[:, 0:GSZ], cx.dr["w3_p_b"][l, :, j * GSZ:(j + 1) * GSZ], w3, wrd)
            p1 = cx.psum()
            mm_group(cx, p1[:], [(w1[:, k * 128:(k + 1) * 128], hT[:, k, :]) for k in range(KC)], [w1, hT], p1)
            p3 = cx.psum()
            mm_group(cx, p3[:], [(w3[:, k * 128:(k + 1) * 128], hT[:, k, :]) for k in range(KC)], [w3, hT], p3)
            s1 = tmpring.next()
            ACT(cx, s1[:], p1[:], AF.Silu, [p1], [s1])
            TT_(cx, "dve", U[:, j, :], s1[:], p3[:], ALU.mult, [s1, p3], [U])
        for f in range(KC):
            w2 = wring.next()
            LOAD(cx, w2[:, 0:W2SZ], cx.dr["w2_p_b"][l, :, f * W2SZ:(f + 1) * W2SZ], w2, wrd)
            ps = cx.psum()
            mm_group(cx, ps[:], [(w2[:, k * 128:(k + 1) * 128], U[:, k, :]) for k in range(FC)], [w2, U], ps)
            STT(cx, "dve", xT[:, f, :], ps[:], M["G2"][:, f, ci:ci + 1], xT[:, f, :], ALU.mult, ALU.add, [ps, xT], [xT])
        if not last:
            STORE(cx, cx.dr["xres"][:, t0:t0 + 512].rearrange("(k p) n -> p k n", p=128), xT[:], xT)
        else:
            outs = []
            stg = [tmpring.next() for _ in range(4)]
            ps = cx.psum()
            ones = cx.C["ones"]
            for k in range(KC):
                sq = sqring.next()
                ACT(cx, sq[:], xT[:, k, :], AF.Square, [xT], [sq])
                X(cx, "pe", "matmul", ps[:], lhsT=ones[:], rhs=sq[:], start=(k == 0), stop=(k == KC - 1), reads=[sq, ones], writes=[ps])
            sd, rstd = small
            ACT(cx, sd[:], ps[:], AF.Sqrt, [ps], [sd], bias=cx.C["eps6"][:, 0:1], scale=1.0 / D)
            X(cx, "dve", "reciprocal", rstd[:], sd[:], reads=[sd], writes=[rstd])
            for k in range(KC):
                t = stg[k % 4]
                STT(cx, "dve", t[:], xT[:, k, :], cx.FNORM[:, k:k + 1], rstd[:], ALU.mult, ALU.mult, [xT, rstd], [t])
                STORE(cx, cx.out["yT"][k * 128:(k + 1) * 128, t0:t0 + 512], t[:], t)
    cx.phase_end()


def phase_attn(cx, l):
    TS, TT = cx.TS, cx.TT
    qn_d = cx.dr["qn_d"]
    cx.phase_begin()
    gq = cx.sb([128, 1]); gk = cx.sb([128, 1])
    LOAD(cx, gq[:], cx.inp["qgT"][l], gq)
    LOAD(cx, gk[:], cx.inp["kgT"][l], gk)
    cos = cx.sb([128, TS]); sin = cx.sb([128, TS]); rotP = cx.sb([128, 128])
    LOAD(cx, cos[:], cx.inp["ropecos"], cos)
    LOAD(cx, sin[:], cx.inp["ropesin"], sin)
    LOAD(cx, rotP[:], cx.inp["rotP"], rotP)
    rawr = cx.ring(2, [128, 5, 512], F32, "raw")
    sqr = cx.ring(2, [128, 512], F32, "sq")
    sdr = cx.ring(2, [128, 512], F32, "sd")
    qnr = cx.ring(2, [128, 512], F32, "qn")
    t1r = cx.ring(2, [128, 512], F32, "t1")
    obr = cx.ring(3, [128, 512], BF16, "ob")
    ktr = cx.ring(2, [128, 4, 128], F32, "kt")
    bones = cx.C["bones"]
    for ti in range(cx.NTILE):
        t0 = ti * 512
        samp = t0 < TS
        raw = rawr.next()
        LOAD(cx, raw[:], cx.dr["qkT"][:, t0:t0 + 512].rearrange("(c p) n -> p c n", p=128), raw)
        for c in range(5):
            g = gq if c < 4 else gk
            sq = sqr.next()
            ACT(cx, sq[:], raw[:, c, :], AF.Square, [raw], [sq])
            ps = cx.psum()
            X(cx, "pe", "matmul", ps[:], lhsT=bones[:], rhs=sq[:], start=True, stop=True, reads=[sq, bones], writes=[ps])
            sd = sdr.next()
            ACT(cx, sd[:], ps[:], AF.Sqrt, [ps], [sd], bias=cx.C["eps6"][:, 0:1], scale=1.0 / 64)
            X(cx, "dve", "reciprocal", sd[:], sd[:], reads=[sd], writes=[sd])
            qn = qnr.next()
            STT(cx, "dve", qn[:], raw[:, c, :], g[:, 0:1], sd[:], ALU.mult, ALU.mult, [raw, g, sd], [qn])
            ob = obr.next()
            if samp:
                ps2 = cx.psum()
                X(cx, "pe", "matmul", ps2[:], lhsT=rotP[:], rhs=qn[:], start=True, stop=True, reads=[qn, rotP], writes=[ps2])
                t1 = t1r.next()
                TT_(cx, "dve", t1[:], qn[:], cos[:, t0:t0 + 512], ALU.mult, [qn, cos], [t1])
                t2 = t1r.next()
                TT_(cx, "dve", t2[:], ps2[:], sin[:, t0:t0 + 512], ALU.mult, [ps2, sin], [t2])
                TT_(cx, "dve", ob[:], t1[:], t2[:], ALU.add, [t1, t2], [ob])
            else:
                ACT(cx, ob[:], qn[:], AF.Copy, [qn], [ob])
                if c == 4:
                    ps3 = cx.psum()
                    transposes(cx, [(ps3[:, s * 128:(s + 1) * 128], qn[:, s * 128:(s + 1) * 128]) for s in range(4)], [qn], ps3)
                    kt = ktr.next()
                    COPY(cx, cx.evac_eng(), kt[:].rearrange("p s c -> p (s c)"), ps3[:], [ps3], [kt])
                    for s in range(4):
                        tok = t0 + s * 128 - TS
                        p, tin = tok // cx.TP, tok % cx.TP
                        STORE(cx, cx.out["new_k"][p, l, tin:tin + 128, :], kt[:, s, :], kt)
            STORE(cx, qn_d[c * 128:(c + 1) * 128, t0:t0 + 512], ob[:], ob)
    for (st, T, samp, p) in cx.seqs:
        if not samp:
            cx.fw.dma("sp", cx.out["new_v"][p, l], cx.dr["av_tm"][st:st + T, :])
    cx.phase_end()
    cx.phase_begin()
    save_ps = cx.ps_n
    cx.ps_n = 6
    cx.ps_i = 0
    po_banks = [cx.ps[6], cx.ps[7]]
    po_i = 0
    Smax = TS + cx.PAST
    KTa = cx.sb([128, Smax], BF16, "KTa")
    KTb = cx.sb([128, Smax], BF16, "KTb")
    NCHmax = Smax // 128
    V1 = cx.sb([128, NCHmax, 2, 128], BF16, "V1")
    ckr = cx.ring(2, [128, 128], F32, "ck")
    qTr = cx.ring(2, [128, 4, 512], BF16, "qT")
    pTr = cx.ring(3, [128, 512], BF16, "pT")
    osr = cx.ring(2, [128, 4, 512], F32, "os")
    recr = cx.ring(2, [128, 4], F32, "rec")
    for (st, T, samp, p) in cx.seqs:
        S = T + (cx.PAST if samp else 0)
        NCH = S // 128
        NO = T // 128
        LOAD(cx, KTa[:, 0:T], qn_d[512:640, st:st + T], KTa)
        LOAD(cx, KTb[0:64, 0:T], qn_d[576:640, st:st + T], KTb)
        LOAD(cx, KTb[64:128, 0:T], qn_d[512:576, st:st + T], KTb)
        X(cx, "dve", "memset", V1[:, :, :, 64:66], 1.0, reads=[], writes=[V1])
        for kv_ in range(2):
            for c0_ in range(0, NO, 8):
                c1_ = min(NO, c0_ + 8)
                LOAD(cx, V1[:, c0_:c1_, kv_, 0:64],
                     cx.dr["av_tm"][st + c0_ * 128:st + c1_ * 128, kv_ * 64:(kv_ + 1) * 64].rearrange("(c p) d -> p c d", p=128), V1)
        if samp:
            NPC = cx.PAST // 128
            for kv_ in range(2):
                LOAD(cx, V1[:, NO:NO + NPC, kv_, 0:64], cx.inp["cache_v"][l][:, kv_ * 64:(kv_ + 1) * 64].rearrange("(c p) d -> p c d", p=128), V1)
            for j in range(NPC):
                for (KTx, swap) in ((KTa, False), (KTb, True)):
                    ck = ckr.next()
                    src = cx.inp["cache_k"][l, j * 128:(j + 1) * 128, :]
                    if not swap:
                        LOAD(cx, ck[:], src, ck)
                    else:
                        LOAD(cx, ck[:, 0:64], src[:, 64:128], ck)
                        LOAD(cx, ck[:, 64:128], src[:, 0:64], ck)
                    ps = cx.psum()
                    transposes(cx, [(ps[:, 0:128], ck[:])], [ck], ps)
                    COPY(cx, cx.evac_eng(), KTx[:, T + j * 128:T + (j + 1) * 128], ps[:, 0:128], [ps], [KTx])
        QN = min(512, T)
        ns = QN // 128
        for q0 in range(0, T, QN):
            qT = qTr.next()
            LOAD(cx, qT[:, :, 0:QN], qn_d[0:512, st + q0:st + q0 + QN].rearrange("(c p) n -> p c n", p=128), qT)
            osb = osr.next()
            for h in range(8):
                kv = h // 4
                base = (h % 2) * 64
                c = h // 2
                KT = KTa if (h % 2) == kv else KTb
                po = po_banks[po_i]
                po_i = 1 - po_i
                for ch in range(NCH):
                    ps = cx.psum()
                    X(cx, "pe", "matmul", ps[:, 0:QN], lhsT=KT[base:base + 64, ch * 128:(ch + 1) * 128], rhs=qT[base:base + 64, c, 0:QN],
                      start=True, stop=True, reads=[KT, qT], writes=[ps])
                    pT = pTr.next()
                    ACT(cx, pT[:, 0:QN], ps[:, 0:QN], AF.Exp, [ps], [pT], scale=0.125)
                    cx.fw.op("pe", _pv_fn(po, pT, V1, ch, kv, ns, ch == 0, ch == NCH - 1), reads=[pT, V1], writes=[po])
                pov = po[:, 0:ns * 128].rearrange("p (s c) -> p s c", c=128)
                rec = recr.next()
                X(cx, "dve", "reciprocal", rec[:, 0:ns], pov[:, :, 64], reads=[po], writes=[rec])
                TT_(cx, "dve", osb[:, 0:ns, h * 64:(h + 1) * 64], pov[:, :, 0:64], rec[:, 0:ns].unsqueeze(2).broadcast_to([128, ns, 64]),
                    ALU.mult, [po, rec], [osb])
            STORE(cx, cx.dr["o_att"][st + q0:st + q0 + QN, :].rearrange("(s p) c -> p s c", p=128), osb[:, 0:ns, :], osb)
    cx.ps_n = save_ps
    cx.ps_i = 0
    cx.phase_end()


def _pv_fn(po, pT, V1, ch, kv, ns, first, last):
    def fn(e):
        ins = None
        for s in range(ns):
            ins = e.matmul(po[:, s * 128:s * 128 + 65], lhsT=pT[:, s * 128:(s + 1) * 128], rhs=V1[:, ch, kv, 0:65], start=(first and s == 0), stop=last)
        return ins
    return fn


def _segments(cx, n=512):
    segs = []
    for (st, T, samp, p) in cx.seqs:
        for a in range(0, T, n):
            segs.append((st + a, min(n, T - a), st, st + T))
    return segs


def _load_halo(cx, tile_, nch, src, t0, n, s0, s1):
    lo = t0 - 1 if t0 > s0 else t0
    hi = t0 + n + 1 if t0 + n < s1 else t0 + n
    if lo == t0:
        X(cx, "dve", "memset", tile_[:, :, 0:1], 0.0, reads=[], writes=[tile_])
    if hi == t0 + n:
        X(cx, "dve", "memset", tile_[:, :, n + 1:n + 2], 0.0, reads=[], writes=[tile_])
    LOAD(cx, tile_[:, :, lo - (t0 - 1):hi - (t0 - 1)], src[:, lo:hi].rearrange("(c p) n -> p c n", p=128), tile_)


def phase_ssd(cx, l):
    TS, TT = cx.TS, cx.TT
    C = cx.C
    cx.phase_begin()
    cw = cx.sb([128, 6, 3]); cb = cx.sb([128, 6])
    LOAD(cx, cw[:], cx.inp["ssd_cwT"][l], cw)
    LOAD(cx, cb[:], cx.inp["ssd_cbT"][l], cb)
    dtb = cx.sb([128, 16]); nA = cx.sb([128, 16])
    LOAD(cx, dtb[:], cx.inp["ssd_dt_bias"][l:l + 1, :].partition_broadcast(128), dtb)
    LOAD(cx, nA[:], cx.inp["ssd_a_log"][l:l + 1, :].partition_broadcast(128), nA)
    ACT(cx, nA[:], nA[:], AF.Exp, [nA], [nA])
    TS_(cx, "dve", nA[:], nA[:], -1.0, None, ALU.mult, None, [nA], [nA])
    xbr = cx.ring(2, [128, 6, 514], F32, "xb")
    xcr = cx.ring(2, [128, 6, 512], F32, "xc")
    tr_ = cx.ring(2, [128, 512], F32, "t")
    xtr = cx.ring(2, [128, 640], F32, "xt")
    smr = cx.ring(6, [128, 16], F32, "sm")
    for (t0, n, s0, s1) in _segments(cx):
        xb = xbr.next()
        _load_halo(cx, xb, 6, cx.dr["xbcT"], t0, n, s0, s1)
        xc = xcr.next()
        for c in range(6):
            t = tr_.next()
            TS_(cx, "dve", t[:, 0:n], xb[:, c, 0:n], cw[:, c, 0:1], cb[:, c:c + 1], ALU.mult, ALU.add, [xb, cw, cb], [t])
            STT(cx, "dve", t[:, 0:n], xb[:, c, 1:n + 1], cw[:, c, 1:2], t[:, 0:n], ALU.mult, ALU.add, [xb, cw, t], [t])
            STT(cx, "dve", t[:, 0:n], xb[:, c, 2:n + 2], cw[:, c, 2:3], t[:, 0:n], ALU.mult, ALU.add, [xb, cw, t], [t])
            ACT(cx, xc[:, c, 0:n], t[:, 0:n], AF.Silu, [t], [xc])
        STORE(cx, cx.dr["bcT"][:, t0:t0 + n].rearrange("(c p) n -> p c n", p=128), xc[:, 4:6, 0:n], xc)
        for s in range(n // 128):
            psa = cx.psum()
            transposes(cx, [(psa[:, c * 128:(c + 1) * 128], xc[:, c, s * 128:(s + 1) * 128]) for c in range(4)], [xc], psa)
            psb = cx.psum()
            transposes(cx, [(psb[:, 0:128], xc[:, 4, s * 128:(s + 1) * 128])], [xc], psb)
            xt = xtr.next()
            COPY(cx, "act", xt[:, 0:512], psa[:], [psa], [xt])
            COPY(cx, "dve", xt[:, 512:640], psb[:, 0:128], [psb], [xt])
            STORE(cx, cx.dr["xs_tm"][t0 + s * 128:t0 + (s + 1) * 128, :], xt[:, 0:512], xt)
            STORE(cx, cx.dr["b_tm"][t0 + s * 128:t0 + (s + 1) * 128, :], xt[:, 512:640], xt)
            x0 = smr.next(); a0 = smr.next(); dt = smr.next()
            LOAD(cx, x0[:], cx.dr["sdt_tm"][t0 + s * 128:t0 + (s + 1) * 128, :], x0)
            TT_(cx, "dve", x0[:], x0[:], dtb[:], ALU.add, [x0, dtb], [x0])
            ACT(cx, a0[:], x0[:], AF.Abs, [x0], [a0])
            ACT(cx, a0[:], a0[:], AF.Exp, [a0], [a0], scale=-1.0)
            ACT(cx, a0[:], a0[:], AF.Ln, [a0], [a0], bias=C["one"][:, 0:1], scale=1.0)
            TS_(cx, "dve", x0[:], x0[:], 0.0, None, ALU.max, None, [x0], [x0])
            TT_(cx, "dve", dt[:], x0[:], a0[:], ALU.add, [x0, a0], [dt])
            TT_(cx, "dve", a0[:], dt[:], nA[:], ALU.mult, [dt, nA], [a0])
            STORE(cx, cx.dr["dt_tm"][t0 + s * 128:t0 + (s + 1) * 128, :], dt[:], dt)
            STORE(cx, cx.dr["ld_tm"][t0 + s * 128:t0 + (s + 1) * 128, :], a0[:], a0)
    cx.phase_end()
    import os
    if os.environ.get("SSD_STOP") == "1":
        return
    cx.phase_begin()
    S = [cx.sb([128, 256], F32, f"S{d}") for d in range(2)]
    bcr = cx.ring(3, [128, 2, 128], F32, "bc")
    btr = cx.ring(3, [128, 128], F32, "bt")
    xkr = cx.ring(3, [128, 512], F32, "xk")
    dlr = cx.ring(3, [128, 32], F32, "dl")
    rbr = cx.ring(2, [128, 8, 128], F32, "rb")
    gcr = cx.ring(2, [128, 48], F32, "gc")
    bsr = cx.ring(2, [128, 256], F32, "bs")
    der = cx.ring(3, [128, 128], F32, "de")
    atr = cx.ring(2, [128, 8, 128], F32, "at")
    tmr = cx.ring(4, [128, 512], F32, "tm")
    ones, ident = C["ones"], C["ident"]
    for (st, T, samp, p) in cx.seqs:
        NC = T // 128
        for d in range(2):
            if samp and not os.environ.get("SSD_NOSTATE"):
                for g in range(2):
                    LOAD(cx, S[d][g * 64:(g + 1) * 64, :].rearrange("n (h p) -> n h p", p=64),
                         cx.inp["state_ssd"][l, d, g * 4:(g + 1) * 4].rearrange("h n p -> n h p"), S[d])
            else:
                X(cx, "dve", "memset", S[d][:], 0.0, reads=[], writes=[S[d]])
        for step in range(NC):
            for d in range(2):
                ck = step if d == 0 else NC - 1 - step
                k0 = st + ck * 128
                CUM = C["cum128f"] if d == 0 else C["cum128b"]
                NEGM = C["negmf"] if d == 0 else C["negmb"]
                end = 127 if d == 0 else 0
                Sd = S[d]
                bc = bcr.next(); bt = btr.next(); xk = xkr.next(); dl = dlr.next()
                LOAD(cx, bc[:], cx.dr["bcT"][:, k0:k0 + 128].rearrange("(c p) n -> p c n", p=128), bc)
                LOAD(cx, bt[:], cx.dr["b_tm"][k0:k0 + 128, :], bt)
                LOAD(cx, xk[:], cx.dr["xs_tm"][k0:k0 + 128, :], xk)
                LOAD(cx, dl[:, 0:16], cx.dr["dt_tm"][k0:k0 + 128, :], dl)
                LOAD(cx, dl[:, 16:32], cx.dr["ld_tm"][k0:k0 + 128, :], dl)
                dtc = dl[:, d * 8:(d + 1) * 8]
                ldc = dl[:, 16 + d * 8:16 + (d + 1) * 8]
                rb = rbr.next()
                TT_(cx, "dve", rb[:], CUM[:].unsqueeze(1).broadcast_to([128, 8, 128]), ldc.unsqueeze(2).broadcast_to([128, 8, 128]),
                    ALU.mult, [CUM, dl], [rb])
                if int(os.environ.get("SSD_STEPS", 99)) <= 1:
                    continue
                pg = [cx.psum(), cx.psum()]
                for half in range(2):
                    mm_group(cx, pg[half][:], [(ones[:], rb[:, half * 4:(half + 1) * 4, :].rearrange("p h i -> p (h i)")), (ident[:], NEGM[:])],
                             [rb, ones, ident, NEGM], pg[half])
                if int(os.environ.get("SSD_STEPS", 99)) <= 2:
                    continue
                pc = cx.psum()
                X(cx, "pe", "matmul", pc[:, 0:8], lhsT=CUM[:], rhs=ldc, start=True, stop=True, reads=[CUM, dl], writes=[pc])
                gc = gcr.next()
                COPY(cx, "dve", gc[:, 0:8], pc[:, 0:8], [pc], [gc])
                TS_(cx, "dve", gc[:, 8:16], gc[:, 0:8], -1.0, None, ALU.mult, None, [gc], [gc])
                ACT(cx, gc[:, 16:24], gc[:, 0:8], AF.Exp, [gc], [gc])
                for half in range(2):
                    gend = pg[half][:, :].rearrange("p (h i) -> p h i", i=128)[:, :, end]
                    COPY(cx, "dve", gc[:, 24 + half * 4:28 + half * 4], gend, [pg[half]], [gc])
                ACT(cx, gc[:, 40:48], gc[:, 24:32], AF.Exp, [gc], [gc])
                TT_(cx, "dve", gc[:, 32:40], gc[:, 24:32], gc[:, 0:8], ALU.subtract, [gc], [gc])
                ACT(cx, gc[:, 32:40], gc[:, 32:40], AF.Exp, [gc], [gc])
                TT_(cx, "dve", gc[:, 32:40], gc[:, 32:40], dtc, ALU.mult, [gc, dl], [gc])
                if int(os.environ.get("SSD_STEPS", 99)) <= 3:
                    continue
                bs = bsr.next()
                for g in range(2):
                    pbc = cx.psum()
                    X(cx, "pe", "matmul", pbc[:, 0:128], lhsT=bc[g * 64:(g + 1) * 64, 0, :], rhs=bc[g * 64:(g + 1) * 64, 1, :],
                      start=True, stop=True, reads=[bc], writes=[pbc])
                    COPY(cx, "act", bs[:, g * 128:(g + 1) * 128], pbc[:, 0:128], [pbc], [bs])
                if int(os.environ.get("SSD_STEPS", 99)) <= 4:
                    continue
                at = atr.next()
                for h in range(8):
                    de = der.next()
                    ACT(cx, de[:], pg[h // 4][:, (h % 4) * 128:(h % 4 + 1) * 128], AF.Exp, [pg[h // 4], gc], [de], bias=gc[:, 8 + h:9 + h], scale=1.0)
                    STT(cx, "dve", at[:, h, :], de[:], dtc[:, h:h + 1], bs[:, (h // 4) * 128:(h // 4 + 1) * 128], ALU.mult, ALU.mult, [de, dl, bs], [at])
                if int(os.environ.get("SSD_STEPS", 99)) <= 5:
                    continue
                po = cx.psum()
                cx.fw.op("pe", _ssd_intra_fn(po, at, xk), reads=[at, xk], writes=[po])
                pis = []
                for g in range(2):
                    pi = cx.psum()
                    X(cx, "pe", "matmul", pi[:, 0:256], lhsT=bc[g * 64:(g + 1) * 64, 1, :], rhs=Sd[g * 64:(g + 1) * 64, :],
                      start=True, stop=True, reads=[bc, Sd], writes=[pi])
                    pis.append(pi)
                if int(os.environ.get("SSD_STEPS", 99)) <= 7:
                    continue
                t1 = tmr.next()
                for g in range(2):
                    TT_(cx, "dve", t1[:, g * 256:(g + 1) * 256].rearrange("p (h q) -> p h q", q=64), pis[g][:, 0:256].rearrange("p (h q) -> p h q", q=64),
                        gc[:, 16 + g * 4:20 + g * 4].unsqueeze(2).broadcast_to([128, 4, 64]), ALU.mult, [pis[g], gc], [t1])
                TT_(cx, "dve", t1[:], t1[:], po[:], ALU.add, [t1, po], [t1])
                STORE(cx, cx.dr["o_ssd_d"][d, k0:k0 + 128, :], t1[:], t1)
                if int(os.environ.get("SSD_STEPS", 99)) <= 8:
                    continue
                xs2 = tmr.next()
                TT_(cx, "dve", xs2[:].rearrange("p (h q) -> p h q", q=64), xk[:].rearrange("p (h q) -> p h q", q=64),
                    gc[:, 32:40].unsqueeze(2).broadcast_to([128, 8, 64]), ALU.mult, [xk, gc], [xs2])
                pst = cx.psum()
                X(cx, "pe", "matmul", pst[:], lhsT=bt[:], rhs=xs2[:], start=True, stop=True, reads=[bt, xs2], writes=[pst])
                for g in range(2):
                    Sg = Sd[g * 64:(g + 1) * 64, :]
                    TT_(cx, "dve", Sg.rearrange("n (h q) -> n h q", q=64), Sg.rearrange("n (h q) -> n h q", q=64),
                        gc[g * 64:(g + 1) * 64, 40 + g * 4:44 + g * 4].unsqueeze(2).broadcast_to([64, 4, 64]), ALU.mult, [Sd, gc], [Sd])
                    TT_(cx, "dve", Sg, Sg, pst[g * 64:(g + 1) * 64, g * 256:(g + 1) * 256], ALU.add, [Sd, pst], [Sd])
        if not samp and not os.environ.get("SSD_NOSTATE"):
            for d in range(2):
                for g in range(2):
                    STORE(cx, cx.out["new_ssd"][p, l, d, g * 4:(g + 1) * 4].rearrange("h n p -> n h p"),
                          S[d][g * 64:(g + 1) * 64, :].rearrange("n (h p) -> n h p", p=64), S[d])
    cx.phase_end()
    if os.environ.get("SSD_STOP") == "2":
        return
    cx.phase_begin()
    Db = cx.sb([128, 8]); nb = cx.sb([128, 512])
    LOAD(cx, Db[:], cx.inp["ssd_d"][l:l + 1, :].partition_broadcast(128), Db)
    LOAD(cx, nb[:], cx.inp["ssd_norm"][l:l + 1, :].partition_broadcast(128), nb)
    inr = cx.ring(8, [128, 512], F32, "in")
    ssr = cx.ring(4, [128, 1], F32, "ss")
    for b0 in range(0, TT, 128):
        of = inr.next(); ob = inr.next(); xs = inr.next(); sz = inr.next()
        LOAD(cx, of[:], cx.dr["o_ssd_d"][0, b0:b0 + 128, :], of)
        LOAD(cx, ob[:], cx.dr["o_ssd_d"][1, b0:b0 + 128, :], ob)
        LOAD(cx, xs[:], cx.dr["xs_tm"][b0:b0 + 128, :], xs)
        LOAD(cx, sz[:], cx.dr["sz_tm"][b0:b0 + 128, :], sz)
        TT_(cx, "dve", of[:], of[:], ob[:], ALU.add, [of, ob], [of])
        TT_(cx, "dve", xs[:].rearrange("p (h q) -> p h q", q=64), xs[:].rearrange("p (h q) -> p h q", q=64),
            Db[:].unsqueeze(2).broadcast_to([128, 8, 64]), ALU.mult, [xs, Db], [xs])
        TT_(cx, "dve", of[:], of[:], xs[:], ALU.add, [of, xs], [of])
        ACT(cx, sz[:], sz[:], AF.Silu, [sz], [sz])
        TT_(cx, "dve", of[:], of[:], sz[:], ALU.mult, [of, sz], [of])
        ss = ssr.next()
        ACT(cx, ob[:], of[:], AF.Square, [of], [ob, ss], accum_out=ss[:, 0:1])
        ACT(cx, ss[:], ss[:], AF.Sqrt, [ss], [ss], bias=C["eps6"][:, 0:1], scale=1.0 / 512)
        X(cx, "dve", "reciprocal", ss[:], ss[:], reads=[ss], writes=[ss])
        STT(cx, "dve", of[:], of[:], ss[:, 0:1], nb[:], ALU.mult, ALU.mult, [of, ss, nb], [of])
        STORE(cx, cx.dr["o_ssd"][b0:b0 + 128, :], of[:], of)
    cx.phase_end()


def _ssd_intra_fn(po, at, xk):
    def fn(e):
        ins = None
        for h in range(8):
            ins = e.matmul(po[:, h * 64:(h + 1) * 64], lhsT=at[:, h, :], rhs=xk[:, h * 64:(h + 1) * 64], start=True, stop=True)
        return ins
    return fn


def phase_gla(cx, l):
    TS, TT = cx.TS, cx.TT
    C = cx.C
    cx.phase_begin()
    g2 = cx.sb([16, 2, 256]); gb = cx.sb([128, 2, 256])
    LOAD(cx, g2[:], cx.inp["gla_g2"][l].rearrange("z r k -> r z k"), g2)
    for d in range(2):
        LOAD(cx, gb[:, d, :], cx.inp["gla_gb"][l, d:d + 1, :].partition_broadcast(128), gb)
    S = [[cx.sb([128, 128], F32, f"S{d}{pr}") for pr in range(2)] for d in range(2)]
    qkr = cx.ring(3, [128, 4, 128], F32, "qk")
    gkr = cx.ring(3, [128, 256], F32, "gk")
    gvr = cx.ring(3, [128, 512], F32, "gv")
    ggr = cx.ring(3, [16, 128], F32, "gg")
    t2r = cx.ring(8, [128, 256], F32, "t2")
    atr = cx.ring(2, [128, 4, 128], F32, "at")
    osr = cx.ring(2, [128, 512], F32, "os")
    for (st, T, samp, p) in cx.seqs:
        NC = T // 128
        for d in range(2):
            for pr in range(2):
                if samp:
                    LOAD(cx, S[d][pr][:], cx.inp["state_gla"][l, d, 2 * pr:2 * pr + 2].rearrange("h k v -> (h k) v"), S[d][pr])
                else:
                    X(cx, "dve", "memset", S[d][pr][:], 0.0, reads=[], writes=[S[d][pr]])
        for step in range(NC):
            for d in range(2):
                ck = step if d == 0 else NC - 1 - step
                k0 = st + ck * 128
                CUM = C["cum128f"] if d == 0 else C["cum128b"]
                SFX = C["sfx128f"] if d == 0 else C["sfx128b"]
                end = 127 if d == 0 else 0
                qk = qkr.next(); gk = gkr.next(); gv = gvr.next(); gg = ggr.next()
                LOAD(cx, qk[:], cx.dr["gqkT"][:, k0:k0 + 128].rearrange("(c p) n -> p c n", p=128), qk)
                LOAD(cx, gk[:], cx.dr["gk_tm"][k0:k0 + 128, :], gk)
                LOAD(cx, gv[:], cx.dr["gv_tm"][k0:k0 + 128, :], gv)
                LOAD(cx, gg[:], cx.dr["gglT"][:, k0:k0 + 128], gg)
                pl = cx.psum()
                X(cx, "pe", "matmul", pl[:, 0:256], lhsT=gg[:], rhs=g2[:, d, :], start=True, stop=True, reads=[gg, g2], writes=[pl])
                lg = t2r.next(); ab = t2r.next()
                TT_(cx, "dve", lg[:], pl[:, 0:256], gb[:, d, :], ALU.add, [pl, gb], [lg])
                ACT(cx, ab[:], lg[:], AF.Abs, [lg], [ab])
                ACT(cx, ab[:], ab[:], AF.Exp, [ab], [ab], scale=-1.0)
                ACT(cx, ab[:], ab[:], AF.Ln, [ab], [ab], bias=C["one"][:, 0:1], scale=1.0)
                TS_(cx, "dve", lg[:], lg[:], 0.0, None, ALU.min, None, [lg], [lg])
                TT_(cx, "dve", lg[:], lg[:], ab[:], ALU.subtract, [lg, ab], [lg])
                TS_(cx, "dve", lg[:], lg[:], 1.0 / 16.0, None, ALU.mult, None, [lg], [lg])
                pg = cx.psum()
                for pr in range(2):
                    X(cx, "pe", "matmul", pg[:, pr * 128:(pr + 1) * 128], lhsT=lg[:, pr * 128:(pr + 1) * 128], rhs=CUM[:], start=True, stop=True,
                      reads=[lg, CUM], writes=[pg])
                E = t2r.next(); Ei = t2r.next()
                ACT(cx, E[:], pg[:, 0:256], AF.Exp, [pg], [E])
                ACT(cx, Ei[:], pg[:, 0:256], AF.Exp, [pg], [Ei], scale=-1.0)
                qt = t2r.next(); kt = t2r.next()
                STT(cx, "dve", qt[:], qk[:, 0:2, :].rearrange("p c n -> p (c n)"), 0.125, E[:], ALU.mult, ALU.mult, [qk, E], [qt])
                TT_(cx, "dve", kt[:], qk[:, 2:4, :].rearrange("p c n -> p (c n)"), Ei[:], ALU.mult, [qk, Ei], [kt])
                pd = cx.psum()
                X(cx, "pe", "matmul", pd[:, 0:256], lhsT=SFX[:], rhs=lg[:], start=True, stop=True, reads=[SFX, lg], writes=[pd])
                kd = t2r.next()
                ACT(cx, kd[:], pd[:, 0:256], AF.Exp, [pd], [kd])
                TT_(cx, "dve", kd[:], kd[:], gk[:], ALU.mult, [kd, gk], [kd])
                at = atr.next()
                po = cx.psum()
                for h in range(4):
                    pr, base = h // 2, (h % 2) * 64
                    pa = cx.psum()
                    X(cx, "pe", "matmul", pa[:, 0:128], lhsT=kt[base:base + 64, pr * 128:(pr + 1) * 128], rhs=qt[base:base + 64, pr * 128:(pr + 1) * 128],
                      start=True, stop=True, reads=[kt, qt], writes=[pa])
                    TT_(cx, "dve", at[:, h, :], pa[:, 0:128], CUM[:], ALU.mult, [pa, CUM], [at])
                    mm_group(cx, po[:, h * 128:(h + 1) * 128],
                             [(at[:, h, :], gv[:, h * 128:(h + 1) * 128]),
                              (qt[base:base + 64, pr * 128:(pr + 1) * 128], S[d][pr][base:base + 64, :])], [at, gv, qt, S[d][pr]], po)
                osb = osr.next()
                COPY(cx, "act", osb[:], po[:], [po], [osb])
                STORE(cx, cx.dr["o_gla_d"][d, k0:k0 + 128, :], osb[:], osb)
                pst = cx.psum()
                for pr in range(2):
                    for hl in range(2):
                        h = 2 * pr + hl
                        X(cx, "pe", "matmul", pst[:, h * 128:(h + 1) * 128], lhsT=kd[:, pr * 128:(pr + 1) * 128], rhs=gv[:, h * 128:(h + 1) * 128],
                          start=True, stop=True, reads=[kd, gv], writes=[pst])
                for pr in range(2):
                    for hl in range(2):
                        h = 2 * pr + hl
                        rows = slice(hl * 64, (hl + 1) * 64)
                        STT(cx, "dve", S[d][pr][rows, :], S[d][pr][rows, :], E[rows, pr * 128 + end:pr * 128 + end + 1], pst[rows, h * 128:(h + 1) * 128],
                            ALU.mult, ALU.add, [S[d][pr], E, pst], [S[d][pr]])
        if not samp:
            for d in range(2):
                for pr in range(2):
                    STORE(cx, cx.out["new_gla"][p, l, d, 2 * pr:2 * pr + 2].rearrange("h k v -> (h k) v"), S[d][pr][:], S[d][pr])
    cx.phase_end()
    cx.phase_begin()
    nb = cx.sb([128, 128])
    LOAD(cx, nb[:], cx.inp["gla_norm"][l:l + 1, :].partition_broadcast(128), nb)
    inr = cx.ring(8, [128, 512], F32, "in")
    ssr = cx.ring(4, [128, 4], F32, "ss")
    for b0 in range(0, TT, 128):
        of = inr.next(); ob = inr.next(); go = inr.next()
        LOAD(cx, of[:], cx.dr["o_gla_d"][0, b0:b0 + 128, :], of)
        LOAD(cx, ob[:], cx.dr["o_gla_d"][1, b0:b0 + 128, :], ob)
        LOAD(cx, go[:], cx.dr["gog_tm"][b0:b0 + 128, :], go)
        TT_(cx, "dve", of[:], of[:], ob[:], ALU.add, [of, ob], [of])
        ACT(cx, ob[:], of[:], AF.Square, [of], [ob])
        ss = ssr.next()
        X(cx, "dve", "tensor_reduce", ss[:], ob[:].rearrange("p (h q) -> p h q", q=128), AX.X, ALU.add, reads=[ob], writes=[ss])
        ACT(cx, ss[:], ss[:], AF.Sqrt, [ss], [ss], bias=C["eps6"][:, 0:1], scale=1.0 / 128)
        X(cx, "dve", "reciprocal", ss[:], ss[:], reads=[ss], writes=[ss])
        o3 = of[:].rearrange("p (h q) -> p h q", q=128)
        TT_(cx, "dve", o3, o3, ss[:].unsqueeze(2).broadcast_to([128, 4, 128]), ALU.mult, [of, ss], [of])
        TT_(cx, "dve", o3, o3, nb[:].unsqueeze(1).broadcast_to([128, 4, 128]), ALU.mult, [of, nb], [of])
        ACT(cx, go[:], go[:], AF.Silu, [go], [go])
        TT_(cx, "dve", of[:], of[:], go[:], ALU.mult, [of, go], [of])
        STORE(cx, cx.dr["o_gla"][b0:b0 + 128, :], of[:], of)
    cx.phase_end()


def phase_rwkv(cx, l):
    TS, TT = cx.TS, cx.TT
    C = cx.C
    fw = cx.fw
    cx.phase_begin()
    mu = cx.sb([128, 14]); om = cx.sb([128, 14]); hm = cx.sb([128, 14])
    LOAD(cx, mu[:], cx.inp["rw_muT"][l], mu)
    TS_(cx, "dve", om[:], mu[:], -1.0, 1.0, ALU.mult, ALU.add, [mu], [om])
    TS_(cx, "dve", hm[:], mu[:], 0.5, None, ALU.mult, None, [mu], [hm])
    pp = cx.sb([128, 5, 4])
    for i, nm in enumerate(("rw_a0T", "rw_kkT", "rw_kaT", "rw_rkT")):
        LOAD(cx, pp[:, i, :], cx.inp[nm][l], pp)
    TS_(cx, "dve", pp[:, 4, :], pp[:, 2, :], -1.0, 1.0, ALU.mult, ALU.add, [pp], [pp])
    w0b = cx.sb([128, 2, 512])
    for d in range(2):
        LOAD(cx, w0b[:, d, :], cx.inp["rw_w0"][l, d:d + 1, :].partition_broadcast(128), w0b)
    w2s = cx.sb([64, 2, 512])
    LOAD(cx, w2s[:], cx.inp["rw_w2"][l].rearrange("z r c -> r z c"), w2s)
    a2s = cx.sb([128, 512])
    LOAD(cx, a2s[64:128, :], cx.inp["rw_a2"][l], a2s)
    g2s = cx.sb([128, 512])
    LOAD(cx, g2s[:], cx.inp["rw_g2"][l], g2s)
    rbr = cx.ring(1, [128, 14, 514], F32, "rb")
    rs = cx.sb([128, 14, 512], F32, "rs")
    t5r = cx.ring(3, [128, 512], F32, "t5")
    aT = cx.sb([128, 4, 512], F32, "aT")
    kkn = cx.sb([128, 4, 512], F32, "kkn")
    kp = cx.sb([128, 4, 512], F32, "kp")
    bet = cx.sb([128, 4, 512], F32, "bet")
    big = cx.ring(2, [128, 4, 512], F32, "big")
    twl = cx.sb([64, 512], F32, "twl")
    sgl = cx.sb([128, 512], F32, "sgl")
    o5r = cx.ring(4, [128, 512], F32, "o5")
    s8r = cx.ring(2, [128, 8], F32, "s8")
    bones = C["bones"]
    for (t0, n, s0, s1) in _segments(cx):
        rb = rbr.next()
        _load_halo(cx, rb, 14, cx.dr["rblkT"], t0, n, s0, s1)
        for c in range(14):
            t = t5r.next()
            TT_(cx, "dve", t[:, 0:n], rb[:, c, 0:n], rb[:, c, 2:n + 2], ALU.add, [rb], [t])
            TS_(cx, "dve", rs[:, c, 0:n], rb[:, c, 1:n + 1], om[:, c:c + 1], None, ALU.mult, None, [rb, om], [rs])
            STT(cx, "dve", rs[:, c, 0:n], t[:, 0:n], hm[:, c:c + 1], rs[:, c, 0:n], ALU.mult, ALU.add, [t, hm, rs], [rs])
        for c4 in range(4):
            ps = cx.psum()
            X(cx, "pe", "matmul", ps[:, 0:n], lhsT=a2s[64:128, c4 * 128:(c4 + 1) * 128], rhs=rs[64:128, 12, 0:n], start=True, stop=True,
              reads=[a2s, rs], writes=[ps])
            ACT(cx, aT[:, c4, 0:n], ps[:, 0:n], AF.Sigmoid, [ps, pp], [aT], bias=pp[:, 0, c4:c4 + 1], scale=1.0)
        kr = big.next()
        TT_(cx, "dve", kr[:, :, 0:n], rs[:, 4:8, 0:n], pp[:, 1, :].unsqueeze(2).broadcast_to([128, 4, n]), ALU.mult, [rs, pp], [kr])
        for c4 in range(4):
            sq = t5r.next()
            ACT(cx, sq[:, 0:n], kr[:, c4, 0:n], AF.Square, [kr], [sq])
            ps = cx.psum()
            X(cx, "pe", "matmul", ps[:, 0:n], lhsT=bones[:], rhs=sq[:, 0:n], start=True, stop=True, reads=[sq, bones], writes=[ps])
            sd = t5r.next()
            ACT(cx, sd[:, 0:n], ps[:, 0:n], AF.Sqrt, [ps], [sd], bias=C["eps12"][:, 0:1], scale=1.0)
            X(cx, "dve", "reciprocal", sd[:, 0:n], sd[:, 0:n], reads=[sd], writes=[sd])
            TT_(cx, "dve", kkn[:, c4, 0:n], kr[:, c4, 0:n], sd[:, 0:n], ALU.mult, [kr, sd], [kkn])
        tb = big.next()
        TT_(cx, "dve", tb[:, :, 0:n], aT[:, :, 0:n], pp[:, 2, :].unsqueeze(2).broadcast_to([128, 4, n]), ALU.mult, [aT, pp], [tb])
        TT_(cx, "dve", tb[:, :, 0:n], tb[:, :, 0:n], pp[:, 4, :].unsqueeze(2).broadcast_to([128, 4, n]), ALU.add, [tb, pp], [tb])
        TT_(cx, "dve", kp[:, :, 0:n], rs[:, 4:8, 0:n], tb[:, :, 0:n], ALU.mult, [rs, tb], [kp])
        TT_(cx, "dve", bet[:, :, 0:n], kkn[:, :, 0:n], aT[:, :, 0:n], ALU.mult, [kkn, aT], [bet])
        for q, src in enumerate((None, kp, kkn, bet)):
            sap = rs[:, 0:4, 0:n] if src is None else src[:, :, 0:n]
            STORE(cx, cx.dr["rw_fm"][q, :, t0:t0 + n].rearrange("(c p) n -> p c n", p=128), sap, rs if src is None else src)
        pr_ = big.next()
        TT_(cx, "dve", pr_[:, :, 0:n], rs[:, 0:4, 0:n], kp[:, :, 0:n], ALU.mult, [rs, kp], [pr_])
        TT_(cx, "dve", pr_[:, :, 0:n], pr_[:, :, 0:n], pp[:, 3, :].unsqueeze(2).broadcast_to([128, 4, n]), ALU.mult, [pr_, pp], [pr_])
        ACT(cx, twl[:, 0:n], rs[0:64, 12, 0:n], AF.Tanh, [rs], [twl])
        ACT(cx, sgl[:, 0:n], rs[:, 13, 0:n], AF.Sigmoid, [rs], [sgl])
        for s in range(n // 128):
            b0 = t0 + s * 128
            sl = slice(s * 128, (s + 1) * 128)
            ps = cx.psum()
            for c4 in range(4):
                X(cx, "pe", "matmul", ps[:, c4 * 2:(c4 + 1) * 2], lhsT=pr_[:, c4, sl], rhs=C["ind2"][:], start=True, stop=True,
                  reads=[pr_, C["ind2"]], writes=[ps])
            s8 = s8r.next()
            COPY(cx, "dve", s8[:], ps[:, 0:8], [ps], [s8])
            STORE(cx, cx.dr["rw_s_tm"][b0:b0 + 128, :], s8[:], s8)
            pv = cx.psum()
            transposes(cx, [(pv[:, c4 * 128:(c4 + 1) * 128], rs[:, 8 + c4, sl]) for c4 in range(4)], [rs], pv)
            o5 = o5r.next()
            COPY(cx, "act", o5[:], pv[:], [pv], [o5])
            STORE(cx, cx.dr["rw_v_tm"][b0:b0 + 128, :], o5[:], o5)
            pgm = cx.psum()
            X(cx, "pe", "matmul", pgm[:], lhsT=sgl[:, sl], rhs=g2s[:], start=True, stop=True, reads=[sgl, g2s], writes=[pgm])
            o5 = o5r.next()
            COPY(cx, "act", o5[:], pgm[:], [pgm], [o5])
            STORE(cx, cx.dr["rw_g_tm"][b0:b0 + 128, :], o5[:], o5)
            for d in range(2):
                pw = cx.psum()
                X(cx, "pe", "matmul", pw[:], lhsT=twl[:, sl], rhs=w2s[:, d, :], start=True, stop=True, reads=[twl, w2s], writes=[pw])
                o5 = o5r.next()
                TT_(cx, "dve", o5[:], pw[:], w0b[:, d, :], ALU.add, [pw, w0b], [o5])
                ACT(cx, o5[:], o5[:], AF.Sigmoid, [o5], [o5])
                TS_(cx, "dve", o5[:], o5[:], -0.6065306597126334, None, ALU.mult, None, [o5], [o5])
                STORE(cx, cx.dr["rw_lw_tm"][d, b0:b0 + 128, :], o5[:], o5)
    cx.phase_end()
    cx.phase_begin()
    NU = 8
    S0 = [[cx.sb([128, 64], F32, f"S{d}{pr}") for pr in range(4)] for d in range(2)]
    U = []
    for u in range(NU):
        t = {}
        for nm, shp in (("AR", [128, 256]), ("Bb", [128, 128]), ("Kb", [128, 128]), ("A1", [128, 256]), ("A2", [128, 256]), ("Mq", [128, 128]),
                        ("PQ0", [128, 256]), ("PQ1", [128, 256]), ("X0", [128, 64]), ("X1", [128, 64]), ("EE", [128, 128]), ("E2", [128, 64]),
                        ("KT", [128, 128]), ("BT", [128, 128]), ("tmp", [128, 64])):
            t[nm] = cx.sb(shp, F32, nm)
        for nm in ("AR", "Bb", "Kb"):
            X(cx, "dve", "memset", t[nm][:], 0.0, reads=[], writes=[t[nm]])
        U.append(t)
    x4r = [cx.ring(2, [128, 4, 4, 64], F32, "x4") for d in range(2)]
    lwr = [cx.ring(2, [64, 512], F32, "lw") for d in range(2)]
    vsr = [cx.ring(2, [128, 4, 64], F32, "vs") for d in range(2)]
    osr = [cx.ring(2, [128, 4, 64], F32, "os") for d in range(2)]
    ident = C["ident"]

    def unit(d, pr, T_, x4, lw, vs, osb, end):
        CUMc = C["cum64f"] if d == 0 else C["cum64b"]
        MSK = C["msk64f"] if d == 0 else C["msk64b"]
        MST = C["mst64f"] if d == 0 else C["mst64b"]
        S = S0[d][pr]
        AR, Bb, Kb = T_["AR"], T_["Bb"], T_["Kb"]
        EE, E2 = T_["EE"], T_["E2"]
        pG = cx.psum()
        X(cx, "pe", "matmul", pG[:, 0:128], lhsT=lw[:, pr * 128:(pr + 1) * 128], rhs=CUMc[:], start=True, stop=True, reads=[lw, CUMc], writes=[pG])
        ACT(cx, EE[:], pG[:, 0:128], AF.Exp, [pG], [EE])
        ACT(cx, E2[:], pG[:, 0:64], AF.Exp, [pG], [E2], scale=-1.0)
        yield
        for hl in range(2):
            r_ = slice(hl * 64, (hl + 1) * 64)
            c_ = slice(hl * 64, (hl + 1) * 64)
            c2 = slice(128 + hl * 64, 128 + (hl + 1) * 64)
            STT(cx, "dve", AR[r_, c_], x4[r_, 2, pr, :], -1.0, EE[r_, 64:128], ALU.mult, ALU.mult, [x4, EE], [AR])
            TT_(cx, "dve", AR[r_, c2], x4[r_, 0, pr, :], EE[r_, 0:64], ALU.mult, [x4, EE], [AR])
            TT_(cx, "dve", Bb[r_, c_], x4[r_, 3, pr, :], E2[r_, :], ALU.mult, [x4, E2], [Bb])
            TT_(cx, "dve", Kb[r_, c_], x4[r_, 1, pr, :], E2[r_, :], ALU.mult, [x4, E2], [Kb])
        yield
        p1 = cx.psum()
        X(cx, "pe", "matmul", p1[:, 0:256], lhsT=Bb[:], rhs=AR[:], start=True, stop=True, reads=[Bb, AR], writes=[p1])
        p2 = cx.psum()
        X(cx, "pe", "matmul", p2[:, 0:256], lhsT=Kb[:], rhs=AR[:], start=True, stop=True, reads=[Kb, AR], writes=[p2])
        p3 = cx.psum()
        X(cx, "pe", "matmul", p3[:, 0:128], lhsT=AR[:, 0:128], rhs=Bb[:], start=True, stop=True, reads=[Bb, AR], writes=[p3])
        A1, A2, Mq = T_["A1"], T_["A2"], T_["Mq"]
        TT_(cx, "dve", A1[:], p1[:, 0:256], MSK[:], ALU.mult, [p1, MSK], [A1])
        TT_(cx, "dve", A2[:], p2[:, 0:256], MSK[:], ALU.mult, [p2, MSK], [A2])
        TT_(cx, "dve", Mq[:], p3[:, 0:128], MST[:], ALU.mult, [p3, MST], [Mq])
        yield
        pr_ = cx.psum()
        mm_group(cx, pr_[:, 0:64], [(AR[:, 0:128], S[:]), (A2[:, 0:128], vs[:, pr, :])], [AR, S, A2, vs], pr_)
        Xc = T_["X0"]
        COPY(cx, "act", Xc[:], pr_[:, 0:64], [pr_], [Xc])
        yield
        P_ap, Q_ap = A1[:, 0:128], Mq[:]
        P_t, Q_t = A1, Mq
        PQ = [T_["PQ0"], T_["PQ1"]]
        Xn = T_["X1"]
        for lev in range(6):
            px = cx.psum()
            mm_group(cx, px[:, 0:64], [(ident[:], Xc[:]), (P_ap, Xc[:])], [ident, Xc, P_t], px)
            COPY(cx, "act", Xn[:], px[:, 0:64], [px], [Xn])
            Xc, Xn = Xn, Xc
            if lev < 5:
                pp_ = cx.psum()
                X(cx, "pe", "matmul", pp_[:, 0:128], lhsT=Q_ap, rhs=P_ap, start=True, stop=True, reads=[P_t, Q_t], writes=[pp_])
                X(cx, "pe", "matmul", pp_[:, 128:256], lhsT=P_ap, rhs=Q_ap, start=True, stop=True, reads=[P_t, Q_t], writes=[pp_])
                nt = PQ[lev % 2]
                COPY(cx, "dve", nt[:], pp_[:, 0:256], [pp_], [nt])
                P_ap, Q_ap = nt[:, 0:128], nt[:, 128:256]
                P_t = Q_t = nt
            yield
        Uc = Xc
        po = cx.psum()
        mm_group(cx, po[:, 0:64], [(AR[:, 128:256], S[:]), (A2[:, 128:256], vs[:, pr, :]), (A1[:, 128:256], Uc[:])], [AR, S, A2, vs, A1, Uc], po)
        COPY(cx, "act", osb[:, pr, :], po[:, 0:64], [po], [osb])
        yield
        KT, BT = T_["KT"], T_["BT"]
        pk = cx.psum()
        transposes(cx, [(pk[:, 0:128], Kb[:]), (pk[:, 128:256], Bb[:])], [Kb, Bb], pk)
        COPY(cx, "act", KT[:], pk[:, 0:128], [pk], [KT])
        COPY(cx, "act", BT[:], pk[:, 128:256], [pk], [BT])
        pst = cx.psum()
        mm_group(cx, pst[:, 0:64], [(KT[:], vs[:, pr, :]), (BT[:], Uc[:])], [KT, BT, vs, Uc], pst)
        tmp = T_["tmp"]
        TT_(cx, "dve", tmp[:], pst[:, 0:64], S[:], ALU.add, [pst, S], [tmp])
        TS_(cx, "dve", S[:], tmp[:], EE[:, end:end + 1], None, ALU.mult, None, [tmp, EE], [S])
        yield

    for (st, T, samp, p) in cx.seqs:
        NC = T // 64
        for d in range(2):
            for pr in range(4):
                if samp:
                    LOAD(cx, S0[d][pr][:], cx.inp["state_rwkvT"][l, d, 2 * pr:2 * pr + 2].rearrange("h k v -> (h k) v"), S0[d][pr])
                else:
                    X(cx, "dve", "memset", S0[d][pr][:], 0.0, reads=[], writes=[S0[d][pr]])
        for step in range(NC):
            gens = []
            stores = []
            for d in range(2):
                ck = step if d == 0 else NC - 1 - step
                k0 = st + ck * 64
                end = 63 if d == 0 else 0
                x4 = x4r[d].next(); lw = lwr[d].next(); vs = vsr[d].next(); osb = osr[d].next()
                for q_ in range(4):
                    LOAD(cx, x4[:, q_, :, :], cx.dr["rw_fm"][q_, :, k0:k0 + 64].rearrange("(c p) n -> p c n", p=128), x4)
                LOAD(cx, lw[:], cx.dr["rw_lw_tm"][d, k0:k0 + 64, :], lw)
                for hl in range(2):
                    LOAD(cx, vs[hl * 64:(hl + 1) * 64, :, :],
                         cx.dr["rw_v_tm"][k0:k0 + 64, :].rearrange("t (pr hl v) -> t hl pr v", hl=2, v=64)[:, hl], vs)
                for pr in range(4):
                    gens.append(unit(d, pr, U[d * 4 + pr], x4, lw, vs, osb, end))
                stores.append((d, k0, osb))
            while gens:
                for g in list(gens):
                    try:
                        next(g)
                    except StopIteration:
                        gens.remove(g)
            for (d, k0, osb) in stores:
                for hl in range(2):
                    STORE(cx, cx.dr["o_rw_d"][d, k0:k0 + 64, :].rearrange("t (pr hl v) -> t hl pr v", hl=2, v=64)[:, hl],
                          osb[hl * 64:(hl + 1) * 64, :, :], osb)
        if not samp:
            for d in range(2):
                for pr in range(4):
                    STORE(cx, cx.out["new_rwkvT"][p, l, d, 2 * pr:2 * pr + 2].rearrange("h k v -> (h k) v"), S0[d][pr][:], S0[d][pr])
    cx.phase_end()
    cx.phase_begin()
    lg = cx.sb([128, 512]); lb = cx.sb([128, 512])
    LOAD(cx, lg[:], cx.inp["rw_ln_g"][l:l + 1, :].partition_broadcast(128), lg)
    LOAD(cx, lb[:], cx.inp["rw_ln_b"][l:l + 1, :].partition_broadcast(128), lb)
    inr = cx.ring(10, [128, 512], F32, "in")
    ssr = cx.ring(6, [128, 8], F32, "ss")
    for b0 in range(0, TT, 128):
        of = inr.next(); ob = inr.next(); vt = inr.next(); gt = inr.next(); sq = inr.next()
        s8 = ssr.next(); mean = ssr.next(); var = ssr.next()
        LOAD(cx, of[:], cx.dr["o_rw_d"][0, b0:b0 + 128, :], of)
        LOAD(cx, ob[:], cx.dr["o_rw_d"][1, b0:b0 + 128, :], ob)
        LOAD(cx, vt[:], cx.dr["rw_v_tm"][b0:b0 + 128, :], vt)
        LOAD(cx, gt[:], cx.dr["rw_g_tm"][b0:b0 + 128, :], gt)
        LOAD(cx, s8[:], cx.dr["rw_s_tm"][b0:b0 + 128, :], s8)
        TT_(cx, "dve", of[:], of[:], ob[:], ALU.add, [of, ob], [of])
        o3 = of[:].rearrange("p (h q) -> p h q", q=64)
        X(cx, "dve", "tensor_reduce", mean[:], o3, AX.X, ALU.add, reads=[of], writes=[mean])
        TS_(cx, "dve", mean[:], mean[:], 1.0 / 64, None, ALU.mult, None, [mean], [mean])
        TT_(cx, "dve", o3, o3, mean[:].unsqueeze(2).broadcast_to([128, 8, 64]), ALU.subtract, [of, mean], [of])
        ACT(cx, sq[:], of[:], AF.Square, [of], [sq])
        X(cx, "dve", "tensor_reduce", var[:], sq[:].rearrange("p (h q) -> p h q", q=64), AX.X, ALU.add, reads=[sq], writes=[var])
        ACT(cx, var[:], var[:], AF.Sqrt, [var], [var], bias=C["epsln"][:, 0:1], scale=1.0 / 64)
        X(cx, "dve", "reciprocal", var[:], var[:], reads=[var], writes=[var])
        TT_(cx, "dve", o3, o3, var[:].unsqueeze(2).broadcast_to([128, 8, 64]), ALU.mult, [of, var], [of])
        TT_(cx, "dve", of[:], of[:], lg[:], ALU.mult, [of, lg], [of])
        TT_(cx, "dve", of[:], of[:], lb[:], ALU.add, [of, lb], [of])
        v3 = vt[:].rearrange("p (h q) -> p h q", q=64)
        TT_(cx, "dve", v3, v3, s8[:].unsqueeze(2).broadcast_to([128, 8, 64]), ALU.mult, [vt, s8], [vt])
        TT_(cx, "dve", of[:], of[:], vt[:], ALU.add, [of, vt], [of])
        TT_(cx, "dve", of[:], of[:], gt[:], ALU.mult, [of, gt], [of])
        STORE(cx, cx.dr["o_rwkv"][b0:b0 + 128, :], of[:], of)
    cx.phase_end()


def run_mixers(cx, l):
    phase_attn(cx, l)
    phase_ssd(cx, l)
    phase_gla(cx, l)
    phase_rwkv(cx, l)


CONST_SHAPES = {"ident": [128, 128], "ones": [128, 128], "bones": [128, 128], "eps6": [128, 1], "one": [128, 1], "eps12": [128, 1],
                "epsln": [128, 1], "ind2": [128, 2], "cum128f": [128, 128], "cum128b": [128, 128], "sfx128f": [128, 128], "sfx128b": [128, 128],
                "negmf": [128, 512], "negmb": [128, 512], "cum64f": [64, 128], "cum64b": [64, 128], "msk64f": [128, 256], "msk64b": [128, 256],
                "mst64f": [128, 128], "mst64b": [128, 128]}


def host_consts():
    f32 = np.float32
    d = {}
    d["ident"] = np.eye(128, dtype=f32)
    d["ones"] = np.ones((128, 128), f32)
    bo = np.zeros((128, 128), f32)
    bo[:64, :64] = 1
    bo[64:, 64:] = 1
    d["bones"] = bo
    d["eps6"] = np.full((128, 1), 1e-6, f32)
    d["one"] = np.full((128, 1), 1.0, f32)
    d["eps12"] = np.full((128, 1), 1e-12, f32)
    d["epsln"] = np.full((128, 1), 64e-5, f32)
    ind = np.zeros((128, 2), f32)
    ind[:64, 0] = 1
    ind[64:, 1] = 1
    d["ind2"] = ind
    a = np.arange(128)
    le = (a[:, None] <= a[None, :]).astype(f32)
    ge = (a[:, None] >= a[None, :]).astype(f32)
    gt = (a[:, None] > a[None, :]).astype(f32)
    lt = (a[:, None] < a[None, :]).astype(f32)
    d["cum128f"], d["cum128b"] = le, ge
    d["sfx128f"], d["sfx128b"] = gt, lt
    d["negmf"] = np.tile((le - 1.0) * 1e30, (1, 4)).astype(f32)
    d["negmb"] = np.tile((ge - 1.0) * 1e30, (1, 4)).astype(f32)
    b = np.arange(64)
    le6 = (b[:, None] <= b[None, :]).astype(f32)
    ge6 = (b[:, None] >= b[None, :]).astype(f32)
    gt6 = (b[:, None] > b[None, :]).astype(f32)
    lt6 = (b[:, None] < b[None, :]).astype(f32)
    d["cum64f"] = np.concatenate([le6, lt6], axis=1)
    d["cum64b"] = np.concatenate([ge6, gt6], axis=1)
    t22 = lambda m: np.tile(m, (2, 2))
    d["msk64f"] = np.concatenate([t22(lt6), t22(le6)], axis=1)
    d["msk64b"] = np.concatenate([t22(gt6), t22(ge6)], axis=1)
    d["mst64f"] = t22(gt6)
    d["mst64b"] = t22(lt6)
    return {k: np.ascontiguousarray(v, dtype=f32) for k, v in d.items()}


def rope_tables(TS):
    f32 = np.float32
    freqs = (10000.0 ** (-np.arange(16, dtype=f32) / f32(16))).astype(f32)
    t = np.arange(TS)
    rows = (t // 64).astype(f32)
    cols = (t % 64).astype(f32)
    cos = np.zeros((128, TS), f32)
    sin = np.zeros((128, TS), f32)
    rot = np.zeros((128, 128), f32)
    for p in range(128):
        dd = p % 64
        pos = rows if dd < 32 else cols
        ang = (pos * freqs[dd % 16]).astype(f32)
        cos[p] = np.cos(ang)
        first = (dd % 32) < 16
        sin[p] = (-np.sin(ang) if first else np.sin(ang))
        partner = p + 16 if first else p - 16
        rot[partner, p] = 1.0
    return cos, sin, rot


MIX_IN = ("qkT", "xbcT", "rblkT", "gqkT", "gglT", "av_tm", "sdt_tm", "gk_tm", "sz_tm", "gv_tm", "gog_tm")
BR = ("o_att", "o_ssd", "o_rwkv", "o_gla")


def declare_io(cx, mode="full"):
    nc = cx.nc
    TT, TS, NP, TP, PAST = cx.TT, cx.TS, cx.NP, cx.TP, cx.PAST
    cx.inp = {}
    cx.out = {}

    def I(name, shape, dtype=F32):
        cx.inp[name] = nc.dram_tensor(name, list(shape), dtype, kind="ExternalInput").ap()

    def O(name, shape, dtype=F32):
        cx.out[name] = nc.dram_tensor(name, list(shape), dtype, kind="ExternalOutput").ap()

    def Sx(name, shape, dtype=F32):
        kind = None
        if mode == "mix" and name in MIX_IN:
            kind = "ExternalInput"
        if mode == "mix" and name in BR:
            kind = "ExternalOutput"
        if mode == "dense" and name in BR:
            kind = "ExternalInput"
        cx.dram(name, shape, dtype, kind=kind)

    cx.const_names = []
    for n, shp in CONST_SHAPES.items():
        I(n, shp)
        cx.const_names.append(n)
    if mode in ("full", "dense"):
        _, _, win_sz = w_in_offsets()
        I("xT_in", [D, TT])
        I("condT", [128, KC, 2])
        I("b_modT", [L, 128, 96])
        I("norm1T", [L, 128, KC])
        I("norm2T", [L, 128, KC])
        I("fnormT", [128, KC])
        I("w_mod_p", [L, 128, 96 * KC * 128])
        I("w_in_p", [L, 128, win_sz])
        I("w_gate_p", [L, 128, 64 * KC * 128])
        I("w_br_p", [L, 128, 64 * 4 * 128])
        I("w_o_p", [L, 128, KC * KC * 128])
        I("w1_p", [L, 128, FC * KC * 128])
        I("w3_p", [L, 128, FC * KC * 128])
        I("w2_p", [L, 128, KC * FC * 128])
        O("yT", [D, TT])
        Sx("xres", [D, TT])
        Sx("hT", [D, TT], BF16)
        for n in WNAMES:
            Sx(n + "_b", list(cx.inp[n].shape), BF16)
        cx.wbuf = [Buf() for _ in range(L)]
    if mode in ("full", "mix"):
        I("qgT", [L, 128, 1]); I("kgT", [L, 128, 1])
        I("ropecos", [128, TS]); I("ropesin", [128, TS]); I("rotP", [128, 128])
        I("cache_k", [L, PAST, 128]); I("cache_v", [L, PAST, 128])
        I("ssd_cwT", [L, 128, 6, 3]); I("ssd_cbT", [L, 128, 6])
        I("ssd_dt_bias", [L, 16]); I("ssd_a_log", [L, 16]); I("ssd_d", [L, 8]); I("ssd_norm", [L, 512])
        I("state_ssd", [L, 2, 8, 64, 64])
        I("gla_g2", [L, 2, 16, 256]); I("gla_gb", [L, 2, 256]); I("gla_norm", [L, 128])
        I("state_gla", [L, 2, 4, 64, 128])
        I("rw_muT", [L, 128, 14])
        for n in ("rw_a0T", "rw_kkT", "rw_kaT", "rw_rkT"):
            I(n, [L, 128, 4])
        I("rw_w0", [L, 2, 512]); I("rw_w2", [L, 2, 64, 512]); I("rw_a2", [L, 64, 512]); I("rw_g2", [L, 128, 512])
        I("rw_ln_g", [L, 512]); I("rw_ln_b", [L, 512])
        I("state_rwkvT", [L, 2, 8, 64, 64])
        O("new_k", [NP, L, TP, 128]); O("new_v", [NP, L, TP, 128])
        O("new_ssd", [NP, L, 2, 8, 64, 64]); O("new_rwkvT", [NP, L, 2, 8, 64, 64]); O("new_gla", [NP, L, 2, 4, 64, 128])
        Sx("qn_d", [640, TT], BF16)
        Sx("bcT", [256, TT]); Sx("xs_tm", [TT, 512]); Sx("b_tm", [TT, 128]); Sx("dt_tm", [TT, 16]); Sx("ld_tm", [TT, 16])
        Sx("o_ssd_d", [2, TT, 512]); Sx("o_gla_d", [2, TT, 512])
        Sx("rw_fm", [4, 512, TT]); Sx("rw_s_tm", [TT, 8]); Sx("rw_v_tm", [TT, 512]); Sx("rw_g_tm", [TT, 512])
        Sx("rw_lw_tm", [2, TT, 512]); Sx("o_rw_d", [2, TT, 512])
    Sx("qkT", [640, TT]); Sx("xbcT", [768, TT]); Sx("rblkT", [1792, TT]); Sx("gqkT", [512, TT]); Sx("gglT", [16, TT])
    Sx("av_tm", [TT, 128]); Sx("sdt_tm", [TT, 16]); Sx("gk_tm", [TT, 256]); Sx("sz_tm", [TT, 512]); Sx("gv_tm", [TT, 512]); Sx("gog_tm", [TT, 512])
    for n in BR:
        Sx(n, [TT, 512])


def build_program(cfg, dbg=(), mode="full", layers=None, mixers=("attn", "ssd", "gla", "rwkv")):
    nc = bass.Bass("TRN2", target_bir_lowering=False)
    cx = Ctx(nc, cfg, dbg)
    declare_io(cx, mode)
    load_consts(cx)
    fns = {"attn": phase_attn, "ssd": phase_ssd, "gla": phase_gla, "rwkv": phase_rwkv}
    if mode == "mix":
        for l in (layers if layers is not None else range(L)):
            for m in mixers:
                fns[m](cx, l)
    else:
        phase_convert(cx, 0)
        phase_mod(cx)
        for l in range(L):
            phase_A(cx, l)
            if l + 1 < L:
                phase_convert(cx, l + 1)
            if mode == "full":
                for m in mixers:
                    fns[m](cx, l)
            phase_C(cx, l, last=(l == L - 1))
    cx.fw.finish()
    cx.fw.emit()
    return nc, cx


def host_common_inputs(inp, cfg, mode="full"):
    f32 = np.float32
    A = lambda k: np.asarray(inp[k], f32)
    d = {}
    d.update(host_consts())
    if mode in ("full", "dense"):
        d["b_modT"] = np.stack([vec_pp(A("b_mod")[l]) for l in range(L)])
        d["norm1T"] = np.stack([vec_pp(A("norm1")[l]) for l in range(L)])
        d["norm2T"] = np.stack([vec_pp(A("norm2")[l]) for l in range(L)])
        d["fnormT"] = vec_pp(A("final_norm"))
        d["w_mod_p"] = np.stack([pack_chunks(A("w_mod")[l]) for l in range(L)])
        d["w_in_p"] = np.stack([pack_w_in(A("w_in")[l])[0] for l in range(L)])
        wg, wb = [], []
        for l in range(L):
            g = A("w_gate")[l]
            b = A("w_branch")[l]
            wg.append(np.concatenate([_pack_cols(g[i], f * 128, 128) for f in range(KC) for i in range(4)], axis=1))
            wb.append(np.concatenate([_pack_cols(b[i], f * 128, 128) for f in range(KC) for i in range(4)], axis=1))
        d["w_gate_p"] = np.stack(wg)
        d["w_br_p"] = np.stack(wb)
        d["w_o_p"] = np.stack([pack_chunks(A("w_o")[l]) for l in range(L)])
        d["w1_p"] = np.stack([pack_chunks(A("ffn_w1")[l]) for l in range(L)])
        d["w3_p"] = np.stack([pack_chunks(A("ffn_w3")[l]) for l in range(L)])
        d["w2_p"] = np.stack([pack_chunks(A("ffn_w2")[l]) for l in range(L)])
    if mode in ("full", "mix"):
        d["qgT"] = np.stack([np.tile(A("q_norm")[l], 2)[:, None] for l in range(L)])
        d["kgT"] = np.stack([np.tile(A("k_norm")[l], 2)[:, None] for l in range(L)])
        cos, sin, rot = rope_tables(cfg["TS"])
        d["ropecos"], d["ropesin"], d["rotP"] = cos, sin, rot
        cw = A("ssd_conv_w")
        d["ssd_cwT"] = np.ascontiguousarray(cw.reshape(L, 6, 128, 3).transpose(0, 2, 1, 3))
        d["ssd_cbT"] = np.stack([vec_pp(A("ssd_conv_b")[l]) for l in range(L)])
        d["ssd_dt_bias"] = A("ssd_dt_bias").reshape(L, 16)
        d["ssd_a_log"] = A("ssd_a_log").reshape(L, 16)
        d["ssd_d"] = A("ssd_d")
        d["ssd_norm"] = A("ssd_norm")
        d["gla_g2"] = A("gla_g2"); d["gla_gb"] = A("gla_gb"); d["gla_norm"] = A("gla_norm")
        d["rw_muT"] = np.stack([vec_pp(A("rwkv_mu")[l]) for l in range(L)])
        for n, k in (("rw_a0T", "rwkv_a0"), ("rw_kkT", "rwkv_kk"), ("rw_kaT", "rwkv_ka"), ("rw_rkT", "rwkv_rk")):
            d[n] = np.stack([vec_pp(A(k)[l]) for l in range(L)])
        d["rw_w0"] = A("rwkv_w0"); d["rw_w2"] = A("rwkv_w2"); d["rw_a2"] = A("rwkv_a2"); d["rw_g2"] = A("rwkv_g2")
        d["rw_ln_g"] = A("rwkv_ln_g"); d["rw_ln_b"] = A("rwkv_ln_b")
    return {k: np.ascontiguousarray(v, dtype=f32) for k, v in d.items()}


def host_core_inputs(inp, cfg, b_idx, p_idx, mode="full"):
    f32 = np.float32
    A = lambda k: np.asarray(inp[k], f32)
    d = {}
    if mode in ("full", "dense"):
        xs = A("x_sample")[b_idx]
        xp = np.concatenate([A("x_prompt")[p] for p in p_idx], axis=0)
        d["xT_in"] = np.concatenate([xs, xp], axis=0).T
        cond = np.stack([A("c_ctx"), A("c")[b_idx]], axis=-1)
        d["condT"] = cond.reshape(KC, 128, 2).transpose(1, 0, 2)
    if mode in ("full", "mix"):
        d["cache_k"] = A("cache_attn_k")[b_idx].reshape(L, cfg["PAST"], 128)
        d["cache_v"] = A("cache_attn_v")[b_idx].reshape(L, cfg["PAST"], 128)
        d["state_ssd"] = A("state_ssd")[b_idx]
        d["state_gla"] = A("state_gla")[b_idx]
        d["state_rwkvT"] = A("state_rwkv")[b_idx].transpose(0, 1, 2, 4, 3)
    return {k: np.ascontiguousarray(v, dtype=f32) for k, v in d.items()}


_PROG_CACHE = {}


def kernel(**inp):
    import sys
    global DEBUG_SITES
    DEBUG_SITES = False
    xs = np.asarray(inp["x_sample"])
    xp = np.asarray(inp["x_prompt"])
    n_cores = xs.shape[0]
    TS = xs.shape[1]
    NP = xp.shape[0] // n_cores
    TP = xp.shape[1]
    PAST = np.asarray(inp["cache_attn_k"]).shape[2]
    cfg = dict(TS=TS, NP=NP, TP=TP, PAST=PAST)
    key = (TS, NP, TP, PAST)
    if key not in _PROG_CACHE:
        _PROG_CACHE[key] = build_program(cfg, mode="full")
    nc, cx = _PROG_CACHE[key]
    common = host_common_inputs(inp, cfg, mode="full")
    in_maps = []
    for i in range(n_cores):
        d = dict(common)
        d.update(host_core_inputs(inp, cfg, i, list(range(i * NP, (i + 1) * NP)), mode="full"))
        in_maps.append(d)
    res = run_bass_kernel_spmd(nc, in_maps, core_ids=list(range(n_cores)))
    f32 = np.float32
    B = n_cores
    y_prompt = np.zeros((B * NP, TP, D), f32)
    y_sample = np.zeros((B, TS, D), f32)
    new_k = np.zeros((B * NP, L, TP, 2, 64), f32)
    new_v = np.zeros((B * NP, L, TP, 2, 64), f32)
    new_ssd = np.zeros((B * NP, L, 2, 8, 64, 64), f32)
    new_rwkv = np.zeros((B * NP, L, 2, 8, 64, 64), f32)
    new_gla = np.zeros((B * NP, L, 2, 4, 64, 128), f32)
    for i in range(n_cores):
        r = res.results[i]
        yT = np.asarray(r["yT"])
        y_sample[i] = yT[:, :TS].T
        for p in range(NP):
            y_prompt[i * NP + p] = yT[:, TS + p * TP:TS + (p + 1) * TP].T
        sl = slice(i * NP, (i + 1) * NP)
        new_k[sl] = np.asarray(r["new_k"]).reshape(NP, L, TP, 2, 64)
        new_v[sl] = np.asarray(r["new_v"]).reshape(NP, L, TP, 2, 64)
        new_ssd[sl] = np.asarray(r["new_ssd"])
        new_rwkv[sl] = np.asarray(r["new_rwkvT"]).transpose(0, 1, 2, 3, 5, 4)
        new_gla[sl] = np.asarray(r["new_gla"])
    return (y_prompt, y_sample, new_k, new_v, new_ssd, new_rwkv, new_gla)
```

```python
import numpy as np
import concourse.bass as bass
import concourse.mybir as mybir
from concourse.bass_utils import run_bass_kernel_spmd
from contextlib import ExitStack

F32 = mybir.dt.float32
BF16 = mybir.dt.bfloat16
AF = mybir.ActivationFunctionType
ALU = mybir.AluOpType
AX = mybir.AxisListType

D = 2048
L = 2
KC = D // 128
DFF = 5632
FC = DFF // 128
D_IN = 5408
N_CORES = 8


DEBUG_SITES = True


def _site():
    if not DEBUG_SITES:
        return None
    import sys
    f = sys._getframe(2)
    out = []
    while f is not None and len(out) < 4:
        if f.f_code.co_name not in ("X", "ACT", "TT_", "TS_", "STT", "COPY", "LOAD", "STORE", "mm_group", "transposes"):
            out.append(f"{f.f_code.co_name}:{f.f_lineno}")
        f = f.f_back
    return out


class Buf:
    __slots__ = ("w", "r")

    def __init__(self):
        self.w = None
        self.r = {}


class Tile:
    __slots__ = ("t", "b")

    def __init__(self, t, b=None):
        self.t = t
        self.b = b if b is not None else Buf()

    def __getitem__(self, k):
        return self.t[k]


def _bufs(xs):
    return [x.b if isinstance(x, Tile) else x for x in xs]


class EngState:
    def __init__(self, fw, name):
        self.fw = fw
        self.name = name
        self.prog = []
        self.known = {}
        self.sem = None
        self.cnt = 0
        self.dma_sems = []
        self.dma_cnt = []
        self.dma_rr = 0

    def new_sem(self):
        self.sem = self.fw.nc.alloc_semaphore(f"s_{self.name}_{self.fw.nsem}")
        self.fw.nsem += 1
        self.cnt = 0


class FW:
    SEM_MAX = 8000
    DMA_SEM_MAX = 500

    def __init__(self, nc, n_dma_sems=12):
        self.nc = nc
        self.nsem = 0
        self.eng = {}
        self.old_sems = []
        for n in ("pe", "act", "dve", "pool", "sp"):
            st = EngState(self, n)
            self.eng[n] = st
            if n != "sp":
                st.new_sem()
        for n in ("sp", "pool", "act"):
            st = self.eng[n]
            k = n_dma_sems if n == "sp" else 8
            for i in range(k):
                st.dma_sems.append(nc.alloc_semaphore(f"d_{n}_{i}"))
                st.dma_cnt.append(0)
                self.nsem += 1
        self.n_ops = 0

    def _wait(self, st, tok):
        sem, val = tok
        if st.known.get(sem.num, 0) < val:
            st.prog.append(("wait", sem, val))
            st.known[sem.num] = val

    def _deps(self, eng, reads, writes):
        st = self.eng[eng]
        deps = {}

        def add(tok):
            if tok is None:
                return
            s, v = tok
            if deps.get(s.num, (None, 0))[1] < v:
                deps[s.num] = (s, v)

        for b in reads:
            add(b.w)
        for b in writes:
            add(b.w)
            for t in b.r.values():
                add(t)
        for s, v in deps.values():
            if eng == "pe" and st.sem is not None and s.num == st.sem.num:
                continue
            self._wait(st, (s, v))

    def _mark(self, tok, reads, writes):
        s, v = tok
        for b in reads:
            b.r[s.num] = tok
        for b in writes:
            b.w = tok
            b.r = {}

    def op(self, eng, fn, reads=(), writes=()):
        reads = _bufs(reads)
        writes = _bufs(writes)
        st = self.eng[eng]
        self._deps(eng, reads, writes)
        if st.cnt >= self.SEM_MAX:
            self.old_sems.append((st.sem, st.cnt))
            st.new_sem()
        st.cnt += 1
        tok = (st.sem, st.cnt)
        st.prog.append(("op", fn, st.sem, 1, _site()))
        self._mark(tok, reads, writes)
        self.n_ops += 1
        return tok

    def dma(self, q, out, in_, reads=(), writes=(), **kw):
        reads = _bufs(reads)
        writes = _bufs(writes)
        st = self.eng[q]
        self._deps(q, reads, writes)
        i = st.dma_rr
        st.dma_rr = (i + 1) % len(st.dma_sems)
        sem = st.dma_sems[i]
        c = st.dma_cnt[i]
        if c > 0:
            self._wait(st, (sem, 16 * c))
        if c >= self.DMA_SEM_MAX:
            self.old_sems.append((sem, 16 * c))
            sem = self.nc.alloc_semaphore(f"d_{q}_{self.nsem}")
            self.nsem += 1
            st.dma_sems[i] = sem
            c = 0
        st.dma_cnt[i] = c + 1
        tok = (sem, 16 * (c + 1))
        st.prog.append(("op", lambda e: e.dma_start(out=out, in_=in_, **kw), sem, 16, _site()))
        self._mark(tok, reads, writes)
        self.n_ops += 1
        return tok

    def barrier(self, engines=("pe", "act", "dve", "pool", "sp")):
        toks = list(self.old_sems)
        for n, st in self.eng.items():
            if st.sem is not None and st.cnt > 0:
                toks.append((st.sem, st.cnt))
            for sem, c in zip(st.dma_sems, st.dma_cnt):
                if c > 0:
                    toks.append((sem, 16 * c))
        for n in engines:
            st = self.eng[n]
            for t in toks:
                if st.sem is not None and t[0].num == st.sem.num:
                    continue
                self._wait(st, t)

    def finish(self):
        self.barrier(engines=("sp",))

    def emit(self):
        nc = self.nc
        engs = {"pe": "tensor", "act": "scalar", "dve": "vector", "pool": "gpsimd", "sp": "sync"}
        with nc.Block() as block:
            for n, attr in engs.items():
                st = self.eng[n]

                def body(e, st=st):
                    for it in st.prog:
                        if it[0] == "wait":
                            e.wait_ge(it[1], it[2])
                        else:
                            try:
                                ins = it[1](e)
                            except Exception:
                                print("FAILED OP SITE:", it[4])
                                raise
                            ins.then_inc(it[2], it[3])

                getattr(block, attr)(body)


class Ctx:
    def __init__(self, nc, cfg, dbg=()):
        self.nc = nc
        self.fw = FW(nc)
        self.cfg = cfg
        self.dbg = set(dbg)
        self.dr = {}
        self.uid = 0
        self.es = None
        self.ps = [Tile(nc.alloc_psum_tensor(f"psb{i}", [128, 512], F32)) for i in range(8)]
        self.ps_i = 0
        self.ps_n = 8
        self.ev_i = 0
        self.TS = cfg["TS"]
        self.NP = cfg["NP"]
        self.TP = cfg["TP"]
        self.PAST = cfg["PAST"]
        self.TT = self.TS + self.NP * self.TP
        assert self.TS % 512 == 0 and (self.NP * self.TP) % 512 == 0
        self.NTILE = self.TT // 512
        self.seqs = [(0, self.TS, True, -1)] + [(self.TS + p * self.TP, self.TP, False, p) for p in range(self.NP)]

    def psum(self):
        p = self.ps[self.ps_i]
        self.ps_i = (self.ps_i + 1) % self.ps_n
        return p

    def dram(self, name, shape, dtype=F32, kind=None):
        if kind is None:
            kind = "ExternalOutput" if name in self.dbg else "Internal"
        t = self.nc.dram_tensor(name, list(shape), dtype, kind=kind).ap()
        self.dr[name] = t
        return t

    def phase_begin(self):
        assert self.es is None
        self.es = ExitStack()

    def phase_end(self):
        self.fw.barrier()
        self.es.close()
        self.es = None

    def sb(self, shape, dtype=F32, name="t"):
        self.uid += 1
        t = self.es.enter_context(self.nc.sbuf_tensor(f"{name}_{self.uid}", list(shape), dtype))
        return Tile(t)

    def ring(self, n, shape, dtype=F32, name="r"):
        return Ring([self.sb(shape, dtype, name) for _ in range(n)])

    def evac_eng(self):
        self.ev_i += 1
        return "act" if self.ev_i % 2 == 0 else "dve"

    def copy(self, eng, out_ap, in_ap, reads, writes):
        if eng == "act":
            return self.fw.op("act", lambda e: e.activation(out=out_ap, in_=in_ap, func=AF.Copy), reads, writes)
        return self.fw.op(eng, lambda e: e.tensor_copy(out_ap, in_ap), reads, writes)


class Ring:
    def __init__(self, tiles):
        self.tiles = tiles
        self.i = 0

    def next(self):
        t = self.tiles[self.i]
        self.i = (self.i + 1) % len(self.tiles)
        return t


FM_JOBS = []
for c in range(5):
    FM_JOBS.append((c * 128, 128, "qkT", c * 128))
for c in range(6):
    FM_JOBS.append((1280 + c * 128, 128, "xbcT", c * 128))
for c in range(14):
    FM_JOBS.append((2064 + c * 128, 128, "rblkT", c * 128))
for c in range(4):
    FM_JOBS.append((3856 + c * 128, 128, "gqkT", c * 128))
FM_JOBS.append((4880, 16, "gglT", 0))
TM_GROUPS = [
    [(640, 128, "av_tm", 0), (2048, 16, "sdt_tm", 0), (4112, 256, "gk_tm", 0)],
    [(768, 512, "sz_tm", 0)],
    [(4368, 512, "gv_tm", 0)],
    [(4896, 512, "gog_tm", 0)],
]


def _pack_cols(W, c0, n):
    K = W.shape[0]
    blk = W[:, c0:c0 + n].reshape(K // 128, 128, n).transpose(1, 0, 2).reshape(128, (K // 128) * n)
    return blk


def pack_w_in(w_in_l):
    parts = []
    offs_fm = []
    off = 0
    for (c0, n, _, _) in FM_JOBS:
        parts.append(_pack_cols(w_in_l, c0, n))
        offs_fm.append(off)
        off += KC * n
    offs_tm = []
    for grp in TM_GROUPS:
        cols = np.concatenate([np.arange(c0, c0 + n) for (c0, n, _, _) in grp])
        Wg = w_in_l[:, cols]
        parts.append(_pack_cols(Wg, 0, Wg.shape[1]))
        offs_tm.append(off)
        off += KC * Wg.shape[1]
    return np.ascontiguousarray(np.concatenate(parts, axis=1)), offs_fm, offs_tm


def w_in_offsets():
    offs_fm = []
    off = 0
    for (c0, n, _, _) in FM_JOBS:
        offs_fm.append(off)
        off += KC * n
    offs_tm = []
    for grp in TM_GROUPS:
        n = sum(g[1] for g in grp)
        offs_tm.append(off)
        off += KC * n
    return offs_fm, offs_tm, off


def pack_chunks(W):
    C = W.shape[1]
    return np.ascontiguousarray(np.concatenate([_pack_cols(W, c * 128, 128) for c in range(C // 128)], axis=1))


def vec_pp(v):
    return np.ascontiguousarray(v.reshape(-1, 128).T)


def _bind(method, args, kw):
    return lambda e: getattr(e, method)(*args, **kw)


def X(cx, eng, method, *args, reads=(), writes=(), **kw):
    return cx.fw.op(eng, _bind(method, args, kw), reads, writes)


def ACT(cx, out, in_, func, reads, writes, **kw):
    return X(cx, "act", "activation", out=out, in_=in_, func=func, reads=reads, writes=writes, **kw)


def TT_(cx, eng, out, in0, in1, op, reads, writes):
    return X(cx, eng, "tensor_tensor", out, in0, in1, op, reads=reads, writes=writes)


def TS_(cx, eng, out, in0, s1, s2, op0, op1, reads, writes):
    if s2 is None:
        return X(cx, eng, "tensor_scalar", out, in0, s1, None, op0, reads=reads, writes=writes)
    return X(cx, eng, "tensor_scalar", out, in0, s1, s2, op0, op1, reads=reads, writes=writes)


def STT(cx, eng, out, in0, scalar, in1, op0, op1, reads, writes):
    return X(cx, eng, "scalar_tensor_tensor", out, in0, scalar, in1, op0, op1, reads=reads, writes=writes)


def COPY(cx, eng, out, in_, reads, writes):
    if eng == "act":
        return ACT(cx, out, in_, AF.Copy, reads, writes)
    return X(cx, eng, "tensor_copy", out, in_, reads=reads, writes=writes)


def _mm_fn(ps_ap, pairs):
    n = len(pairs)

    def fn(e):
        ins = None
        for i, (l, r) in enumerate(pairs):
            ins = e.matmul(ps_ap, lhsT=l, rhs=r, start=(i == 0), stop=(i == n - 1))
        return ins
    return fn


def mm_group(cx, ps_ap, pairs, reads, ps_tile):
    return cx.fw.op("pe", _mm_fn(ps_ap, list(pairs)), reads=reads, writes=[ps_tile])


def _tr_fn(items, ident_ap):
    def fn(e):
        ins = None
        for (o, i) in items:
            ins = e.transpose(o, i, ident_ap)
        return ins
    return fn


def transposes(cx, items, reads, ps_tile, np_=128):
    ident = cx.C["ident"]
    return cx.fw.op("pe", _tr_fn(list(items), ident[0:np_, 0:np_]), reads=list(reads) + [ident], writes=[ps_tile])


def LOAD(cx, out_ap, in_ap, tile_, reads=()):
    return cx.fw.dma("pool", out_ap, in_ap, reads=list(reads), writes=[tile_])


WNAMES = ("w_in_p", "w_gate_p", "w_br_p", "w_o_p", "w1_p", "w3_p", "w2_p")


def phase_convert(cx, l):
    CH = 16384
    for n in WNAMES:
        src = cx.inp[n]
        dst = cx.dr[n + "_b"]
        X_ = src.shape[2]
        for a in range(0, X_, CH):
            b = min(X_, a + CH)
            cx.fw.dma("pool", dst[l, :, a:b], src[l, :, a:b])
    st = cx.fw.eng["pool"]
    for sem, c in zip(st.dma_sems, st.dma_cnt):
        if c > 0:
            cx.fw._wait(st, (sem, 16 * c))


def LOADQ(cx, q, out_ap, in_ap, tile_, reads=()):
    return cx.fw.dma(q, out_ap, in_ap, reads=list(reads), writes=[tile_])


def STOREQ(cx, q, out_ap, in_ap, tile_):
    return cx.fw.dma(q, out_ap, in_ap, reads=[tile_])


def STORE(cx, out_ap, in_ap, tile_):
    return cx.fw.dma("sp", out_ap, in_ap, reads=[tile_])


def load_consts(cx):
    nc = cx.nc
    C = {}
    for name in cx.const_names:
        shp = list(cx.inp[name].shape)
        t = Tile(nc.alloc_sbuf_tensor("c_" + name, shp, F32))
        LOAD(cx, t[:], cx.inp[name], t)
        C[name] = t
    cx.C = C


def phase_mod(cx):
    nc = cx.nc
    cx.MOD = []
    pers = lambda name, shape: Tile(nc.alloc_sbuf_tensor("m_" + name, list(shape), F32))
    Ms = [{k: pers(f"{k}_{l}", [128, KC, 2]) for k in ("A1", "B1", "G1", "A2", "B2", "G2")} for l in range(L)]
    fn = pers("fnorm", [128, KC])
    cx.phase_begin()
    cond = cx.sb([128, KC, 2])
    LOAD(cx, cond[:], cx.inp["condT"], cond)
    scond = cx.sb([128, KC, 2])
    ACT(cx, scond[:], cond[:], AF.Silu, [cond], [scond])
    wring = cx.ring(3, [128, KC * 128], F32, "wmod")
    for l in range(L):
        modT = cx.sb([128, 96, 2])
        bm = cx.sb([128, 96])
        LOAD(cx, bm[:], cx.inp["b_modT"][l], bm)
        for c in range(96):
            wt = wring.next()
            LOAD(cx, wt[:], cx.inp["w_mod_p"][l, :, c * KC * 128:(c + 1) * KC * 128], wt)
            ps = cx.psum()
            mm_group(cx, ps[:, 0:2], [(wt[:, k * 128:(k + 1) * 128], scond[:, k, :]) for k in range(KC)], [wt, scond], ps)
            TS_(cx, "dve", modT[:, c, :], ps[:, 0:2], bm[:, c:c + 1], None, ALU.add, None, [ps, bm], [modT])
        n1 = cx.sb([128, KC])
        n2 = cx.sb([128, KC])
        LOAD(cx, n1[:], cx.inp["norm1T"][l], n1)
        LOAD(cx, n2[:], cx.inp["norm2T"][l], n2)
        M = Ms[l]

        def mk(dst, ch, nrm=None):
            src = modT[:, ch * 16:(ch + 1) * 16, :]
            if nrm is None:
                COPY(cx, "dve", dst[:], src, [modT], [dst])
            else:
                STT(cx, "dve", dst[:], src, 1.0, nrm[:].unsqueeze(2).broadcast_to([128, KC, 2]), ALU.add, ALU.mult, [modT, nrm], [dst])
        mk(M["B1"], 0)
        mk(M["A1"], 1, n1)
        mk(M["G1"], 2)
        mk(M["B2"], 3)
        mk(M["A2"], 4, n2)
        mk(M["G2"], 5)
        cx.MOD.append(M)
    LOAD(cx, fn[:], cx.inp["fnormT"], fn)
    cx.FNORM = fn
    cx.phase_end()


def rms_fm(cx, xT, scal, bias, outs, sqring, tmpring, small):
    ps = cx.psum()
    ones = cx.C["ones"]
    for k in range(KC):
        sq = sqring.next()
        ACT(cx, sq[:], xT[:, k, :], AF.Square, [xT], [sq])
        X(cx, "pe", "matmul", ps[:], lhsT=ones[:], rhs=sq[:], start=(k == 0), stop=(k == KC - 1), reads=[sq, ones], writes=[ps])
    sd, rstd = small
    ACT(cx, sd[:], ps[:], AF.Sqrt, [ps], [sd], bias=cx.C["eps6"][:, 0:1], scale=1.0 / D)
    X(cx, "dve", "reciprocal", rstd[:], sd[:], reads=[sd], writes=[rstd])
    for k in range(KC):
        oap, ot = outs[k]
        if bias is None:
            STT(cx, "dve", oap, xT[:, k, :], scal[k], rstd[:], ALU.mult, ALU.mult, [xT, rstd], [ot])
        else:
            tmp = tmpring.next()
            STT(cx, "dve", tmp[:], xT[:, k, :], scal[k], rstd[:], ALU.mult, ALU.mult, [xT, rstd], [tmp])
            ACT(cx, oap, tmp[:], AF.Identity, [tmp], [ot], bias=bias[k], scale=1.0)


def phase_A(cx, l):
    offs_fm, offs_tm, _ = w_in_offsets()
    wp = cx.dr["w_in_p_b"]
    wrd = [cx.wbuf[l]]
    M = cx.MOD[l]
    cx.phase_begin()
    xring = cx.ring(2, [128, KC, 512], F32, "xT")
    hring = cx.ring(2, [128, KC, 512], BF16, "hT")
    sqring = cx.ring(2, [128, 512], F32, "sq")
    tmpring = cx.ring(2, [128, 512], F32, "tmp")
    small = (cx.sb([128, 512]), cx.sb([128, 512]))
    wring = cx.ring(3, [128, KC * 512], BF16, "w")
    oring = cx.ring(4, [128, 512], F32, "o")
    src = cx.dr["xres"] if l > 0 else cx.inp["xT_in"]
    for ti in range(cx.NTILE):
        t0 = ti * 512
        ci = 1 if t0 < cx.TS else 0
        xT = xring.next()
        LOAD(cx, xT[:], src[:, t0:t0 + 512].rearrange("(k p) n -> p k n", p=128), xT)
        hT = hring.next()
        rms_fm(cx, xT, [M["A1"][:, k, ci:ci + 1] for k in range(KC)], [M["B1"][:, k, ci:ci + 1] for k in range(KC)],
               [(hT[:, k, :], hT) for k in range(KC)], sqring, tmpring, small)
        STORE(cx, cx.dr["hT"][:, t0:t0 + 512].rearrange("(k p) n -> p k n", p=128), hT[:], hT)
        for j, (c0, n, dst, doff) in enumerate(FM_JOBS):
            wt = wring.next()
            LOAD(cx, wt[:, 0:KC * n], wp[l, :, offs_fm[j]:offs_fm[j] + KC * n], wt, wrd)
            ps = cx.psum()
            mm_group(cx, ps[0:n, :], [(wt[:, k * n:(k + 1) * n], hT[:, k, :]) for k in range(KC)], [wt, hT], ps)
            o = oring.next()
            COPY(cx, cx.evac_eng(), o[0:n, :], ps[0:n, :], [ps], [o])
            STORE(cx, cx.dr[dst][doff:doff + n, t0:t0 + 512], o[0:n, :], o)
        for g, grp in enumerate(TM_GROUPS):
            n = sum(x[1] for x in grp)
            wt = wring.next()
            LOAD(cx, wt[:, 0:KC * n], wp[l, :, offs_tm[g]:offs_tm[g] + KC * n], wt, wrd)
            for s in range(4):
                ps = cx.psum()
                mm_group(cx, ps[:, 0:n], [(hT[:, k, s * 128:(s + 1) * 128], wt[:, k * n:(k + 1) * n]) for k in range(KC)], [wt, hT], ps)
                o = oring.next()
                COPY(cx, cx.evac_eng(), o[:, 0:n], ps[:, 0:n], [ps], [o])
                co = 0
                for (c0, nn, dst, dcol) in grp:
                    STORE(cx, cx.dr[dst][t0 + s * 128:t0 + (s + 1) * 128, dcol:dcol + nn], o[:, co:co + nn], o)
                    co += nn
    cx.phase_end()


def phase_C(cx, l, last):
    M = cx.MOD[l]
    wrd = [cx.wbuf[l]]
    cx.phase_begin()
    xT = cx.sb([128, KC, 512], F32, "xT")
    hT = cx.sb([128, KC, 512], BF16, "hT")
    U = cx.sb([128, FC, 512], BF16, "U")
    oin = cx.ring(2, [128, 4, 512], F32, "oin")
    sqring = cx.ring(2, [128, 512], F32, "sq")
    tmpring = cx.ring(4, [128, 512], F32, "tmp")
    small = (cx.sb([128, 512]), cx.sb([128, 512]))
    wring = cx.ring(3, [128, FC * 128], BF16, "w")
    src = cx.dr["xres"] if l > 0 else cx.inp["xT_in"]
    br_names = ["o_att", "o_ssd", "o_rwkv", "o_gla"]
    GSZ = KC * 128
    BSZ = 4 * 128
    W2SZ = FC * 128
    for ti in range(cx.NTILE):
        t0 = ti * 512
        ci = 1 if t0 < cx.TS else 0
        LOAD(cx, xT[:], src[:, t0:t0 + 512].rearrange("(k p) n -> p k n", p=128), xT)
        LOAD(cx, hT[:], cx.dr["hT"][:, t0:t0 + 512].rearrange("(k p) n -> p k n", p=128), hT)
        for i in range(4):
            ot = oin.next()
            LOAD(cx, ot[:], cx.dr[br_names[i]][t0:t0 + 512, :].rearrange("(s p) c -> p s c", p=128), ot)
            for c in range(4):
                ps = cx.psum()
                transposes(cx, [(ps[:, s * 128:(s + 1) * 128], ot[:, s, c * 128:(c + 1) * 128]) for s in range(4)], [ot], ps)
                COPY(cx, cx.evac_eng(), U[:, i * 4 + c, :], ps[:], [ps], [U])
        for f in range(KC):
            acc = None
            for i in range(4):
                wg = wring.next()
                LOADQ(cx, "sp", wg[:, 0:GSZ], cx.dr["w_gate_p_b"][l, :, (f * 4 + i) * GSZ:(f * 4 + i + 1) * GSZ], wg, wrd)
                wb = wring.next()
                LOADQ(cx, "sp", wb[:, 0:BSZ], cx.dr["w_br_p_b"][l, :, (f * 4 + i) * BSZ:(f * 4 + i + 1) * BSZ], wb, wrd)
                psg = cx.psum()
                mm_group(cx, psg[:], [(wg[:, k * 128:(k + 1) * 128], hT[:, k, :]) for k in range(KC)], [wg, hT], psg)
                psb = cx.psum()
                mm_group(cx, psb[:], [(wb[:, k * 128:(k + 1) * 128], U[:, i * 4 + k, :]) for k in range(4)], [wb, U], psb)
                sig = tmpring.next()
                ACT(cx, sig[:], psg[:], AF.Sigmoid, [psg], [sig])
                if i == 0:
                    acc = tmpring.next()
                    TT_(cx, "dve", acc[:], sig[:], psb[:], ALU.mult, [sig, psb], [acc])
                else:
                    TT_(cx, "dve", sig[:], sig[:], psb[:], ALU.mult, [sig, psb], [sig])
                    if i < 3:
                        TT_(cx, "dve", acc[:], acc[:], sig[:], ALU.add, [sig, acc], [acc])
                    else:
                        TT_(cx, "dve", U[:, 16 + f, :], acc[:], sig[:], ALU.add, [sig, acc], [U])
        for f in range(KC):
            wo = wring.next()
            LOADQ(cx, "sp", wo[:, 0:GSZ], cx.dr["w_o_p_b"][l, :, f * GSZ:(f + 1) * GSZ], wo, wrd)
            ps = cx.psum()
            mm_group(cx, ps[:], [(wo[:, k * 128:(k + 1) * 128], U[:, 16 + k, :]) for k in range(KC)], [wo, U], ps)
            STT(cx, "dve", xT[:, f, :], ps[:], M["G1"][:, f, ci:ci + 1], xT[:, f, :], ALU.mult, ALU.add, [ps, xT], [xT])
        rms_fm(cx, xT, [M["A2"][:, k, ci:ci + 1] for k in range(KC)], [M["B2"][:, k, ci:ci + 1] for k in range(KC)],
               [(hT[:, k, :], hT) for k in range(KC)], sqring, tmpring, small)
        for j in range(FC):
            w1 = wring.next()
            LOADQ(cx, "sp", w1[:, 0:GSZ], cx.dr["w1_p_b"][l, :, j * GSZ:(j + 1) * GSZ], w1, wrd)
            w3 = wring.next()
            LOADQ(cx, "sp", w3[:, 0:GSZ], cx.dr["w3_p_b"][l, :, j * GSZ:(j + 1) * GSZ], w3, wrd)
            p1 = cx.psum()
            mm_group(cx, p1[:], [(w1[:, k * 128:(k + 1) * 128], hT[:, k, :]) for k in range(KC)], [w1, hT], p1)
            p3 = cx.psum()
            mm_group(cx, p3[:], [(w3[:, k * 128:(k + 1) * 128], hT[:, k, :]) for k in range(KC)], [w3, hT], p3)
            s1 = tmpring.next()
            ACT(cx, s1[:], p1[:], AF.Silu, [p1], [s1])
            TT_(cx, "dve", U[:, j, :], s1[:], p3[:], ALU.mult, [s1, p3], [U])
        for f in range(KC):
            w2 = wring.next()
            LOADQ(cx, "sp", w2[:, 0:W2SZ], cx.dr["w2_p_b"][l, :, f * W2SZ:(f + 1) * W2SZ], w2, wrd)
            ps = cx.psum()
            mm_group(cx, ps[:], [(w2[:, k * 128:(k + 1) * 128], U[:, k, :]) for k in range(FC)], [w2, U], ps)
            STT(cx, "dve", xT[:, f, :], ps[:], M["G2"][:, f, ci:ci + 1], xT[:, f, :], ALU.mult, ALU.add, [ps, xT], [xT])
        if not last:
            STOREQ(cx, "pool", cx.dr["xres"][:, t0:t0 + 512].rearrange("(k p) n -> p k n", p=128), xT[:], xT)
        else:
            outs = []
            stg = [tmpring.next() for _ in range(4)]
            ps = cx.psum()
            ones = cx.C["ones"]
            for k in range(KC):
                sq = sqring.next()
                ACT(cx, sq[:], xT[:, k, :], AF.Square, [xT], [sq])
                X(cx, "pe", "matmul", ps[:], lhsT=ones[:], rhs=sq[:], start=(k == 0), stop=(k == KC - 1), reads=[sq, ones], writes=[ps])
            sd, rstd = small
            ACT(cx, sd[:], ps[:], AF.Sqrt, [ps], [sd], bias=cx.C["eps6"][:, 0:1], scale=1.0 / D)
            X(cx, "dve", "reciprocal", rstd[:], sd[:], reads=[sd], writes=[rstd])
            for k in range(KC):
                t = stg[k % 4]
                STT(cx, "dve", t[:], xT[:, k, :], cx.FNORM[:, k:k + 1], rstd[:], ALU.mult, ALU.mult, [xT, rstd], [t])
                STOREQ(cx, "pool", cx.out["yT"][k * 128:(k + 1) * 128, t0:t0 + 512], t[:], t)
    cx.phase_end()


def phase_attn(cx, l):
    TS, TT = cx.TS, cx.TT
    qn_d = cx.dr["qn_d"]
    cx.phase_begin()
    gq = cx.sb([128, 1]); gk = cx.sb([128, 1])
    LOAD(cx, gq[:], cx.inp["qgT"][l], gq)
    LOAD(cx, gk[:], cx.inp["kgT"][l], gk)
    cos = cx.sb([128, TS]); sin = cx.sb([128, TS]); rotP = cx.sb([128, 128])
    LOAD(cx, cos[:], cx.inp["ropecos"], cos)
    LOAD(cx, sin[:], cx.inp["ropesin"], sin)
    LOAD(cx, rotP[:], cx.inp["rotP"], rotP)
    rawr = cx.ring(2, [128, 5, 512], F32, "raw")
    sqr = cx.ring(2, [128, 512], F32, "sq")
    sdr = cx.ring(2, [128, 512], F32, "sd")
    qnr = cx.ring(2, [128, 512], F32, "qn")
    t1r = cx.ring(2, [128, 512], F32, "t1")
    obr = cx.ring(3, [128, 512], BF16, "ob")
    ktr = cx.ring(2, [128, 4, 128], F32, "kt")
    bones = cx.C["bones"]
    for ti in range(cx.NTILE):
        t0 = ti * 512
        samp = t0 < TS
        raw = rawr.next()
        LOAD(cx, raw[:], cx.dr["qkT"][:, t0:t0 + 512].rearrange("(c p) n -> p c n", p=128), raw)
        for c in range(5):
            g = gq if c < 4 else gk
            sq = sqr.next()
            ACT(cx, sq[:], raw[:, c, :], AF.Square, [raw], [sq])
            ps = cx.psum()
            X(cx, "pe", "matmul", ps[:], lhsT=bones[:], rhs=sq[:], start=True, stop=True, reads=[sq, bones], writes=[ps])
            sd = sdr.next()
            ACT(cx, sd[:], ps[:], AF.Sqrt, [ps], [sd], bias=cx.C["eps6"][:, 0:1], scale=1.0 / 64)
            X(cx, "dve", "reciprocal", sd[:], sd[:], reads=[sd], writes=[sd])
            qn = qnr.next()
            STT(cx, "dve", qn[:], raw[:, c, :], g[:, 0:1], sd[:], ALU.mult, ALU.mult, [raw, g, sd], [qn])
            ob = obr.next()
            if samp:
                ps2 = cx.psum()
                X(cx, "pe", "matmul", ps2[:], lhsT=rotP[:], rhs=qn[:], start=True, stop=True, reads=[qn, rotP], writes=[ps2])
                t1 = t1r.next()
                TT_(cx, "dve", t1[:], qn[:], cos[:, t0:t0 + 512], ALU.mult, [qn, cos], [t1])
                t2 = t1r.next()
                TT_(cx, "dve", t2[:], ps2[:], sin[:, t0:t0 + 512], ALU.mult, [ps2, sin], [t2])
                TT_(cx, "dve", ob[:], t1[:], t2[:], ALU.add, [t1, t2], [ob])
            else:
                ACT(cx, ob[:], qn[:], AF.Copy, [qn], [ob])
                if c == 4:
                    ps3 = cx.psum()
                    transposes(cx, [(ps3[:, s * 128:(s + 1) * 128], qn[:, s * 128:(s + 1) * 128]) for s in range(4)], [qn], ps3)
                    kt = ktr.next()
                    COPY(cx, cx.evac_eng(), kt[:].rearrange("p s c -> p (s c)"), ps3[:], [ps3], [kt])
                    for s in range(4):
                        tok = t0 + s * 128 - TS
                        p, tin = tok // cx.TP, tok % cx.TP
                        STORE(cx, cx.out["new_k"][p, l, tin:tin + 128, :], kt[:, s, :], kt)
            STORE(cx, qn_d[c * 128:(c + 1) * 128, t0:t0 + 512], ob[:], ob)
    for (st, T, samp, p) in cx.seqs:
        if not samp:
            cx.fw.dma("sp", cx.out["new_v"][p, l], cx.dr["av_tm"][st:st + T, :])
    cx.phase_end()
    cx.phase_begin()
    save_ps = cx.ps_n
    cx.ps_n = 6
    cx.ps_i = 0
    po_banks = [cx.ps[6], cx.ps[7]]
    po_i = 0
    Smax = TS + cx.PAST
    KTa = cx.sb([128, Smax], BF16, "KTa")
    KTb = cx.sb([128, Smax], BF16, "KTb")
    NCHmax = Smax // 128
    V1 = cx.sb([128, NCHmax, 2, 128], BF16, "V1")
    ckr = cx.ring(2, [128, 128], F32, "ck")
    qTr = cx.ring(2, [128, 4, 512], BF16, "qT")
    pTr = cx.ring(3, [128, 512], BF16, "pT")
    osr = cx.ring(2, [128, 4, 512], F32, "os")
    recr = cx.ring(2, [128, 4], F32, "rec")
    for (st, T, samp, p) in cx.seqs:
        S = T + (cx.PAST if samp else 0)
        NCH = S // 128
        NO = T // 128
        LOAD(cx, KTa[:, 0:T], qn_d[512:640, st:st + T], KTa)
        LOAD(cx, KTb[0:64, 0:T], qn_d[576:640, st:st + T], KTb)
        LOAD(cx, KTb[64:128, 0:T], qn_d[512:576, st:st + T], KTb)
        X(cx, "dve", "memset", V1[:, :, :, 64:66], 1.0, reads=[], writes=[V1])
        for kv_ in range(2):
            for c0_ in range(0, NO, 8):
                c1_ = min(NO, c0_ + 8)
                LOAD(cx, V1[:, c0_:c1_, kv_, 0:64],
                     cx.dr["av_tm"][st + c0_ * 128:st + c1_ * 128, kv_ * 64:(kv_ + 1) * 64].rearrange("(c p) d -> p c d", p=128), V1)
        if samp:
            NPC = cx.PAST // 128
            for kv_ in range(2):
                LOAD(cx, V1[:, NO:NO + NPC, kv_, 0:64], cx.inp["cache_v"][l][:, kv_ * 64:(kv_ + 1) * 64].rearrange("(c p) d -> p c d", p=128), V1)
            for j in range(NPC):
                for (KTx, swap) in ((KTa, False), (KTb, True)):
                    ck = ckr.next()
                    src = cx.inp["cache_k"][l, j * 128:(j + 1) * 128, :]
                    if not swap:
                        LOAD(cx, ck[:], src, ck)
                    else:
                        LOAD(cx, ck[:, 0:64], src[:, 64:128], ck)
                        LOAD(cx, ck[:, 64:128], src[:, 0:64], ck)
                    ps = cx.psum()
                    transposes(cx, [(ps[:, 0:128], ck[:])], [ck], ps)
                    COPY(cx, cx.evac_eng(), KTx[:, T + j * 128:T + (j + 1) * 128], ps[:, 0:128], [ps], [KTx])
        QN = min(512, T)
        ns = QN // 128
        for q0 in range(0, T, QN):
            qT = qTr.next()
            LOAD(cx, qT[:, :, 0:QN], qn_d[0:512, st + q0:st + q0 + QN].rearrange("(c p) n -> p c n", p=128), qT)
            osb = osr.next()
            for h in range(8):
                kv = h // 4
                base = (h % 2) * 64
                c = h // 2
                KT = KTa if (h % 2) == kv else KTb
                po = po_banks[po_i]
                po_i = 1 - po_i
                for ch in range(NCH):
                    ps = cx.psum()
                    X(cx, "pe", "matmul", ps[:, 0:QN], lhsT=KT[base:base + 64, ch * 128:(ch + 1) * 128], rhs=qT[base:base + 64, c, 0:QN],
                      start=True, stop=True, reads=[KT, qT], writes=[ps])
                    pT = pTr.next()
                    ACT(cx, pT[:, 0:QN], ps[:, 0:QN], AF.Exp, [ps], [pT], scale=0.125)
                    cx.fw.op("pe", _pv_fn(po, pT, V1, ch, kv, ns, ch == 0, ch == NCH - 1), reads=[pT, V1], writes=[po])
                pov = po[:, 0:ns * 128].rearrange("p (s c) -> p s c", c=128)
                rec = recr.next()
                X(cx, "dve", "reciprocal", rec[:, 0:ns], pov[:, :, 64], reads=[po], writes=[rec])
                TT_(cx, "dve", osb[:, 0:ns, h * 64:(h + 1) * 64], pov[:, :, 0:64], rec[:, 0:ns].unsqueeze(2).broadcast_to([128, ns, 64]),
                    ALU.mult, [po, rec], [osb])
            STORE(cx, cx.dr["o_att"][st + q0:st + q0 + QN, :].rearrange("(s p) c -> p s c", p=128), osb[:, 0:ns, :], osb)
    cx.ps_n = save_ps
    cx.ps_i = 0
    cx.phase_end()


def _pv_fn(po, pT, V1, ch, kv, ns, first, last):
    def fn(e):
        ins = None
        for s in range(ns):
            ins = e.matmul(po[:, s * 128:s * 128 + 65], lhsT=pT[:, s * 128:(s + 1) * 128], rhs=V1[:, ch, kv, 0:65], start=(first and s == 0), stop=last)
        return ins
    return fn


def _segments(cx, n=512):
    segs = []
    for (st, T, samp, p) in cx.seqs:
        for a in range(0, T, n):
            segs.append((st + a, min(n, T - a), st, st + T))
    return segs


def _load_halo(cx, tile_, nch, src, t0, n, s0, s1):
    lo = t0 - 1 if t0 > s0 else t0
    hi = t0 + n + 1 if t0 + n < s1 else t0 + n
    if lo == t0:
        X(cx, "dve", "memset", tile_[:, :, 0:1], 0.0, reads=[], writes=[tile_])
    if hi == t0 + n:
        X(cx, "dve", "memset", tile_[:, :, n + 1:n + 2], 0.0, reads=[], writes=[tile_])
    LOAD(cx, tile_[:, :, lo - (t0 - 1):hi - (t0 - 1)], src[:, lo:hi].rearrange("(c p) n -> p c n", p=128), tile_)


def phase_ssd(cx, l):
    TS, TT = cx.TS, cx.TT
    C = cx.C
    cx.phase_begin()
    cw = cx.sb([128, 6, 3]); cb = cx.sb([128, 6])
    LOAD(cx, cw[:], cx.inp["ssd_cwT"][l], cw)
    LOAD(cx, cb[:], cx.inp["ssd_cbT"][l], cb)
    dtb = cx.sb([128, 16]); nA = cx.sb([128, 16])
    LOAD(cx, dtb[:], cx.inp["ssd_dt_bias"][l:l + 1, :].partition_broadcast(128), dtb)
    LOAD(cx, nA[:], cx.inp["ssd_a_log"][l:l + 1, :].partition_broadcast(128), nA)
    ACT(cx, nA[:], nA[:], AF.Exp, [nA], [nA])
    TS_(cx, "dve", nA[:], nA[:], -1.0, None, ALU.mult, None, [nA], [nA])
    xbr = cx.ring(2, [128, 6, 514], F32, "xb")
    xcr = cx.ring(2, [128, 6, 512], F32, "xc")
    tr_ = cx.ring(2, [128, 512], F32, "t")
    xtr = cx.ring(2, [128, 640], F32, "xt")
    smr = cx.ring(6, [128, 16], F32, "sm")
    for (t0, n, s0, s1) in _segments(cx):
        xb = xbr.next()
        _load_halo(cx, xb, 6, cx.dr["xbcT"], t0, n, s0, s1)
        xc = xcr.next()
        for c in range(6):
            t = tr_.next()
            TS_(cx, "dve", t[:, 0:n], xb[:, c, 0:n], cw[:, c, 0:1], cb[:, c:c + 1], ALU.mult, ALU.add, [xb, cw, cb], [t])
            STT(cx, "dve", t[:, 0:n], xb[:, c, 1:n + 1], cw[:, c, 1:2], t[:, 0:n], ALU.mult, ALU.add, [xb, cw, t], [t])
            STT(cx, "dve", t[:, 0:n], xb[:, c, 2:n + 2], cw[:, c, 2:3], t[:, 0:n], ALU.mult, ALU.add, [xb, cw, t], [t])
            ACT(cx, xc[:, c, 0:n], t[:, 0:n], AF.Silu, [t], [xc])
        STORE(cx, cx.dr["bcT"][:, t0:t0 + n].rearrange("(c p) n -> p c n", p=128), xc[:, 4:6, 0:n], xc)
        for s in range(n // 128):
            psa = cx.psum()
            transposes(cx, [(psa[:, c * 128:(c + 1) * 128], xc[:, c, s * 128:(s + 1) * 128]) for c in range(4)], [xc], psa)
            psb = cx.psum()
            transposes(cx, [(psb[:, 0:128], xc[:, 4, s * 128:(s + 1) * 128])], [xc], psb)
            xt = xtr.next()
            COPY(cx, "act", xt[:, 0:512], psa[:], [psa], [xt])
            COPY(cx, "dve", xt[:, 512:640], psb[:, 0:128], [psb], [xt])
            STORE(cx, cx.dr["xs_tm"][t0 + s * 128:t0 + (s + 1) * 128, :], xt[:, 0:512], xt)
            STORE(cx, cx.dr["b_tm"][t0 + s * 128:t0 + (s + 1) * 128, :], xt[:, 512:640], xt)
            x0 = smr.next(); a0 = smr.next(); dt = smr.next()
            LOAD(cx, x0[:], cx.dr["sdt_tm"][t0 + s * 128:t0 + (s + 1) * 128, :], x0)
            TT_(cx, "dve", x0[:], x0[:], dtb[:], ALU.add, [x0, dtb], [x0])
            ACT(cx, a0[:], x0[:], AF.Abs, [x0], [a0])
            ACT(cx, a0[:], a0[:], AF.Exp, [a0], [a0], scale=-1.0)
            ACT(cx, a0[:], a0[:], AF.Ln, [a0], [a0], bias=C["one"][:, 0:1], scale=1.0)
            TS_(cx, "dve", x0[:], x0[:], 0.0, None, ALU.max, None, [x0], [x0])
            TT_(cx, "dve", dt[:], x0[:], a0[:], ALU.add, [x0, a0], [dt])
            TT_(cx, "dve", a0[:], dt[:], nA[:], ALU.mult, [dt, nA], [a0])
            STORE(cx, cx.dr["dt_tm"][t0 + s * 128:t0 + (s + 1) * 128, :], dt[:], dt)
            STORE(cx, cx.dr["ld_tm"][t0 + s * 128:t0 + (s + 1) * 128, :], a0[:], a0)
    cx.phase_end()
    import os
    if os.environ.get("SSD_STOP") == "1":
        return
    cx.phase_begin()
    S = [cx.sb([128, 256], F32, f"S{d}") for d in range(2)]
    bcr = cx.ring(3, [128, 2, 128], F32, "bc")
    btr = cx.ring(3, [128, 128], F32, "bt")
    xkr = cx.ring(3, [128, 512], F32, "xk")
    dlr = cx.ring(3, [128, 32], F32, "dl")
    rbr = cx.ring(2, [128, 8, 128], F32, "rb")
    gcr = cx.ring(2, [128, 48], F32, "gc")
    bsr = cx.ring(2, [128, 256], F32, "bs")
    der = cx.ring(3, [128, 128], F32, "de")
    atr = cx.ring(2, [128, 8, 128], F32, "at")
    tmr = cx.ring(4, [128, 512], F32, "tm")
    ones, ident = C["ones"], C["ident"]
    for (st, T, samp, p) in cx.seqs:
        NC = T // 128
        for d in range(2):
            if samp and not os.environ.get("SSD_NOSTATE"):
                for g in range(2):
                    LOAD(cx, S[d][g * 64:(g + 1) * 64, :].rearrange("n (h p) -> n h p", p=64),
                         cx.inp["state_ssd"][l, d, g * 4:(g + 1) * 4].rearrange("h n p -> n h p"), S[d])
            else:
                X(cx, "dve", "memset", S[d][:], 0.0, reads=[], writes=[S[d]])
        for step in range(NC):
            for d in range(2):
                ck = step if d == 0 else NC - 1 - step
                k0 = st + ck * 128
                CUM = C["cum128f"] if d == 0 else C["cum128b"]
                NEGM = C["negmf"] if d == 0 else C["negmb"]
                end = 127 if d == 0 else 0
                Sd = S[d]
                bc = bcr.next(); bt = btr.next(); xk = xkr.next(); dl = dlr.next()
                LOAD(cx, bc[:], cx.dr["bcT"][:, k0:k0 + 128].rearrange("(c p) n -> p c n", p=128), bc)
                LOAD(cx, bt[:], cx.dr["b_tm"][k0:k0 + 128, :], bt)
                LOAD(cx, xk[:], cx.dr["xs_tm"][k0:k0 + 128, :], xk)
                LOAD(cx, dl[:, 0:16], cx.dr["dt_tm"][k0:k0 + 128, :], dl)
                LOAD(cx, dl[:, 16:32], cx.dr["ld_tm"][k0:k0 + 128, :], dl)
                dtc = dl[:, d * 8:(d + 1) * 8]
                ldc = dl[:, 16 + d * 8:16 + (d + 1) * 8]
                rb = rbr.next()
                TT_(cx, "dve", rb[:], CUM[:].unsqueeze(1).broadcast_to([128, 8, 128]), ldc.unsqueeze(2).broadcast_to([128, 8, 128]),
                    ALU.mult, [CUM, dl], [rb])
                if int(os.environ.get("SSD_STEPS", 99)) <= 1:
                    continue
                pg = [cx.psum(), cx.psum()]
                for half in range(2):
                    mm_group(cx, pg[half][:], [(ones[:], rb[:, half * 4:(half + 1) * 4, :].rearrange("p h i -> p (h i)")), (ident[:], NEGM[:])],
                             [rb, ones, ident, NEGM], pg[half])
                if int(os.environ.get("SSD_STEPS", 99)) <= 2:
                    continue
                pc = cx.psum()
                X(cx, "pe", "matmul", pc[:, 0:8], lhsT=CUM[:], rhs=ldc, start=True, stop=True, reads=[CUM, dl], writes=[pc])
                gc = gcr.next()
                COPY(cx, "dve", gc[:, 0:8], pc[:, 0:8], [pc], [gc])
                TS_(cx, "dve", gc[:, 8:16], gc[:, 0:8], -1.0, None, ALU.mult, None, [gc], [gc])
                ACT(cx, gc[:, 16:24], gc[:, 0:8], AF.Exp, [gc], [gc])
                for half in range(2):
                    gend = pg[half][:, :].rearrange("p (h i) -> p h i", i=128)[:, :, end]
                    COPY(cx, "dve", gc[:, 24 + half * 4:28 + half * 4], gend, [pg[half]], [gc])
                ACT(cx, gc[:, 40:48], gc[:, 24:32], AF.Exp, [gc], [gc])
                TT_(cx, "dve", gc[:, 32:40], gc[:, 24:32], gc[:, 0:8], ALU.subtract, [gc], [gc])
                ACT(cx, gc[:, 32:40], gc[:, 32:40], AF.Exp, [gc], [gc])
                TT_(cx, "dve", gc[:, 32:40], gc[:, 32:40], dtc, ALU.mult, [gc, dl], [gc])
                if int(os.environ.get("SSD_STEPS", 99)) <= 3:
                    continue
                bs = bsr.next()
                for g in range(2):
                    pbc = cx.psum()
                    X(cx, "pe", "matmul", pbc[:, 0:128], lhsT=bc[g * 64:(g + 1) * 64, 0, :], rhs=bc[g * 64:(g + 1) * 64, 1, :],
                      start=True, stop=True, reads=[bc], writes=[pbc])
                    COPY(cx, "act", bs[:, g * 128:(g + 1) * 128], pbc[:, 0:128], [pbc], [bs])
                if int(os.environ.get("SSD_STEPS", 99)) <= 4:
                    continue
                at = atr.next()
                for h in range(8):
                    de = der.next()
                    ACT(cx, de[:], pg[h // 4][:, (h % 4) * 128:(h % 4 + 1) * 128], AF.Exp, [pg[h // 4], gc], [de], bias=gc[:, 8 + h:9 + h], scale=1.0)
                    STT(cx, "dve", at[:, h, :], de[:], dtc[:, h:h + 1], bs[:, (h // 4) * 128:(h // 4 + 1) * 128], ALU.mult, ALU.mult, [de, dl, bs], [at])
                if int(os.environ.get("SSD_STEPS", 99)) <= 5:
                    continue
                po = cx.psum()
                cx.fw.op("pe", _ssd_intra_fn(po, at, xk), reads=[at, xk], writes=[po])
                pis = []
                for g in range(2):
                    pi = cx.psum()
                    X(cx, "pe", "matmul", pi[:, 0:256], lhsT=bc[g * 64:(g + 1) * 64, 1, :], rhs=Sd[g * 64:(g + 1) * 64, :],
                      start=True, stop=True, reads=[bc, Sd], writes=[pi])
                    pis.append(pi)
                if int(os.environ.get("SSD_STEPS", 99)) <= 7:
                    continue
                t1 = tmr.next()
                for g in range(2):
                    TT_(cx, "dve", t1[:, g * 256:(g + 1) * 256].rearrange("p (h q) -> p h q", q=64), pis[g][:, 0:256].rearrange("p (h q) -> p h q", q=64),
                        gc[:, 16 + g * 4:20 + g * 4].unsqueeze(2).broadcast_to([128, 4, 64]), ALU.mult, [pis[g], gc], [t1])
                TT_(cx, "dve", t1[:], t1[:], po[:], ALU.add, [t1, po], [t1])
                STORE(cx, cx.dr["o_ssd_d"][d, k0:k0 + 128, :], t1[:], t1)
                if int(os.environ.get("SSD_STEPS", 99)) <= 8:
                    continue
                xs2 = tmr.next()
                TT_(cx, "dve", xs2[:].rearrange("p (h q) -> p h q", q=64), xk[:].rearrange("p (h q) -> p h q", q=64),
                    gc[:, 32:40].unsqueeze(2).broadcast_to([128, 8, 64]), ALU.mult, [xk, gc], [xs2])
                pst = cx.psum()
                X(cx, "pe", "matmul", pst[:], lhsT=bt[:], rhs=xs2[:], start=True, stop=True, reads=[bt, xs2], writes=[pst])
                for g in range(2):
                    Sg = Sd[g * 64:(g + 1) * 64, :]
                    TT_(cx, "dve", Sg.rearrange("n (h q) -> n h q", q=64), Sg.rearrange("n (h q) -> n h q", q=64),
                        gc[g * 64:(g + 1) * 64, 40 + g * 4:44 + g * 4].unsqueeze(2).broadcast_to([64, 4, 64]), ALU.mult, [Sd, gc], [Sd])
                    TT_(cx, "dve", Sg, Sg, pst[g * 64:(g + 1) * 64, g * 256:(g + 1) * 256], ALU.add, [Sd, pst], [Sd])
        if not samp and not os.environ.get("SSD_NOSTATE"):
            for d in range(2):
                for g in range(2):
                    STORE(cx, cx.out["new_ssd"][p, l, d, g * 4:(g + 1) * 4].rearrange("h n p -> n h p"),
                          S[d][g * 64:(g + 1) * 64, :].rearrange("n (h p) -> n h p", p=64), S[d])
    cx.phase_end()
    if os.environ.get("SSD_STOP") == "2":
        return
    cx.phase_begin()
    Db = cx.sb([128, 8]); nb = cx.sb([128, 512])
    LOAD(cx, Db[:], cx.inp["ssd_d"][l:l + 1, :].partition_broadcast(128), Db)
    LOAD(cx, nb[:], cx.inp["ssd_norm"][l:l + 1, :].partition_broadcast(128), nb)
    inr = cx.ring(8, [128, 512], F32, "in")
    ssr = cx.ring(4, [128, 1], F32, "ss")
    for b0 in range(0, TT, 128):
        of = inr.next(); ob = inr.next(); xs = inr.next(); sz = inr.next()
        LOAD(cx, of[:], cx.dr["o_ssd_d"][0, b0:b0 + 128, :], of)
        LOAD(cx, ob[:], cx.dr["o_ssd_d"][1, b0:b0 + 128, :], ob)
        LOAD(cx, xs[:], cx.dr["xs_tm"][b0:b0 + 128, :], xs)
        LOAD(cx, sz[:], cx.dr["sz_tm"][b0:b0 + 128, :], sz)
        TT_(cx, "dve", of[:], of[:], ob[:], ALU.add, [of, ob], [of])
        TT_(cx, "dve", xs[:].rearrange("p (h q) -> p h q", q=64), xs[:].rearrange("p (h q) -> p h q", q=64),
            Db[:].unsqueeze(2).broadcast_to([128, 8, 64]), ALU.mult, [xs, Db], [xs])
        TT_(cx, "dve", of[:], of[:], xs[:], ALU.add, [of, xs], [of])
        ACT(cx, sz[:], sz[:], AF.Silu, [sz], [sz])
        TT_(cx, "dve", of[:], of[:], sz[:], ALU.mult, [of, sz], [of])
        ss = ssr.next()
        ACT(cx, ob[:], of[:], AF.Square, [of], [ob, ss], accum_out=ss[:, 0:1])
        ACT(cx, ss[:], ss[:], AF.Sqrt, [ss], [ss], bias=C["eps6"][:, 0:1], scale=1.0 / 512)
        X(cx, "dve", "reciprocal", ss[:], ss[:], reads=[ss], writes=[ss])
        STT(cx, "dve", of[:], of[:], ss[:, 0:1], nb[:], ALU.mult, ALU.mult, [of, ss, nb], [of])
        STORE(cx, cx.dr["o_ssd"][b0:b0 + 128, :], of[:], of)
    cx.phase_end()


def _ssd_intra_fn(po, at, xk):
    def fn(e):
        ins = None
        for h in range(8):
            ins = e.matmul(po[:, h * 64:(h + 1) * 64], lhsT=at[:, h, :], rhs=xk[:, h * 64:(h + 1) * 64], start=True, stop=True)
        return ins
    return fn


def phase_gla(cx, l):
    TS, TT = cx.TS, cx.TT
    C = cx.C
    cx.phase_begin()
    g2 = cx.sb([16, 2, 256]); gb = cx.sb([128, 2, 256])
    LOAD(cx, g2[:], cx.inp["gla_g2"][l].rearrange("z r k -> r z k"), g2)
    for d in range(2):
        LOAD(cx, gb[:, d, :], cx.inp["gla_gb"][l, d:d + 1, :].partition_broadcast(128), gb)
    S = [[cx.sb([128, 128], F32, f"S{d}{pr}") for pr in range(2)] for d in range(2)]
    qkr = cx.ring(3, [128, 4, 128], F32, "qk")
    gkr = cx.ring(3, [128, 256], F32, "gk")
    gvr = cx.ring(3, [128, 512], F32, "gv")
    ggr = cx.ring(3, [16, 128], F32, "gg")
    t2r = cx.ring(8, [128, 256], F32, "t2")
    atr = cx.ring(2, [128, 4, 128], F32, "at")
    osr = cx.ring(2, [128, 512], F32, "os")
    for (st, T, samp, p) in cx.seqs:
        NC = T // 128
        for d in range(2):
            for pr in range(2):
                if samp:
                    LOAD(cx, S[d][pr][:], cx.inp["state_gla"][l, d, 2 * pr:2 * pr + 2].rearrange("h k v -> (h k) v"), S[d][pr])
                else:
                    X(cx, "dve", "memset", S[d][pr][:], 0.0, reads=[], writes=[S[d][pr]])
        for step in range(NC):
            for d in range(2):
                ck = step if d == 0 else NC - 1 - step
                k0 = st + ck * 128
                CUM = C["cum128f"] if d == 0 else C["cum128b"]
                SFX = C["sfx128f"] if d == 0 else C["sfx128b"]
                end = 127 if d == 0 else 0
                qk = qkr.next(); gk = gkr.next(); gv = gvr.next(); gg = ggr.next()
                LOAD(cx, qk[:], cx.dr["gqkT"][:, k0:k0 + 128].rearrange("(c p) n -> p c n", p=128), qk)
                LOAD(cx, gk[:], cx.dr["gk_tm"][k0:k0 + 128, :], gk)
                LOAD(cx, gv[:], cx.dr["gv_tm"][k0:k0 + 128, :], gv)
                LOAD(cx, gg[:], cx.dr["gglT"][:, k0:k0 + 128], gg)
                pl = cx.psum()
                X(cx, "pe", "matmul", pl[:, 0:256], lhsT=gg[:], rhs=g2[:, d, :], start=True, stop=True, reads=[gg, g2], writes=[pl])
                lg = t2r.next(); ab = t2r.next()
                TT_(cx, "dve", lg[:], pl[:, 0:256], gb[:, d, :], ALU.add, [pl, gb], [lg])
                ACT(cx, ab[:], lg[:], AF.Abs, [lg], [ab])
                ACT(cx, ab[:], ab[:], AF.Exp, [ab], [ab], scale=-1.0)
                ACT(cx, ab[:], ab[:], AF.Ln, [ab], [ab], bias=C["one"][:, 0:1], scale=1.0)
                TS_(cx, "dve", lg[:], lg[:], 0.0, None, ALU.min, None, [lg], [lg])
                TT_(cx, "dve", lg[:], lg[:], ab[:], ALU.subtract, [lg, ab], [lg])
                TS_(cx, "dve", lg[:], lg[:], 1.0 / 16.0, None, ALU.mult, None, [lg], [lg])
                pg = cx.psum()
                for pr in range(2):
                    X(cx, "pe", "matmul", pg[:, pr * 128:(pr + 1) * 128], lhsT=lg[:, pr * 128:(pr + 1) * 128], rhs=CUM[:], start=True, stop=True,
                      reads=[lg, CUM], writes=[pg])
                E = t2r.next(); Ei = t2r.next()
                ACT(cx, E[:], pg[:, 0:256], AF.Exp, [pg], [E])
                ACT(cx, Ei[:], pg[:, 0:256], AF.Exp, [pg], [Ei], scale=-1.0)
                qt = t2r.next(); kt = t2r.next()
                STT(cx, "dve", qt[:], qk[:, 0:2, :].rearrange("p c n -> p (c n)"), 0.125, E[:], ALU.mult, ALU.mult, [qk, E], [qt])
                TT_(cx, "dve", kt[:], qk[:, 2:4, :].rearrange("p c n -> p (c n)"), Ei[:], ALU.mult, [qk, Ei], [kt])
                pd = cx.psum()
                X(cx, "pe", "matmul", pd[:, 0:256], lhsT=SFX[:], rhs=lg[:], start=True, stop=True, reads=[SFX, lg], writes=[pd])
                kd = t2r.next()
                ACT(cx, kd[:], pd[:, 0:256], AF.Exp, [pd], [kd])
                TT_(cx, "dve", kd[:], kd[:], gk[:], ALU.mult, [kd, gk], [kd])
                at = atr.next()
                po = cx.psum()
                for h in range(4):
                    pr, base = h // 2, (h % 2) * 64
                    pa = cx.psum()
                    X(cx, "pe", "matmul", pa[:, 0:128], lhsT=kt[base:base + 64, pr * 128:(pr + 1) * 128], rhs=qt[base:base + 64, pr * 128:(pr + 1) * 128],
                      start=True, stop=True, reads=[kt, qt], writes=[pa])
                    TT_(cx, "dve", at[:, h, :], pa[:, 0:128], CUM[:], ALU.mult, [pa, CUM], [at])
                    mm_group(cx, po[:, h * 128:(h + 1) * 128],
                             [(at[:, h, :], gv[:, h * 128:(h + 1) * 128]),
                              (qt[base:base + 64, pr * 128:(pr + 1) * 128], S[d][pr][base:base + 64, :])], [at, gv, qt, S[d][pr]], po)
                osb = osr.next()
                COPY(cx, "act", osb[:], po[:], [po], [osb])
                STORE(cx, cx.dr["o_gla_d"][d, k0:k0 + 128, :], osb[:], osb)
                pst = cx.psum()
                for pr in range(2):
                    for hl in range(2):
                        h = 2 * pr + hl
                        X(cx, "pe", "matmul", pst[:, h * 128:(h + 1) * 128], lhsT=kd[:, pr * 128:(pr + 1) * 128], rhs=gv[:, h * 128:(h + 1) * 128],
                          start=True, stop=True, reads=[kd, gv], writes=[pst])
                for pr in range(2):
                    for hl in range(2):
                        h = 2 * pr + hl
                        rows = slice(hl * 64, (hl + 1) * 64)
                        STT(cx, "dve", S[d][pr][rows, :], S[d][pr][rows, :], E[rows, pr * 128 + end:pr * 128 + end + 1], pst[rows, h * 128:(h + 1) * 128],
                            ALU.mult, ALU.add, [S[d][pr], E, pst], [S[d][pr]])
        if not samp:
            for d in range(2):
                for pr in range(2):
                    STORE(cx, cx.out["new_gla"][p, l, d, 2 * pr:2 * pr + 2].rearrange("h k v -> (h k) v"), S[d][pr][:], S[d][pr])
    cx.phase_end()
    cx.phase_begin()
    nb = cx.sb([128, 128])
    LOAD(cx, nb[:], cx.inp["gla_norm"][l:l + 1, :].partition_broadcast(128), nb)
    inr = cx.ring(8, [128, 512], F32, "in")
    ssr = cx.ring(4, [128, 4], F32, "ss")
    for b0 in range(0, TT, 128):
        of = inr.next(); ob = inr.next(); go = inr.next()
        LOAD(cx, of[:], cx.dr["o_gla_d"][0, b0:b0 + 128, :], of)
        LOAD(cx, ob[:], cx.dr["o_gla_d"][1, b0:b0 + 128, :], ob)
        LOAD(cx, go[:], cx.dr["gog_tm"][b0:b0 + 128, :], go)
        TT_(cx, "dve", of[:], of[:], ob[:], ALU.add, [of, ob], [of])
        ACT(cx, ob[:], of[:], AF.Square, [of], [ob])
        ss = ssr.next()
        X(cx, "dve", "tensor_reduce", ss[:], ob[:].rearrange("p (h q) -> p h q", q=128), AX.X, ALU.add, reads=[ob], writes=[ss])
        ACT(cx, ss[:], ss[:], AF.Sqrt, [ss], [ss], bias=C["eps6"][:, 0:1], scale=1.0 / 128)
        X(cx, "dve", "reciprocal", ss[:], ss[:], reads=[ss], writes=[ss])
        o3 = of[:].rearrange("p (h q) -> p h q", q=128)
        TT_(cx, "dve", o3, o3, ss[:].unsqueeze(2).broadcast_to([128, 4, 128]), ALU.mult, [of, ss], [of])
        TT_(cx, "dve", o3, o3, nb[:].unsqueeze(1).broadcast_to([128, 4, 128]), ALU.mult, [of, nb], [of])
        ACT(cx, go[:], go[:], AF.Silu, [go], [go])
        TT_(cx, "dve", of[:], of[:], go[:], ALU.mult, [of, go], [of])
        STORE(cx, cx.dr["o_gla"][b0:b0 + 128, :], of[:], of)
    cx.phase_end()


def phase_rwkv(cx, l):
    TS, TT = cx.TS, cx.TT
    C = cx.C
    fw = cx.fw
    cx.phase_begin()
    mu = cx.sb([128, 14]); om = cx.sb([128, 14]); hm = cx.sb([128, 14])
    LOAD(cx, mu[:], cx.inp["rw_muT"][l], mu)
    TS_(cx, "dve", om[:], mu[:], -1.0, 1.0, ALU.mult, ALU.add, [mu], [om])
    TS_(cx, "dve", hm[:], mu[:], 0.5, None, ALU.mult, None, [mu], [hm])
    pp = cx.sb([128, 5, 4])
    for i, nm in enumerate(("rw_a0T", "rw_kkT", "rw_kaT", "rw_rkT")):
        LOAD(cx, pp[:, i, :], cx.inp[nm][l], pp)
    TS_(cx, "dve", pp[:, 4, :], pp[:, 2, :], -1.0, 1.0, ALU.mult, ALU.add, [pp], [pp])
    w0b = cx.sb([128, 2, 512])
    for d in range(2):
        LOAD(cx, w0b[:, d, :], cx.inp["rw_w0"][l, d:d + 1, :].partition_broadcast(128), w0b)
    w2s = cx.sb([64, 2, 512])
    LOAD(cx, w2s[:], cx.inp["rw_w2"][l].rearrange("z r c -> r z c"), w2s)
    a2s = cx.sb([128, 512])
    LOAD(cx, a2s[64:128, :], cx.inp["rw_a2"][l], a2s)
    g2s = cx.sb([128, 512])
    LOAD(cx, g2s[:], cx.inp["rw_g2"][l], g2s)
    rbr = cx.ring(1, [128, 14, 514], F32, "rb")
    rs = cx.sb([128, 14, 512], F32, "rs")
    t5r = cx.ring(3, [128, 512], F32, "t5")
    aT = cx.sb([128, 4, 512], F32, "aT")
    kkn = cx.sb([128, 4, 512], F32, "kkn")
    kp = cx.sb([128, 4, 512], F32, "kp")
    bet = cx.sb([128, 4, 512], F32, "bet")
    big = cx.ring(2, [128, 4, 512], F32, "big")
    twl = cx.sb([64, 512], F32, "twl")
    sgl = cx.sb([128, 512], F32, "sgl")
    o5r = cx.ring(4, [128, 512], F32, "o5")
    s8r = cx.ring(2, [128, 8], F32, "s8")
    bones = C["bones"]
    for (t0, n, s0, s1) in _segments(cx):
        rb = rbr.next()
        _load_halo(cx, rb, 14, cx.dr["rblkT"], t0, n, s0, s1)
        for c in range(14):
            t = t5r.next()
            TT_(cx, "dve", t[:, 0:n], rb[:, c, 0:n], rb[:, c, 2:n + 2], ALU.add, [rb], [t])
            TS_(cx, "dve", rs[:, c, 0:n], rb[:, c, 1:n + 1], om[:, c:c + 1], None, ALU.mult, None, [rb, om], [rs])
            STT(cx, "dve", rs[:, c, 0:n], t[:, 0:n], hm[:, c:c + 1], rs[:, c, 0:n], ALU.mult, ALU.add, [t, hm, rs], [rs])
        for c4 in range(4):
            ps = cx.psum()
            X(cx, "pe", "matmul", ps[:, 0:n], lhsT=a2s[64:128, c4 * 128:(c4 + 1) * 128], rhs=rs[64:128, 12, 0:n], start=True, stop=True,
              reads=[a2s, rs], writes=[ps])
            ACT(cx, aT[:, c4, 0:n], ps[:, 0:n], AF.Sigmoid, [ps, pp], [aT], bias=pp[:, 0, c4:c4 + 1], scale=1.0)
        kr = big.next()
        TT_(cx, "dve", kr[:, :, 0:n], rs[:, 4:8, 0:n], pp[:, 1, :].unsqueeze(2).broadcast_to([128, 4, n]), ALU.mult, [rs, pp], [kr])
        for c4 in range(4):
            sq = t5r.next()
            ACT(cx, sq[:, 0:n], kr[:, c4, 0:n], AF.Square, [kr], [sq])
            ps = cx.psum()
            X(cx, "pe", "matmul", ps[:, 0:n], lhsT=bones[:], rhs=sq[:, 0:n], start=True, stop=True, reads=[sq, bones], writes=[ps])
            sd = t5r.next()
            ACT(cx, sd[:, 0:n], ps[:, 0:n], AF.Sqrt, [ps], [sd], bias=C["eps12"][:, 0:1], scale=1.0)
            X(cx, "dve", "reciprocal", sd[:, 0:n], sd[:, 0:n], reads=[sd], writes=[sd])
            TT_(cx, "dve", kkn[:, c4, 0:n], kr[:, c4, 0:n], sd[:, 0:n], ALU.mult, [kr, sd], [kkn])
        tb = big.next()
        TT_(cx, "dve", tb[:, :, 0:n], aT[:, :, 0:n], pp[:, 2, :].unsqueeze(2).broadcast_to([128, 4, n]), ALU.mult, [aT, pp], [tb])
        TT_(cx, "dve", tb[:, :, 0:n], tb[:, :, 0:n], pp[:, 4, :].unsqueeze(2).broadcast_to([128, 4, n]), ALU.add, [tb, pp], [tb])
        TT_(cx, "dve", kp[:, :, 0:n], rs[:, 4:8, 0:n], tb[:, :, 0:n], ALU.mult, [rs, tb], [kp])
        TT_(cx, "dve", bet[:, :, 0:n], kkn[:, :, 0:n], aT[:, :, 0:n], ALU.mult, [kkn, aT], [bet])
        for q, src in enumerate((None, kp, kkn, bet)):
            sap = rs[:, 0:4, 0:n] if src is None else src[:, :, 0:n]
            STORE(cx, cx.dr["rw_fm"][q, :, t0:t0 + n].rearrange("(c p) n -> p c n", p=128), sap, rs if src is None else src)
        pr_ = big.next()
        TT_(cx, "dve", pr_[:, :, 0:n], rs[:, 0:4, 0:n], kp[:, :, 0:n], ALU.mult, [rs, kp], [pr_])
        TT_(cx, "dve", pr_[:, :, 0:n], pr_[:, :, 0:n], pp[:, 3, :].unsqueeze(2).broadcast_to([128, 4, n]), ALU.mult, [pr_, pp], [pr_])
        ACT(cx, twl[:, 0:n], rs[0:64, 12, 0:n], AF.Tanh, [rs], [twl])
        ACT(cx, sgl[:, 0:n], rs[:, 13, 0:n], AF.Sigmoid, [rs], [sgl])
        for s in range(n // 128):
            b0 = t0 + s * 128
            sl = slice(s * 128, (s + 1) * 128)
            ps = cx.psum()
            for c4 in range(4):
                X(cx, "pe", "matmul", ps[:, c4 * 2:(c4 + 1) * 2], lhsT=pr_[:, c4, sl], rhs=C["ind2"][:], start=True, stop=True,
                  reads=[pr_, C["ind2"]], writes=[ps])
            s8 = s8r.next()
            COPY(cx, "dve", s8[:], ps[:, 0:8], [ps], [s8])
            STORE(cx, cx.dr["rw_s_tm"][b0:b0 + 128, :], s8[:], s8)
            pv = cx.psum()
            transposes(cx, [(pv[:, c4 * 128:(c4 + 1) * 128], rs[:, 8 + c4, sl]) for c4 in range(4)], [rs], pv)
            o5 = o5r.next()
            COPY(cx, "act", o5[:], pv[:], [pv], [o5])
            STORE(cx, cx.dr["rw_v_tm"][b0:b0 + 128, :], o5[:], o5)
            pgm = cx.psum()
            X(cx, "pe", "matmul", pgm[:], lhsT=sgl[:, sl], rhs=g2s[:], start=True, stop=True, reads=[sgl, g2s], writes=[pgm])
            o5 = o5r.next()
            COPY(cx, "act", o5[:], pgm[:], [pgm], [o5])
            STORE(cx, cx.dr["rw_g_tm"][b0:b0 + 128, :], o5[:], o5)
            for d in range(2):
                pw = cx.psum()
                X(cx, "pe", "matmul", pw[:], lhsT=twl[:, sl], rhs=w2s[:, d, :], start=True, stop=True, reads=[twl, w2s], writes=[pw])
                o5 = o5r.next()
                TT_(cx, "dve", o5[:], pw[:], w0b[:, d, :], ALU.add, [pw, w0b], [o5])
                ACT(cx, o5[:], o5[:], AF.Sigmoid, [o5], [o5])
                TS_(cx, "dve", o5[:], o5[:], -0.6065306597126334, None, ALU.mult, None, [o5], [o5])
                STORE(cx, cx.dr["rw_lw_tm"][d, b0:b0 + 128, :], o5[:], o5)
    cx.phase_end()
    cx.phase_begin()
    NU = 8
    S0 = [[cx.sb([128, 64], F32, f"S{d}{pr}") for pr in range(4)] for d in range(2)]
    U = []
    for u in range(NU):
        t = {}
        for nm, shp in (("AR", [128, 256]), ("Bb", [128, 128]), ("Kb", [128, 128]), ("A1", [128, 256]), ("A2", [128, 256]), ("Mq", [128, 128]),
                        ("PQ0", [128, 256]), ("PQ1", [128, 256]), ("X0", [128, 64]), ("X1", [128, 64]), ("EE", [128, 128]), ("E2", [128, 64]),
                        ("KT", [128, 128]), ("BT", [128, 128]), ("tmp", [128, 64])):
            t[nm] = cx.sb(shp, F32, nm)
        for nm in ("AR", "Bb", "Kb"):
            X(cx, "dve", "memset", t[nm][:], 0.0, reads=[], writes=[t[nm]])
        U.append(t)
    x4r = [cx.ring(2, [128, 4, 4, 64], F32, "x4") for d in range(2)]
    lwr = [cx.ring(2, [64, 512], F32, "lw") for d in range(2)]
    vsr = [cx.ring(2, [128, 4, 64], F32, "vs") for d in range(2)]
    osr = [cx.ring(2, [128, 4, 64], F32, "os") for d in range(2)]
    ident = C["ident"]

    def unit(d, pr, T_, x4, lw, vs, osb, end):
        CUMc = C["cum64f"] if d == 0 else C["cum64b"]
        MSK = C["msk64f"] if d == 0 else C["msk64b"]
        MST = C["mst64f"] if d == 0 else C["mst64b"]
        S = S0[d][pr]
        AR, Bb, Kb = T_["AR"], T_["Bb"], T_["Kb"]
        EE, E2 = T_["EE"], T_["E2"]
        pG = cx.psum()
        X(cx, "pe", "matmul", pG[:, 0:128], lhsT=lw[:, pr * 128:(pr + 1) * 128], rhs=CUMc[:], start=True, stop=True, reads=[lw, CUMc], writes=[pG])
        ACT(cx, EE[:], pG[:, 0:128], AF.Exp, [pG], [EE])
        ACT(cx, E2[:], pG[:, 0:64], AF.Exp, [pG], [E2], scale=-1.0)
        yield
        for hl in range(2):
            r_ = slice(hl * 64, (hl + 1) * 64)
            c_ = slice(hl * 64, (hl + 1) * 64)
            c2 = slice(128 + hl * 64, 128 + (hl + 1) * 64)
            STT(cx, "dve", AR[r_, c_], x4[r_, 2, pr, :], -1.0, EE[r_, 64:128], ALU.mult, ALU.mult, [x4, EE], [AR])
            TT_(cx, "dve", AR[r_, c2], x4[r_, 0, pr, :], EE[r_, 0:64], ALU.mult, [x4, EE], [AR])
            TT_(cx, "dve", Bb[r_, c_], x4[r_, 3, pr, :], E2[r_, :], ALU.mult, [x4, E2], [Bb])
            TT_(cx, "dve", Kb[r_, c_], x4[r_, 1, pr, :], E2[r_, :], ALU.mult, [x4, E2], [Kb])
        yield
        p1 = cx.psum()
        X(cx, "pe", "matmul", p1[:, 0:256], lhsT=Bb[:], rhs=AR[:], start=True, stop=True, reads=[Bb, AR], writes=[p1])
        p2 = cx.psum()
        X(cx, "pe", "matmul", p2[:, 0:256], lhsT=Kb[:], rhs=AR[:], start=True, stop=True, reads=[Kb, AR], writes=[p2])
        p3 = cx.psum()
        X(cx, "pe", "matmul", p3[:, 0:128], lhsT=AR[:, 0:128], rhs=Bb[:], start=True, stop=True, reads=[Bb, AR], writes=[p3])
        A1, A2, Mq = T_["A1"], T_["A2"], T_["Mq"]
        TT_(cx, "dve", A1[:], p1[:, 0:256], MSK[:], ALU.mult, [p1, MSK], [A1])
        TT_(cx, "dve", A2[:], p2[:, 0:256], MSK[:], ALU.mult, [p2, MSK], [A2])
        TT_(cx, "dve", Mq[:], p3[:, 0:128], MST[:], ALU.mult, [p3, MST], [Mq])
        yield
        pr_ = cx.psum()
        mm_group(cx, pr_[:, 0:64], [(AR[:, 0:128], S[:]), (A2[:, 0:128], vs[:, pr, :])], [AR, S, A2, vs], pr_)
        Xc = T_["X0"]
        COPY(cx, "act", Xc[:], pr_[:, 0:64], [pr_], [Xc])
        yield
        P_ap, Q_ap = A1[:, 0:128], Mq[:]
        P_t, Q_t = A1, Mq
        PQ = [T_["PQ0"], T_["PQ1"]]
        Xn = T_["X1"]
        for lev in range(6):
            px = cx.psum()
            mm_group(cx, px[:, 0:64], [(ident[:], Xc[:]), (P_ap, Xc[:])], [ident, Xc, P_t], px)
            COPY(cx, "act", Xn[:], px[:, 0:64], [px], [Xn])
            Xc, Xn = Xn, Xc
            if lev < 5:
                pp_ = cx.psum()
                X(cx, "pe", "matmul", pp_[:, 0:128], lhsT=Q_ap, rhs=P_ap, start=True, stop=True, reads=[P_t, Q_t], writes=[pp_])
                X(cx, "pe", "matmul", pp_[:, 128:256], lhsT=P_ap, rhs=Q_ap, start=True, stop=True, reads=[P_t, Q_t], writes=[pp_])
                nt = PQ[lev % 2]
                COPY(cx, "dve", nt[:], pp_[:, 0:256], [pp_], [nt])
                P_ap, Q_ap = nt[:, 0:128], nt[:, 128:256]
                P_t = Q_t = nt
            yield
        Uc = Xc
        po = cx.psum()
        mm_group(cx, po[:, 0:64], [(AR[:, 128:256], S[:]), (A2[:, 128:256], vs[:, pr, :]), (A1[:, 128:256], Uc[:])], [AR, S, A2, vs, A1, Uc], po)
        COPY(cx, "act", osb[:, pr, :], po[:, 0:64], [po], [osb])
        yield
        KT, BT = T_["KT"], T_["BT"]
        pk = cx.psum()
        transposes(cx, [(pk[:, 0:128], Kb[:]), (pk[:, 128:256], Bb[:])], [Kb, Bb], pk)
        COPY(cx, "act", KT[:], pk[:, 0:128], [pk], [KT])
        COPY(cx, "act", BT[:], pk[:, 128:256], [pk], [BT])
        pst = cx.psum()
        mm_group(cx, pst[:, 0:64], [(KT[:], vs[:, pr, :]), (BT[:], Uc[:])], [KT, BT, vs, Uc], pst)
        tmp = T_["tmp"]
        TT_(cx, "dve", tmp[:], pst[:, 0:64], S[:], ALU.add, [pst, S], [tmp])
        TS_(cx, "dve", S[:], tmp[:], EE[:, end:end + 1], None, ALU.mult, None, [tmp, EE], [S])
        yield

    for (st, T, samp, p) in cx.seqs:
        NC = T // 64
        for d in range(2):
            for pr in range(4):
                if samp:
                    LOAD(cx, S0[d][pr][:], cx.inp["state_rwkvT"][l, d, 2 * pr:2 * pr + 2].rearrange("h k v -> (h k) v"), S0[d][pr])
                else:
                    X(cx, "dve", "memset", S0[d][pr][:], 0.0, reads=[], writes=[S0[d][pr]])
        for step in range(NC):
            gens = []
            stores = []
            for d in range(2):
                ck = step if d == 0 else NC - 1 - step
                k0 = st + ck * 64
                end = 63 if d == 0 else 0
                x4 = x4r[d].next(); lw = lwr[d].next(); vs = vsr[d].next(); osb = osr[d].next()
                for q_ in range(4):
                    LOAD(cx, x4[:, q_, :, :], cx.dr["rw_fm"][q_, :, k0:k0 + 64].rearrange("(c p) n -> p c n", p=128), x4)
                LOAD(cx, lw[:], cx.dr["rw_lw_tm"][d, k0:k0 + 64, :], lw)
                for hl in range(2):
                    LOAD(cx, vs[hl * 64:(hl + 1) * 64, :, :],
                         cx.dr["rw_v_tm"][k0:k0 + 64, :].rearrange("t (pr hl v) -> t hl pr v", hl=2, v=64)[:, hl], vs)
                for pr in range(4):
                    gens.append(unit(d, pr, U[d * 4 + pr], x4, lw, vs, osb, end))
                stores.append((d, k0, osb))
            while gens:
                for g in list(gens):
                    try:
                        next(g)
                    except StopIteration:
                        gens.remove(g)
            for (d, k0, osb) in stores:
                for hl in range(2):
                    STORE(cx, cx.dr["o_rw_d"][d, k0:k0 + 64, :].rearrange("t (pr hl v) -> t hl pr v", hl=2, v=64)[:, hl],
                          osb[hl * 64:(hl + 1) * 64, :, :], osb)
        if not samp:
            for d in range(2):
                for pr in range(4):
                    STORE(cx, cx.out["new_rwkvT"][p, l, d, 2 * pr:2 * pr + 2].rearrange("h k v -> (h k) v"), S0[d][pr][:], S0[d][pr])
    cx.phase_end()
    cx.phase_begin()
    lg = cx.sb([128, 512]); lb = cx.sb([128, 512])
    LOAD(cx, lg[:], cx.inp["rw_ln_g"][l:l + 1, :].partition_broadcast(128), lg)
    LOAD(cx, lb[:], cx.inp["rw_ln_b"][l:l + 1, :].partition_broadcast(128), lb)
    inr = cx.ring(10, [128, 512], F32, "in")
    ssr = cx.ring(6, [128, 8], F32, "ss")
    for b0 in range(0, TT, 128):
        of = inr.next(); ob = inr.next(); vt = inr.next(); gt = inr.next(); sq = inr.next()
        s8 = ssr.next(); mean = ssr.next(); var = ssr.next()
        LOAD(cx, of[:], cx.dr["o_rw_d"][0, b0:b0 + 128, :], of)
        LOAD(cx, ob[:], cx.dr["o_rw_d"][1, b0:b0 + 128, :], ob)
        LOAD(cx, vt[:], cx.dr["rw_v_tm"][b0:b0 + 128, :], vt)
        LOAD(cx, gt[:], cx.dr["rw_g_tm"][b0:b0 + 128, :], gt)
        LOAD(cx, s8[:], cx.dr["rw_s_tm"][b0:b0 + 128, :], s8)
        TT_(cx, "dve", of[:], of[:], ob[:], ALU.add, [of, ob], [of])
        o3 = of[:].rearrange("p (h q) -> p h q", q=64)
        X(cx, "dve", "tensor_reduce", mean[:], o3, AX.X, ALU.add, reads=[of], writes=[mean])
        TS_(cx, "dve", mean[:], mean[:], 1.0 / 64, None, ALU.mult, None, [mean], [mean])
        TT_(cx, "dve", o3, o3, mean[:].unsqueeze(2).broadcast_to([128, 8, 64]), ALU.subtract, [of, mean], [of])
        ACT(cx, sq[:], of[:], AF.Square, [of], [sq])
        X(cx, "dve", "tensor_reduce", var[:], sq[:].rearrange("p (h q) -> p h q", q=64), AX.X, ALU.add, reads=[sq], writes=[var])
        ACT(cx, var[:], var[:], AF.Sqrt, [var], [var], bias=C["epsln"][:, 0:1], scale=1.0 / 64)
        X(cx, "dve", "reciprocal", var[:], var[:], reads=[var], writes=[var])
        TT_(cx, "dve", o3, o3, var[:].unsqueeze(2).broadcast_to([128, 8, 64]), ALU.mult, [of, var], [of])
        TT_(cx, "dve", of[:], of[:], lg[:], ALU.mult, [of, lg], [of])
        TT_(cx, "dve", of[:], of[:], lb[:], ALU.add, [of, lb], [of])
        v3 = vt[:].rearrange("p (h q) -> p h q", q=64)
        TT_(cx, "dve", v3, v3, s8[:].unsqueeze(2).broadcast_to([128, 8, 64]), ALU.mult, [vt, s8], [vt])
        TT_(cx, "dve", of[:], of[:], vt[:], ALU.add, [of, vt], [of])
        TT_(cx, "dve", of[:], of[:], gt[:], ALU.mult, [of, gt], [of])
        STORE(cx, cx.dr["o_rwkv"][b0:b0 + 128, :], of[:], of)
    cx.phase_end()


def run_mixers(cx, l):
    phase_attn(cx, l)
    phase_ssd(cx, l)
    phase_gla(cx, l)
    phase_rwkv(cx, l)


CONST_SHAPES = {"ident": [128, 128], "ones": [128, 128], "bones": [128, 128], "eps6": [128, 1], "one": [128, 1], "eps12": [128, 1],
                "epsln": [128, 1], "ind2": [128, 2], "cum128f": [128, 128], "cum128b": [128, 128], "sfx128f": [128, 128], "sfx128b": [128, 128],
                "negmf": [128, 512], "negmb": [128, 512], "cum64f": [64, 128], "cum64b": [64, 128], "msk64f": [128, 256], "msk64b": [128, 256],
                "mst64f": [128, 128], "mst64b": [128, 128]}


def host_consts():
    f32 = np.float32
    d = {}
    d["ident"] = np.eye(128, dtype=f32)
    d["ones"] = np.ones((128, 128), f32)
    bo = np.zeros((128, 128), f32)
    bo[:64, :64] = 1
    bo[64:, 64:] = 1
    d["bones"] = bo
    d["eps6"] = np.full((128, 1), 1e-6, f32)
    d["one"] = np.full((128, 1), 1.0, f32)
    d["eps12"] = np.full((128, 1), 1e-12, f32)
    d["epsln"] = np.full((128, 1), 64e-5, f32)
    ind = np.zeros((128, 2), f32)
    ind[:64, 0] = 1
    ind[64:, 1] = 1
    d["ind2"] = ind
    a = np.arange(128)
    le = (a[:, None] <= a[None, :]).astype(f32)
    ge = (a[:, None] >= a[None, :]).astype(f32)
    gt = (a[:, None] > a[None, :]).astype(f32)
    lt = (a[:, None] < a[None, :]).astype(f32)
    d["cum128f"], d["cum128b"] = le, ge
    d["sfx128f"], d["sfx128b"] = gt, lt
    d["negmf"] = np.tile((le - 1.0) * 1e30, (1, 4)).astype(f32)
    d["negmb"] = np.tile((ge - 1.0) * 1e30, (1, 4)).astype(f32)
    b = np.arange(64)
    le6 = (b[:, None] <= b[None, :]).astype(f32)
    ge6 = (b[:, None] >= b[None, :]).astype(f32)
    gt6 = (b[:, None] > b[None, :]).astype(f32)
    lt6 = (b[:, None] < b[None, :]).astype(f32)
    d["cum64f"] = np.concatenate([le6, lt6], axis=1)
    d["cum64b"] = np.concatenate([ge6, gt6], axis=1)
    t22 = lambda m: np.tile(m, (2, 2))
    d["msk64f"] = np.concatenate([t22(lt6), t22(le6)], axis=1)
    d["msk64b"] = np.concatenate([t22(gt6), t22(ge6)], axis=1)
    d["mst64f"] = t22(gt6)
    d["mst64b"] = t22(lt6)
    return {k: np.ascontiguousarray(v, dtype=f32) for k, v in d.items()}


def rope_tables(TS):
    f32 = np.float32
    freqs = (10000.0 ** (-np.arange(16, dtype=f32) / f32(16))).astype(f32)
    t = np.arange(TS)
    rows = (t // 64).astype(f32)
    cols = (t % 64).astype(f32)
    cos = np.zeros((128, TS), f32)
    sin = np.zeros((128, TS), f32)
    rot = np.zeros((128, 128), f32)
    for p in range(128):
        dd = p % 64
        pos = rows if dd < 32 else cols
        ang = (pos * freqs[dd % 16]).astype(f32)
        cos[p] = np.cos(ang)
        first = (dd % 32) < 16
        sin[p] = (-np.sin(ang) if first else np.sin(ang))
        partner = p + 16 if first else p - 16
        rot[partner, p] = 1.0
    return cos, sin, rot


MIX_IN = ("qkT", "xbcT", "rblkT", "gqkT", "gglT", "av_tm", "sdt_tm", "gk_tm", "sz_tm", "gv_tm", "gog_tm")
BR = ("o_att", "o_ssd", "o_rwkv", "o_gla")


def declare_io(cx, mode="full"):
    nc = cx.nc
    TT, TS, NP, TP, PAST = cx.TT, cx.TS, cx.NP, cx.TP, cx.PAST
    cx.inp = {}
    cx.out = {}

    def I(name, shape, dtype=F32):
        cx.inp[name] = nc.dram_tensor(name, list(shape), dtype, kind="ExternalInput").ap()

    def O(name, shape, dtype=F32):
        cx.out[name] = nc.dram_tensor(name, list(shape), dtype, kind="ExternalOutput").ap()

    def Sx(name, shape, dtype=F32):
        kind = None
        if mode == "mix" and name in MIX_IN:
            kind = "ExternalInput"
        if mode == "mix" and name in BR:
            kind = "ExternalOutput"
        if mode == "dense" and name in BR:
            kind = "ExternalInput"
        cx.dram(name, shape, dtype, kind=kind)

    cx.const_names = []
    for n, shp in CONST_SHAPES.items():
        I(n, shp)
        cx.const_names.append(n)
    if mode in ("full", "dense"):
        _, _, win_sz = w_in_offsets()
        I("xT_in", [D, TT])
        I("condT", [128, KC, 2])
        I("b_modT", [L, 128, 96])
        I("norm1T", [L, 128, KC])
        I("norm2T", [L, 128, KC])
        I("fnormT", [128, KC])
        I("w_mod_p", [L, 128, 96 * KC * 128])
        I("w_in_p", [L, 128, win_sz])
        I("w_gate_p", [L, 128, 64 * KC * 128])
        I("w_br_p", [L, 128, 64 * 4 * 128])
        I("w_o_p", [L, 128, KC * KC * 128])
        I("w1_p", [L, 128, FC * KC * 128])
        I("w3_p", [L, 128, FC * KC * 128])
        I("w2_p", [L, 128, KC * FC * 128])
        O("yT", [D, TT])
        Sx("xres", [D, TT])
        Sx("hT", [D, TT], BF16)
        for n in WNAMES:
            Sx(n + "_b", list(cx.inp[n].shape), BF16)
        cx.wbuf = [Buf() for _ in range(L)]
    if mode in ("full", "mix"):
        I("qgT", [L, 128, 1]); I("kgT", [L, 128, 1])
        I("ropecos", [128, TS]); I("ropesin", [128, TS]); I("rotP", [128, 128])
        I("cache_k", [L, PAST, 128]); I("cache_v", [L, PAST, 128])
        I("ssd_cwT", [L, 128, 6, 3]); I("ssd_cbT", [L, 128, 6])
        I("ssd_dt_bias", [L, 16]); I("ssd_a_log", [L, 16]); I("ssd_d", [L, 8]); I("ssd_norm", [L, 512])
        I("state_ssd", [L, 2, 8, 64, 64])
        I("gla_g2", [L, 2, 16, 256]); I("gla_gb", [L, 2, 256]); I("gla_norm", [L, 128])
        I("state_gla", [L, 2, 4, 64, 128])
        I("rw_muT", [L, 128, 14])
        for n in ("rw_a0T", "rw_kkT", "rw_kaT", "rw_rkT"):
            I(n, [L, 128, 4])
        I("rw_w0", [L, 2, 512]); I("rw_w2", [L, 2, 64, 512]); I("rw_a2", [L, 64, 512]); I("rw_g2", [L, 128, 512])
        I("rw_ln_g", [L, 512]); I("rw_ln_b", [L, 512])
        I("state_rwkvT", [L, 2, 8, 64, 64])
        O("new_k", [NP, L, TP, 128]); O("new_v", [NP, L, TP, 128])
        O("new_ssd", [NP, L, 2, 8, 64, 64]); O("new_rwkvT", [NP, L, 2, 8, 64, 64]); O("new_gla", [NP, L, 2, 4, 64, 128])
        Sx("qn_d", [640, TT], BF16)
        Sx("bcT", [256, TT]); Sx("xs_tm", [TT, 512]); Sx("b_tm", [TT, 128]); Sx("dt_tm", [TT, 16]); Sx("ld_tm", [TT, 16])
        Sx("o_ssd_d", [2, TT, 512]); Sx("o_gla_d", [2, TT, 512])
        Sx("rw_fm", [4, 512, TT]); Sx("rw_s_tm", [TT, 8]); Sx("rw_v_tm", [TT, 512]); Sx("rw_g_tm", [TT, 512])
        Sx("rw_lw_tm", [2, TT, 512]); Sx("o_rw_d", [2, TT, 512])
    Sx("qkT", [640, TT]); Sx("xbcT", [768, TT]); Sx("rblkT", [1792, TT]); Sx("gqkT", [512, TT]); Sx("gglT", [16, TT])
    Sx("av_tm", [TT, 128]); Sx("sdt_tm", [TT, 16]); Sx("gk_tm", [TT, 256]); Sx("sz_tm", [TT, 512]); Sx("gv_tm", [TT, 512]); Sx("gog_tm", [TT, 512])
    for n in BR:
        Sx(n, [TT, 512])


def build_program(cfg, dbg=(), mode="full", layers=None, mixers=("attn", "ssd", "gla", "rwkv")):
    nc = bass.Bass("TRN2", target_bir_lowering=False)
    cx = Ctx(nc, cfg, dbg)
    declare_io(cx, mode)
    load_consts(cx)
    fns = {"attn": phase_attn, "ssd": phase_ssd, "gla": phase_gla, "rwkv": phase_rwkv}
    if mode == "mix":
        for l in (layers if layers is not None else range(L)):
            for m in mixers:
                fns[m](cx, l)
    else:
        phase_convert(cx, 0)
        phase_mod(cx)
        for l in range(L):
            phase_A(cx, l)
            if l + 1 < L:
                phase_convert(cx, l + 1)
            if mode == "full":
                for m in mixers:
                    fns[m](cx, l)
            phase_C(cx, l, last=(l == L - 1))
    cx.fw.finish()
    cx.fw.emit()
    return nc, cx


def host_common_inputs(inp, cfg, mode="full"):
    f32 = np.float32
    A = lambda k: np.asarray(inp[k], f32)
    d = {}
    d.update(host_consts())
    if mode in ("full", "dense"):
        d["b_modT"] = np.stack([vec_pp(A("b_mod")[l]) for l in range(L)])
        d["norm1T"] = np.stack([vec_pp(A("norm1")[l]) for l in range(L)])
        d["norm2T"] = np.stack([vec_pp(A("norm2")[l]) for l in range(L)])
        d["fnormT"] = vec_pp(A("final_norm"))
        d["w_mod_p"] = np.stack([pack_chunks(A("w_mod")[l]) for l in range(L)])
        d["w_in_p"] = np.stack([pack_w_in(A("w_in")[l])[0] for l in range(L)])
        wg, wb = [], []
        for l in range(L):
            g = A("w_gate")[l]
            b = A("w_branch")[l]
            wg.append(np.concatenate([_pack_cols(g[i], f * 128, 128) for f in range(KC) for i in range(4)], axis=1))
            wb.append(np.concatenate([_pack_cols(b[i], f * 128, 128) for f in range(KC) for i in range(4)], axis=1))
        d["w_gate_p"] = np.stack(wg)
        d["w_br_p"] = np.stack(wb)
        d["w_o_p"] = np.stack([pack_chunks(A("w_o")[l]) for l in range(L)])
        d["w1_p"] = np.stack([pack_chunks(A("ffn_w1")[l]) for l in range(L)])
        d["w3_p"] = np.stack([pack_chunks(A("ffn_w3")[l]) for l in range(L)])
        d["w2_p"] = np.stack([pack_chunks(A("ffn_w2")[l]) for l in range(L)])
    if mode in ("full", "mix"):
        d["qgT"] = np.stack([np.tile(A("q_norm")[l], 2)[:, None] for l in range(L)])
        d["kgT"] = np.stack([np.tile(A("k_norm")[l], 2)[:, None] for l in range(L)])
        cos, sin, rot = rope_tables(cfg["TS"])
        d["ropecos"], d["ropesin"], d["rotP"] = cos, sin, rot
        cw = A("ssd_conv_w")
        d["ssd_cwT"] = np.ascontiguousarray(cw.reshape(L, 6, 128, 3).transpose(0, 2, 1, 3))
        d["ssd_cbT"] = np.stack([vec_pp(A("ssd_conv_b")[l]) for l in range(L)])
        d["ssd_dt_bias"] = A("ssd_dt_bias").reshape(L, 16)
        d["ssd_a_log"] = A("ssd_a_log").reshape(L, 16)
        d["ssd_d"] = A("ssd_d")
        d["ssd_norm"] = A("ssd_norm")
        d["gla_g2"] = A("gla_g2"); d["gla_gb"] = A("gla_gb"); d["gla_norm"] = A("gla_norm")
        d["rw_muT"] = np.stack([vec_pp(A("rwkv_mu")[l]) for l in range(L)])
        for n, k in (("rw_a0T", "rwkv_a0"), ("rw_kkT", "rwkv_kk"), ("rw_kaT", "rwkv_ka"), ("rw_rkT", "rwkv_rk")):
            d[n] = np.stack([vec_pp(A(k)[l]) for l in range(L)])
        d["rw_w0"] = A("rwkv_w0"); d["rw_w2"] = A("rwkv_w2"); d["rw_a2"] = A("rwkv_a2"); d["rw_g2"] = A("rwkv_g2")
        d["rw_ln_g"] = A("rwkv_ln_g"); d["rw_ln_b"] = A("rwkv_ln_b")
    return {k: np.ascontiguousarray(v, dtype=f32) for k, v in d.items()}


def host_core_inputs(inp, cfg, b_idx, p_idx, mode="full"):
    f32 = np.float32
    A = lambda k: np.asarray(inp[k], f32)
    d = {}
    if mode in ("full", "dense"):
        xs = A("x_sample")[b_idx]
        xp = np.concatenate([A("x_prompt")[p] for p in p_idx], axis=0)
        d["xT_in"] = np.concatenate([xs, xp], axis=0).T
        cond = np.stack([A("c_ctx"), A("c")[b_idx]], axis=-1)
        d["condT"] = cond.reshape(KC, 128, 2).transpose(1, 0, 2)
    if mode in ("full", "mix"):
        d["cache_k"] = A("cache_attn_k")[b_idx].reshape(L, cfg["PAST"], 128)
        d["cache_v"] = A("cache_attn_v")[b_idx].reshape(L, cfg["PAST"], 128)
        d["state_ssd"] = A("state_ssd")[b_idx]
        d["state_gla"] = A("state_gla")[b_idx]
        d["state_rwkvT"] = A("state_rwkv")[b_idx].transpose(0, 1, 2, 4, 3)
    return {k: np.ascontiguousarray(v, dtype=f32) for k, v in d.items()}


_PROG_CACHE = {}


def kernel(**inp):
    import sys
    global DEBUG_SITES
    DEBUG_SITES = False
    xs = np.asarray(inp["x_sample"])
    xp = np.asarray(inp["x_prompt"])
    n_cores = xs.shape[0]
    TS = xs.shape[1]
    NP = xp.shape[0] // n_cores
    TP = xp.shape[1]
    PAST = np.asarray(inp["cache_attn_k"]).shape[2]
    cfg = dict(TS=TS, NP=NP, TP=TP, PAST=PAST)
    key = (TS, NP, TP, PAST)
    if key not in _PROG_CACHE:
        _PROG_CACHE[key] = build_program(cfg, mode="full")
    nc, cx = _PROG_CACHE[key]
    common = host_common_inputs(inp, cfg, mode="full")
    in_maps = []
    for i in range(n_cores):
        d = dict(common)
        d.update(host_core_inputs(inp, cfg, i, list(range(i * NP, (i + 1) * NP)), mode="full"))
        in_maps.append(d)
    res = run_bass_kernel_spmd(nc, in_maps, core_ids=list(range(n_cores)))
    f32 = np.float32
    B = n_cores
    y_prompt = np.zeros((B * NP, TP, D), f32)
    y_sample = np.zeros((B, TS, D), f32)
    new_k = np.zeros((B * NP, L, TP, 2, 64), f32)
    new_v = np.zeros((B * NP, L, TP, 2, 64), f32)
    new_ssd = np.zeros((B * NP, L, 2, 8, 64, 64), f32)
    new_rwkv = np.zeros((B * NP, L, 2, 8, 64, 64), f32)
    new_gla = np.zeros((B * NP, L, 2, 4, 64, 128), f32)
    for i in range(n_cores):
        r = res.results[i]
        yT = np.asarray(r["yT"])
        y_sample[i] = yT[:, :TS].T
        for p in range(NP):
            y_prompt[i * NP + p] = yT[:, TS + p * TP:TS + (p + 1) * TP].T
        sl = slice(i * NP, (i + 1) * NP)
        new_k[sl] = np.asarray(r["new_k"]).reshape(NP, L, TP, 2, 64)
        new_v[sl] = np.asarray(r["new_v"]).reshape(NP, L, TP, 2, 64)
        new_ssd[sl] = np.asarray(r["new_ssd"])
        new_rwkv[sl] = np.asarray(r["new_rwkvT"]).transpose(0, 1, 2, 3, 5, 4)
        new_gla[sl] = np.asarray(r["new_gla"])
    return (y_prompt, y_sample, new_k, new_v, new_ssd, new_rwkv, new_gla)
```
